# Optimizing a Trainium2 kernel written in Bass

```python
import math
import jax
import jax.numpy as jnp
from jax import lax
import numpy as np

D_MODEL = 2048
BATCH = 16
SEQ = 256
DEPTH = 2
DEC_BATCH = 8
DEC_SEQ = 1024
PAST_LEN = 512

GRID_W = 64
N_EVEN = (DEPTH + 1) // 2
N_ODD = DEPTH // 2
MIX_W = D_MODEL
HEAD_DIM = 128
HY_W = MIX_W // 2
HY_ORDER = 2
SHORT_CONV = 3
POS_EMB = 33
FILTER_ORDER = 64
HY_FAST_DECAY = 0.3
HY_SLOW_DECAY = 1.5
HY_TARGET = 1e-2
ATT_HEADS = (MIX_W // 2) // HEAD_DIM
ATT_KV_HEADS = ATT_HEADS // 4
ATT_GROUP = ATT_HEADS // ATT_KV_HEADS
WINDOW = 128
BLK = 128
EV_IN = (HY_ORDER + 1) * HY_W + (ATT_HEADS + 2 * ATT_KV_HEADS) * HEAD_DIM
HG_W = MIX_W // 2
HG_HEADS = HG_W // HEAD_DIM
HG_DK = HEAD_DIM
HG_DV = HEAD_DIM
CHUNK = 64
Q_LORA = 512
KV_LORA = 256
NOPE = 128
ROPE = 64
V_DIM = 128
MLA_HEADS = (MIX_W // 2) // V_DIM
OD_IN = 5 * HG_W + Q_LORA + KV_LORA + ROPE
D_FF = 5632
MACARON_W = 0.5
N_MOD = 9
ROPE_BASE = 10000.0
EPS = 1e-6
QBLK = 128
ATT_SCALE = HEAD_DIM ** -0.5
MLA_SCALE = (NOPE + ROPE) ** -0.5

kernel_name = 'hybrid_diffusion_prefix_step'


def _rmsnorm(x, g):
    xf = x.astype(jnp.float32)
    y = xf * lax.rsqrt(jnp.mean(xf * xf, axis=-1, keepdims=True) + EPS)
    return (y * g.astype(jnp.float32)).astype(x.dtype)


def _modulation(cvec, w, b):
    m = jax.nn.silu(cvec) @ w + b
    return m.reshape(cvec.shape[0], N_MOD, 1, D_MODEL)


def _modulate(x, g, mods, j):
    return _rmsnorm(x, g) * (1.0 + mods[:, 3 * j + 1]) + mods[:, 3 * j]


def _residual(x, out, g, mods, j, w):
    return x + w * mods[:, 3 * j + 2] * _rmsnorm(out.astype(x.dtype), g)


def _ffn_sublayer(x, mods, j, g_pre, g_post, wg, wu, wd):
    h = _modulate(x, g_pre, mods, j)
    out = (jax.nn.silu(h @ wg) * (h @ wu)) @ wd
    return _residual(x, out, g_post, mods, j, MACARON_W)


def _axial_angles(L, rot_dim):
    rows = L // GRID_W
    half = rot_dim // 2
    inv = ROPE_BASE ** (-jnp.arange(0, half, 2, dtype=jnp.float32) / half)
    row = jnp.repeat(jnp.arange(rows), GRID_W).astype(jnp.float32)
    col = jnp.tile(jnp.arange(GRID_W), rows).astype(jnp.float32)
    return row[:, None] * inv, col[:, None] * inv


def _rot_half(x, ang):
    x1, x2 = jnp.split(x, 2, axis=-1)
    cos = jnp.cos(ang).astype(x.dtype)
    sin = jnp.sin(ang).astype(x.dtype)
    return jnp.concatenate([x1 * cos - x2 * sin, x2 * cos + x1 * sin], axis=-1)


def _rope2d(x):
    row, col = _axial_angles(x.shape[-2], x.shape[-1])
    xr, xc = jnp.split(x, 2, axis=-1)
    return jnp.concatenate([_rot_half(xr, row), _rot_half(xc, col)], axis=-1)


def _probs(s, sink):
    if sink is None:
        return jax.nn.softmax(s, axis=-1)
    sk = sink.astype(jnp.float32)[None, :, :, None, None]
    m = jnp.maximum(jnp.max(s, axis=-1, keepdims=True), sk)
    p = jnp.exp(s - m)
    return p / (jnp.sum(p, axis=-1, keepdims=True) + jnp.exp(sk - m))


def _attend_dense(q, k, v, scale, sink=None):
    B, Hk, G, Lq, dq = q.shape
    nb = Lq // QBLK
    qb = jnp.moveaxis(q.reshape(B, Hk, G, nb, QBLK, dq), 3, 0)

    def one(qi):
        s = jnp.einsum('bhgqd,bhkd->bhgqk', qi, k).astype(jnp.float32) * scale
        p = _probs(s, sink)
        return jnp.einsum('bhgqk,bhkd->bhgqd', p.astype(v.dtype), v)

    o = lax.map(one, qb)
    return jnp.moveaxis(o, 0, 3).reshape(B, Hk, G, Lq, v.shape[-1])


def _attend_window_ctx(q, k, v, k_ctx, v_ctx, sink, scale):
    B, Hk, G, L, d = q.shape
    nb = L // BLK

    def band(a):
        ap = jnp.pad(a, ((0, 0), (0, 0), (BLK, BLK), (0, 0))).reshape(B, Hk, nb + 2, BLK, a.shape[-1])
        return jnp.concatenate([ap[:, :, :-2], ap[:, :, 1:-1], ap[:, :, 2:]], axis=3)

    kb, vb = band(k), band(v)
    jb = jnp.arange(nb)[:, None, None]
    qpos = jb * BLK + jnp.arange(BLK)[None, :, None]
    kpos = (jb - 1) * BLK + jnp.arange(3 * BLK)[None, None, :]
    mask = (jnp.abs(kpos - qpos) <= WINDOW) & (kpos >= 0) & (kpos < L)
    xs = (jnp.moveaxis(q.reshape(B, Hk, G, nb, BLK, d), 3, 0), jnp.moveaxis(kb, 2, 0),
          jnp.moveaxis(vb, 2, 0), mask)

    def one(args):
        qi, ki, vi, mk = args
        s_loc = jnp.einsum('bhgqd,bhkd->bhgqk', qi, ki).astype(jnp.float32) * scale
        s_loc = jnp.where(mk, s_loc, -jnp.inf)
        s_ctx = jnp.einsum('bhgqd,bhkd->bhgqk', qi, k_ctx).astype(jnp.float32) * scale
        p = _probs(jnp.concatenate([s_loc, s_ctx], axis=-1), sink)
        p_loc, p_ctx = p[..., :3 * BLK], p[..., 3 * BLK:]
        return (jnp.einsum('bhgqk,bhkd->bhgqd', p_loc.astype(vi.dtype), vi)
                + jnp.einsum('bhgqk,bhkd->bhgqd', p_ctx.astype(v_ctx.dtype), v_ctx))

    o = lax.map(one, xs)
    return jnp.moveaxis(o, 0, 3).reshape(B, Hk, G, L, d)


def _short_conv(u, w, b):
    L = u.shape[1]
    pad = SHORT_CONV // 2
    up = jnp.pad(u, ((0, 0), (pad, pad), (0, 0)))
    out = b
    for j in range(SHORT_CONV):
        out = out + up[:, j:j + L] * w[j]
    return out


def _hyena_filters(L, w1, b1, w2, b2, w3, freq):
    f32 = jnp.float32
    t = jnp.linspace(0.0, 1.0, L, dtype=f32)[:, None]
    bands = (POS_EMB - 1) // 2
    w = 2.0 * math.pi * jnp.arange(L, dtype=f32)[:, None] / L
    fb = jnp.linspace(1e-4, bands - 1, bands, dtype=f32)[None, :]
    z = jnp.concatenate([t, jnp.cos(fb * w), -jnp.sin(fb * w)], axis=-1)
    fr = freq.astype(f32)
    h = jnp.sin(fr * (z @ w1.astype(f32) + b1.astype(f32)))
    h = jnp.sin(fr * (h @ w2.astype(f32) + b2.astype(f32)))
    h = (h @ w3.astype(f32)).reshape(L, HY_ORDER, 2, HY_W)
    deltas = jnp.abs(jnp.linspace(math.log(HY_TARGET) / HY_SLOW_DECAY,
                                  math.log(HY_TARGET) / HY_FAST_DECAY, HY_W, dtype=f32))
    decay = jnp.exp(-t * deltas)
    return jnp.moveaxis(h * decay[:, None, None, :], 0, 2)


def _long_conv(z, hf, hb):
    L = z.shape[1]
    filt = jnp.concatenate([hf, jnp.zeros_like(hf[:1]), hb[:0:-1]], axis=0)
    Z = jnp.fft.rfft(z.astype(jnp.float32), n=2 * L, axis=1)
    K = jnp.fft.rfft(filt, n=2 * L, axis=0)
    return jnp.fft.irfft(Z * K[None], n=2 * L, axis=1)[:, :L]


def _hyena(u, conv_w, conv_b, f_w1, f_b1, f_w2, f_b2, f_w3, f_freq, h_bias):
    L = u.shape[1]
    uc = _short_conv(u, conv_w, conv_b).astype(jnp.float32)
    v, *gates = jnp.split(uc, HY_ORDER + 1, axis=-1)
    h = _hyena_filters(L, f_w1, f_b1, f_w2, f_b2, f_w3, f_freq)
    z = v
    for n in range(HY_ORDER):
        z = gates[n] * (_long_conv(z, h[n, 0], h[n, 1]) + h_bias[n].astype(jnp.float32) * z)
    return z


def _gla_chunk(q, k, v, logf, S0):
    B, H, L, dk = q.shape
    n = L // CHUNK

    def blocks(a):
        return a.reshape(B, H, n, CHUNK, a.shape[-1])

    q, k, v, logf = blocks(q), blocks(k), blocks(v), blocks(logf)
    b = jnp.cumsum(logf, axis=3)
    b_end = b[:, :, :, -1:]
    qd = q * jnp.exp(b)
    kd = k * jnp.exp(-b)
    kend = k * jnp.exp(b_end - b)
    causal = jnp.tril(jnp.ones((CHUNK, CHUNK), dtype=bool))
    A = jnp.where(causal, jnp.einsum('bhncd,bhnsd->bhncs', qd, kd), 0.0)
    o_intra = jnp.einsum('bhncs,bhnsv->bhncv', A, v)

    def step(S, xs):
        qc, kc, vc, dc = xs
        o = jnp.einsum('bhcd,bhdv->bhcv', qc, S)
        S = dc[..., None] * S + jnp.einsum('bhcd,bhcv->bhdv', kc, vc)
        return S, o

    xs = tuple(jnp.moveaxis(a, 2, 0) for a in (qd, kend, v, jnp.exp(b_end[:, :, :, 0])))
    S, o_inter = lax.scan(step, S0.astype(jnp.float32), xs)
    o = o_intra + jnp.moveaxis(o_inter, 0, 2)
    return o.reshape(B, H, L, -1), S


def _heads(a):
    B, L, _ = a.shape
    return a.reshape(B, L, HG_HEADS, -1).transpose(0, 2, 1, 3).astype(jnp.float32)


def _hgrn(q_h, f_f, f_b, i_h, g_h, lb, norm_g, S0):
    q = jax.nn.silu(_heads(q_h))
    v = _heads(i_h)
    o = 0.0
    states = []
    for d, fz in enumerate((f_f, f_b)):
        lbd = lb[d].reshape(HG_HEADS, 1, HG_DK)
        f = lbd + (1.0 - lbd) * jax.nn.sigmoid(_heads(fz))
        args = (q, 1.0 - f, v, jnp.log(f))
        if d == 1:
            args = tuple(a[:, :, ::-1] for a in args)
        od, Sd = _gla_chunk(*args, S0[:, d])
        o = o + (od[:, :, ::-1] if d == 1 else od)
        states.append(Sd)
    o = _rmsnorm(o, norm_g) * jax.nn.silu(_heads(g_h))
    B, H, L, dv = o.shape
    return o.transpose(0, 2, 1, 3).reshape(B, L, H * dv), jnp.stack(states, axis=1)


def _mla_q(q_lat, q_norm, w_qb):
    B, L, _ = q_lat.shape
    q = (_rmsnorm(q_lat, q_norm) @ w_qb).reshape(B, L, MLA_HEADS, NOPE + ROPE).transpose(0, 2, 1, 3)
    return q[..., :NOPE], q[..., NOPE:]


def _mla_kv(ckv, krope, w_kvb):
    B, L, _ = ckv.shape
    kv = (ckv @ w_kvb).reshape(B, L, MLA_HEADS, NOPE + V_DIM).transpose(0, 2, 1, 3)
    kr = jnp.broadcast_to(krope[:, None], (B, MLA_HEADS, L, ROPE)).astype(kv.dtype)
    return jnp.concatenate([kv[..., :NOPE], kr], axis=-1), kv[..., NOPE:]


def _merge(y_a, attn, w_out, dtype):
    B, Hk, G, L, dv = attn.shape
    a = attn.transpose(0, 3, 1, 2, 4).reshape(B, L, Hk * G * dv)
    return jnp.concatenate([y_a.astype(dtype), a.astype(dtype)], axis=-1) @ w_out


def _even_project(h, w_in):
    B, L, _ = h.shape
    p = h @ w_in
    n_hy = (HY_ORDER + 1) * HY_W
    n_q = ATT_HEADS * HEAD_DIM
    n_kv = ATT_KV_HEADS * HEAD_DIM
    u = p[..., :n_hy]
    q = p[..., n_hy:n_hy + n_q].reshape(B, L, ATT_KV_HEADS, ATT_GROUP, HEAD_DIM).transpose(0, 2, 3, 1, 4)
    k = p[..., n_hy + n_q:n_hy + n_q + n_kv].reshape(B, L, ATT_KV_HEADS, HEAD_DIM).transpose(0, 2, 1, 3)
    v = p[..., n_hy + n_q + n_kv:].reshape(B, L, ATT_KV_HEADS, HEAD_DIM).transpose(0, 2, 1, 3)
    return u, q, k, v


def _even_context(h, w_in, w_out, sink, hy_w):
    u, q, k, v = _even_project(h, w_in)
    a = _attend_dense(q, k, v, ATT_SCALE, sink)
    return _merge(_hyena(u, *hy_w), a, w_out, h.dtype), k, v


def _even_latent(h, k_ctx, v_ctx, w_in, w_out, sink, hy_w):
    u, q, k, v = _even_project(h, w_in)
    a = _attend_window_ctx(_rope2d(q), _rope2d(k), v, k_ctx, v_ctx, sink, ATT_SCALE)
    return _merge(_hyena(u, *hy_w), a, w_out, h.dtype)


def _odd_project(h, w_in):
    sizes = [HG_W] * 5 + [Q_LORA, KV_LORA, ROPE]
    idx = np.cumsum(sizes)[:-1].tolist()
    return jnp.split(h @ w_in, idx, axis=-1)


def _odd_context(h, lb, w_in, w_out, hg_g, qn, wqb, kvn, wkvb):
    q_h, f_f, f_b, i_h, g_h, q_lat, kv_lat, k_rope = _odd_project(h, w_in)
    S0 = jnp.zeros((h.shape[0], 2, HG_HEADS, HG_DK, HG_DV), jnp.float32)
    o_hg, S = _hgrn(q_h, f_f, f_b, i_h, g_h, lb, hg_g, S0)
    q_nope, q_rope = _mla_q(q_lat, qn, wqb)
    ckv = _rmsnorm(kv_lat, kvn)
    k, v = _mla_kv(ckv, k_rope, wkvb)
    q = jnp.concatenate([q_nope, q_rope], axis=-1)[:, :, None]
    a = _attend_dense(q, k, v, MLA_SCALE)
    return _merge(o_hg, a, w_out, h.dtype), S, ckv, k_rope


def _odd_latent(h, S_ctx, ckv_ctx, kr_ctx, lb, w_in, w_out, hg_g, qn, wqb, kvn, wkvb):
    q_h, f_f, f_b, i_h, g_h, q_lat, kv_lat, k_rope = _odd_project(h, w_in)
    o_hg, _ = _hgrn(q_h, f_f, f_b, i_h, g_h, lb, hg_g, S_ctx)
    q_nope, q_rope = _mla_q(q_lat, qn, wqb)
    q = jnp.concatenate([q_nope, _rope2d(q_rope)], axis=-1)[:, :, None]
    k_l, v_l = _mla_kv(_rmsnorm(kv_lat, kvn), _rope2d(k_rope), wkvb)
    k_c, v_c = _mla_kv(ckv_ctx, kr_ctx, wkvb)
    a = _attend_dense(q, jnp.concatenate([k_l, k_c.astype(k_l.dtype)], axis=2),
                      jnp.concatenate([v_l, v_c.astype(v_l.dtype)], axis=2), MLA_SCALE)
    return _merge(o_hg, a, w_out, h.dtype)


def setup_inputs(seed: int = 0) -> dict:
    key = jax.random.key(seed)
    ks = iter(jax.random.split(key, 64))
    D = D_MODEL

    def nrm(shape, s):
        return jax.random.normal(next(ks), shape, jnp.float32) * s

    def gain(shape):
        return 1.0 + nrm(shape, 0.05)

    return {
        'x_prompt': nrm((BATCH, SEQ, D), 1.0),
        'x_sample': nrm((DEC_BATCH, DEC_SEQ, D), 1.0),
        'c': nrm((DEC_BATCH, D), 1.0),
        'c_ctx': nrm((D,), 1.0),
        'cache_attn_k': nrm((DEC_BATCH, N_EVEN, ATT_KV_HEADS, PAST_LEN, HEAD_DIM), 1.0),
        'cache_attn_v': nrm((DEC_BATCH, N_EVEN, ATT_KV_HEADS, PAST_LEN, HEAD_DIM), 1.0),
        'cache_mla_ckv': nrm((DEC_BATCH, N_ODD, PAST_LEN, KV_LORA), 1.0),
        'cache_mla_krope': nrm((DEC_BATCH, N_ODD, PAST_LEN, ROPE), 1.0),
        'state_hgrn': nrm((DEC_BATCH, N_ODD, 2, HG_HEADS, HG_DK, HG_DV), 0.5),
        'mod_w': nrm((DEPTH, D, N_MOD * D), 0.5 * D ** -0.5),
        'mod_b': nrm((DEPTH, N_MOD * D), 0.02),
        'norm_g': gain((DEPTH, 6, D)),
        'ffn_wg': nrm((DEPTH, 2, D, D_FF), D ** -0.5),
        'ffn_wu': nrm((DEPTH, 2, D, D_FF), D ** -0.5),
        'ffn_wd': nrm((DEPTH, 2, D_FF, D), D_FF ** -0.5),
        'ev_w_in': nrm((N_EVEN, D, EV_IN), D ** -0.5),
        'ev_w_out': nrm((N_EVEN, MIX_W, D), MIX_W ** -0.5),
        'hy_conv_w': nrm((N_EVEN, SHORT_CONV, (HY_ORDER + 1) * HY_W), SHORT_CONV ** -0.5),
        'hy_conv_b': nrm((N_EVEN, (HY_ORDER + 1) * HY_W), 0.02),
        'hy_f_w1': nrm((N_EVEN, POS_EMB, FILTER_ORDER), POS_EMB ** -0.5),
        'hy_f_b1': nrm((N_EVEN, FILTER_ORDER), 0.1),
        'hy_f_w2': nrm((N_EVEN, FILTER_ORDER, FILTER_ORDER), FILTER_ORDER ** -0.5),
        'hy_f_b2': nrm((N_EVEN, FILTER_ORDER), 0.1),
        'hy_f_w3': nrm((N_EVEN, FILTER_ORDER, HY_ORDER * 2 * HY_W), 0.1 * FILTER_ORDER ** -0.5),
        'hy_f_freq': 1.0 + nrm((N_EVEN, FILTER_ORDER), 0.1),
        'hy_bias': nrm((N_EVEN, HY_ORDER, HY_W), 0.5),
        'attn_sink': nrm((N_EVEN, ATT_HEADS), 0.5),
        'od_w_in': nrm((N_ODD, D, OD_IN), D ** -0.5),
        'od_w_out': nrm((N_ODD, MIX_W, D), MIX_W ** -0.5),
        'hg_lb': nrm((DEPTH, 2, HG_W), 0.1),
        'hg_norm': gain((N_ODD, HG_DV)),
        'mla_q_norm': gain((N_ODD, Q_LORA)),
        'mla_w_qb': nrm((N_ODD, Q_LORA, MLA_HEADS * (NOPE + ROPE)), Q_LORA ** -0.5),
        'mla_kv_norm': gain((N_ODD, KV_LORA)),
        'mla_w_kvb': nrm((N_ODD, KV_LORA, MLA_HEADS * (NOPE + V_DIM)), KV_LORA ** -0.5),
    }


def reference(x_prompt, x_sample, c, c_ctx, cache_attn_k, cache_attn_v, cache_mla_ckv, cache_mla_krope,
              state_hgrn, mod_w, mod_b, norm_g, ffn_wg, ffn_wu, ffn_wd, ev_w_in, ev_w_out, hy_conv_w,
              hy_conv_b, hy_f_w1, hy_f_b1, hy_f_w2, hy_f_b2, hy_f_w3, hy_f_freq, hy_bias, attn_sink,
              od_w_in, od_w_out, hg_lb, hg_norm, mla_q_norm, mla_w_qb, mla_kv_norm, mla_w_kvb):
    lb_all = jnp.cumsum(jax.nn.softmax(hg_lb.astype(jnp.float32), axis=0), axis=0)
    lb_all = lb_all - lb_all[:1]
    xp, xs = x_prompt, x_sample
    new_k, new_v, new_ckv, new_kr, new_s = [], [], [], [], []
    for l in range(DEPTH):
        mp = _modulation(c_ctx[None, :], mod_w[l], mod_b[l])
        ms = _modulation(c, mod_w[l], mod_b[l])
        xp = _ffn_sublayer(xp, mp, 0, norm_g[l, 0], norm_g[l, 1], ffn_wg[l, 0], ffn_wu[l, 0], ffn_wd[l, 0])
        xs = _ffn_sublayer(xs, ms, 0, norm_g[l, 0], norm_g[l, 1], ffn_wg[l, 0], ffn_wu[l, 0], ffn_wd[l, 0])
        hp = _modulate(xp, norm_g[l, 2], mp, 1)
        hs = _modulate(xs, norm_g[l, 2], ms, 1)
        if l % 2 == 0:
            e = l // 2
            hy_w = (hy_conv_w[e], hy_conv_b[e], hy_f_w1[e], hy_f_b1[e], hy_f_w2[e], hy_f_b2[e],
                    hy_f_w3[e], hy_f_freq[e], hy_bias[e])
            sink = attn_sink[e].reshape(ATT_KV_HEADS, ATT_GROUP)
            op, kc, vc = _even_context(hp, ev_w_in[e], ev_w_out[e], sink, hy_w)
            osm = _even_latent(hs, cache_attn_k[:, e], cache_attn_v[:, e], ev_w_in[e], ev_w_out[e], sink, hy_w)
            new_k.append(kc)
            new_v.append(vc)
        else:
            o = l // 2
            ow = (od_w_in[o], od_w_out[o], hg_norm[o], mla_q_norm[o], mla_w_qb[o], mla_kv_norm[o], mla_w_kvb[o])
            op, sc, ckv, kr = _odd_context(hp, lb_all[l], *ow)
            osm = _odd_latent(hs, state_hgrn[:, o], cache_mla_ckv[:, o], cache_mla_krope[:, o], lb_all[l], *ow)
            new_s.append(sc)
            new_ckv.append(ckv)
            new_kr.append(kr)
        xp = _residual(xp, op, norm_g[l, 3], mp, 1, 1.0)
        xs = _residual(xs, osm, norm_g[l, 3], ms, 1, 1.0)
        xp = _ffn_sublayer(xp, mp, 2, norm_g[l, 4], norm_g[l, 5], ffn_wg[l, 1], ffn_wu[l, 1], ffn_wd[l, 1])
        xs = _ffn_sublayer(xs, ms, 2, norm_g[l, 4], norm_g[l, 5], ffn_wg[l, 1], ffn_wu[l, 1], ffn_wd[l, 1])
    return (xp, xs, jnp.stack(new_k, axis=1), jnp.stack(new_v, axis=1), jnp.stack(new_ckv, axis=1),
            jnp.stack(new_kr, axis=1), jnp.stack(new_s, axis=1))
```

```python
import math
from contextlib import ExitStack
import numpy as np
import ml_dtypes
import concourse.bass as bass
import concourse.mybir as mybir
from concourse.bass_utils import run_bass_kernel_spmd

F32 = mybir.dt.float32
BF16 = mybir.dt.bfloat16
AF = mybir.ActivationFunctionType
ALU = mybir.AluOpType

D = 2048
DC = 16
DFF = 5632
FC = 44
LS = 1024
LP = 256
NP_ = 2
EPS = 1e-6
NCORES = 8
SKIP_SAME_ENGINE_WAITS = False


class Res:
    __slots__ = ("w", "rs", "excl")

    def __init__(self, excl=False):
        self.w = None
        self.rs = {}
        self.excl = excl


class Eng:
    def __init__(self, name, h, si):
        self.name = name
        self.h = h
        self.si = si
        self.cnt = 0
        self.seen = {}
        self.dma_sems = []
        self.dma_vals = []
        self.dma_next = 0


class PB:
    def __init__(self, t):
        self.t = t
        self.r = Res(excl=True)

    def ap(self, p=128, n=512):
        return self.t[0:p, 0:n]


class Tile:
    def __init__(self, t, nchunk):
        self.t = t
        self.r = [Res() for _ in range(nchunk)]
        self.n = nchunk

    def __getitem__(self, i):
        return self.t[:, i, :]


class K:
    def __init__(self, nc):
        self.nc = nc
        self.sems = []
        self.pe = self._eng("pe", nc.tensor)
        self.act = self._eng("act", nc.scalar)
        self.dve = self._eng("dve", nc.vector)
        self.pool = self._eng("pool", nc.gpsimd)
        self.sp = self._eng("sp", nc.sync)
        self.compute = [self.pe, self.act, self.dve]
        for q, n in ((self.sp, 24), (self.pool, 8)):
            for i in range(n):
                q.dma_sems.append(self._sem(f"{q.name}_d{i}"))
                q.dma_vals.append(0)
        self.ninst = 0
        self._names = 0
        self.skip_same = SKIP_SAME_ENGINE_WAITS
        self.banks = [PB(nc.alloc_psum_tensor(f"psb{i}", [128, 512], F32)) for i in range(8)]
        self.bank_i = 0
        self.NW = 3
        self.wslots = [Tile(nc.alloc_sbuf_tensor(f"wslot{i}", [128, 16, 512], BF16), 1) for i in range(self.NW)]
        self.w_i = 0

    def _sem(self, name):
        s = self.nc.semaphore(name).__enter__()
        self.sems.append(s)
        return len(self.sems) - 1

    def _eng(self, name, h):
        return Eng(name, h, self._sem(name))

    def name(self, p):
        self._names += 1
        return f"{p}{self._names}"

    def wait(self, eng, ev):
        si, val = ev
        if eng.seen.get(si, 0) >= val:
            return
        if si == eng.si and (eng is self.pe or self.skip_same):
            return
        eng.h.wait_ge(self.sems[si], val)
        eng.seen[si] = val
        self.ninst += 1

    def _deps(self, eng, reads, writes):
        ex = [r for r in reads if r.excl]
        if ex:
            writes = list(writes) + ex
        for r in reads:
            if r.w is not None:
                self.wait(eng, r.w)
        for w in writes:
            if w.w is not None:
                self.wait(eng, w.w)
            for si, val in w.rs.items():
                self.wait(eng, (si, val))

    def _commit(self, me, reads, writes):
        si, val = me
        ex = [r for r in reads if r.excl]
        if ex:
            writes = list(writes) + ex
        for r in reads:
            if r.rs.get(si, 0) < val:
                r.rs[si] = val
        for w in writes:
            w.w = me
            w.rs = {}

    def op(self, eng, fn, reads=(), writes=()):
        self._deps(eng, reads, writes)
        inst = fn()
        eng.cnt += 1
        inst.then_inc(self.sems[eng.si], 1)
        self.ninst += 1
        self._commit((eng.si, eng.cnt), reads, writes)
        return inst

    def group(self, eng, fns, reads=(), writes=()):
        self._deps(eng, reads, writes)
        inst = None
        for fn in fns:
            inst = fn()
            self.ninst += 1
        eng.cnt += 1
        inst.then_inc(self.sems[eng.si], 1)
        self._commit((eng.si, eng.cnt), reads, writes)

    def dma(self, q, out, in_, reads=(), writes=()):
        slot = q.dma_next
        q.dma_next = (slot + 1) % len(q.dma_sems)
        si = q.dma_sems[slot]
        prev = q.dma_vals[slot]
        if prev > 0:
            self.wait(q, (si, prev))
        self._deps(q, reads, writes)
        inst = q.h.dma_start(out=out, in_=in_)
        inst.then_inc(self.sems[si], 16)
        q.dma_vals[slot] = prev + 16
        self.ninst += 1
        self._commit((si, prev + 16), reads, writes)

    def barrier(self, engines=None):
        engines = engines or (self.pe, self.act, self.dve, self.sp)
        evs = [(e.si, e.cnt) for e in (self.pe, self.act, self.dve) if e.cnt > 0]
        for q in (self.sp, self.pool):
            for si, v in zip(q.dma_sems, q.dma_vals):
                if v > 0 and q is self.sp:
                    evs.append((si, v))
        for e in engines:
            for ev in evs:
                if ev[0] == e.si:
                    continue
                self.wait(e, ev)

    def psum(self):
        b = self.banks[self.bank_i]
        self.bank_i = (self.bank_i + 1) % 6
        return b

    def psum_hold(self, i):
        return self.banks[6 + (i % 2)]

    def wslot(self):
        s = self.wslots[self.w_i]
        self.w_i = (self.w_i + 1) % self.NW
        return s

    def tile(self, name, shape, dtype, nchunk=None, es=None):
        if es is None:
            t = self.nc.alloc_sbuf_tensor(self.name(name), list(shape), dtype)
        else:
            t = es.enter_context(self.nc.sbuf_tensor(self.name(name), list(shape), dtype))
        return Tile(t, nchunk if nchunk is not None else (shape[1] if len(shape) == 3 else 1))

    def wload(self, Wd, k0, kn, n0, nn):
        s = self.wslot()
        if kn >= 128:
            assert kn % 128 == 0
            kc = kn // 128
            src = Wd[k0:k0 + kn, n0:n0 + nn].rearrange("(c p) n -> p c n", p=128)
            self.dma(self.pool, s.t[:, 0:kc, 0:nn], src, writes=[s.r[0]])
        else:
            kc = 1
            self.dma(self.pool, s.t[0:kn, 0, 0:nn], Wd[k0:k0 + kn, n0:n0 + nn], writes=[s.r[0]])
        return s, kc

    def wload_ap(self, src, kn, nn, q=None):
        s = self.wslot()
        q = q or self.pool
        if kn >= 128:
            assert kn % 128 == 0
            kc = kn // 128
            self.dma(q, s.t[:, 0:kc, 0:nn], src.rearrange("(c p) n -> p c n", p=128), writes=[s.r[0]])
        else:
            kc = 1
            self.dma(q, s.t[0:kn, 0, 0:nn], src, writes=[s.r[0]])
        return s, kc

    def mm(self, ps_ap, lhsT, rhs, start, stop):
        nc = self.nc
        return lambda: nc.tensor.matmul(ps_ap, lhsT, rhs, start=start, stop=stop)

    def linear_fm(self, Wd, K_, n0, n1, acts, evac, hook=None):
        nkb = (K_ + 2047) // 2048
        if nkb > 1:
            assert len(acts) == 1
        for nb in range(n0, n1, 512):
            nn = min(512, n1 - nb)
            ncj = (nn + 127) // 128
            if nkb == 1:
                s, kc = self.wload(Wd, 0, K_, nb, nn)
                kp = min(K_, 128)
                for tt, (aps, res, T) in enumerate(acts):
                    for j in range(ncj):
                        mw = min(128, nn - j * 128)
                        b = self.psum()
                        fns = [self.mm(b.t[0:mw, 0:T], s.t[0:kp, c, j * 128:j * 128 + mw], aps[c],
                                       c == 0, c == kc - 1) for c in range(kc)]
                        self.group(self.pe, fns, reads=[s.r[0]] + list(res), writes=[b.r])
                        evac(b, nb // 128 + j, tt, mw)
            else:
                aps, res, T = acts[0]
                bs = [self.psum() for _ in range(ncj)]
                for kb in range(nkb):
                    k0 = kb * 2048
                    kn = min(2048, K_ - k0)
                    s, kc = self.wload(Wd, k0, kn, nb, nn)
                    for j in range(ncj):
                        mw = min(128, nn - j * 128)
                        b = bs[j]
                        fns = [self.mm(b.t[0:mw, 0:T], s.t[:, c, j * 128:j * 128 + mw], aps[k0 // 128 + c],
                                       kb == 0 and c == 0, kb == nkb - 1 and c == kc - 1) for c in range(kc)]
                        self.group(self.pe, fns, reads=[s.r[0]] + list(res[k0 // 128:k0 // 128 + kc]),
                                   writes=[b.r])
                    if hook is not None and kb < nkb - 1:
                        hook()
                for j in range(ncj):
                    evac(bs[j], nb // 128 + j, 0, min(128, nn - j * 128))
                if hook is not None:
                    hook()


class StopBuild(Exception):
    pass


class Builder:
    def stop(self, n):
        if self.cfg.get("stop") == n:
            raise StopBuild()

    def __init__(self, cfg):
        self.cfg = cfg
        nc = self.nc = bass.Bass("TRN2", target_bir_lowering=False)
        self.k = K(nc)
        self.din = {}
        self.dout = {}

    def inp(self, name, shape, dtype=F32):
        t = self.nc.dram_tensor(name, list(shape), dtype, kind="ExternalInput").ap()
        self.din[name] = t
        return t

    def outp(self, name, shape, dtype=F32):
        t = self.nc.dram_tensor(name, list(shape), dtype, kind="ExternalOutput").ap()
        self.dout[name] = t
        return t

    def end_phase(self, es):
        self.k.barrier()
        es.close()

    def consts(self):
        k, nc = self.k, self.nc
        ident_d = self.inp("ident", [128, 128])
        self.ident = k.tile("ident", [128, 1, 128], F32)
        k.dma(k.sp, self.ident[0], ident_d[:, :], writes=[self.ident.r[0]])
        self.ones_bf = k.tile("ones", [128, 1, 128], BF16)
        k.op(k.dve, lambda: nc.vector.memset(self.ones_bf[0], 1.0), writes=[self.ones_bf.r[0]])
        self.epsT = k.tile("eps", [128, 1, 1], F32)
        k.op(k.dve, lambda: nc.vector.memset(self.epsT[0], EPS), writes=[self.epsT.r[0]])
        self.sqbuf = k.tile("sqbuf", [128, 2, 512], BF16)
        self.ntmp = k.tile("ntmp", [128, 2, 512], F32)
        self.rstd = k.tile("rstd", [128, 1, 512], F32)
        self.rowtmp = k.tile("rowtmp", [1, 1, 512], F32)
        self.rows_t = k.tile("rows_t", [128, 1, 128], F32)
        self.ones_row = k.tile("ones_row", [1, 1, 128], F32)
        k.op(k.dve, lambda: nc.vector.memset(self.ones_row.t[0:1, 0, :], 1.0), writes=[self.ones_row.r[0]])
        self.maskL = k.tile("maskL", [128, 1, 384], F32)
        k.dma(k.sp, self.maskL.t[:, 0, :], self.inp("maskL", [128, 384])[:, :], writes=[self.maskL.r[0]])

    def mods_layer_gen(self, l, csT, btm, mtm):
        k, nc = self.k, self.nc
        mod_w = self.din["mod_w"]
        mod_b = self.din["mod_b"]
        m = self.mods[l]
        macc = k.psum_hold(l)
        for nb in range(0, 18432, 512):
            mi = (nb // 512) % 2
            for r in range(2):
                k.dma(k.sp, btm.t[r:r + 1, mi, :], mod_b[l:l + 1, nb:nb + 512], writes=[btm.r[mi]])
            s, kc = k.wload(mod_w[l], 0, 2048, nb, 512)
            bk = k.psum()
            fns = [k.mm(bk.t[0:2, 0:512], csT.t[:, 0, 2 * c:2 * c + 2], s.t[:, c, 0:512], c == 0, c == 15)
                   for c in range(16)]
            k.group(k.pe, fns, reads=[s.r[0], csT.r[0]], writes=[bk.r])
            k.op(k.dve, lambda bk=bk, mi=mi: nc.vector.tensor_tensor(
                out=mtm.t[0:2, mi, :], in0=bk.t[0:2, 0:512], in1=btm.t[0:2, mi, :], op=ALU.add),
                reads=[bk.r, btm.r[mi]], writes=[mtm.r[mi]])
            fns = [(lambda j=j, mi=mi, nb=nb: nc.tensor.transpose(
                macc.t[:, 2 * (nb // 128 + j):2 * (nb // 128 + j) + 2], mtm.t[0:2, mi, j * 128:(j + 1) * 128],
                self.ident.t[0:2, 0, 0:2])) for j in range(4)]
            k.group(k.pe, fns, reads=[mtm.r[mi], self.ident.r[0]], writes=[macc.r])
            yield
        k.op(k.dve, lambda m=m, macc=macc: nc.vector.tensor_copy(out=m.t[:, 0, :], in_=macc.t[:, 0:288]),
             reads=[macc.r], writes=[m.r[0]])
        self.cos[l] = [self.mod_coeffs(l, col) for col in range(2)]
        yield

    def mods_all(self):
        k, nc = self.k, self.nc
        c2 = self.din["c2"]
        nlayers = self.cfg.get("nlayers", 2)
        self.mods = [k.tile(f"mods{l}", [128, 1, 288], F32) for l in range(nlayers)]
        self.gfm = k.tile("gfm", [128, 1, 12 * 16], F32)
        self.cotiles = [[k.tile(f"co{l}_{col}", [128, 9, 16], F32, nchunk=1) for col in range(2)] for l in range(nlayers)]
        csT = k.tile("csT", [128, 1, 32], BF16)
        btm = k.tile("btm", [2, 2, 512], F32)
        mtm = k.tile("mtm", [2, 2, 512], F32)
        es = ExitStack()
        ctm = k.tile("ctm", [2, 1, 2048], F32, es=es)
        gtm = k.tile("gtm", [12, 1, 2048], F32, es=es)
        k.dma(k.sp, ctm.t[0:2, 0, :], c2[:, :], writes=[ctm.r[0]])
        k.op(k.act, lambda: nc.scalar.activation(out=ctm.t[0:2, 0, :], in_=ctm.t[0:2, 0, :], func=AF.Silu),
             reads=[ctm.r[0]], writes=[ctm.r[0]])
        b = k.psum()
        fns = [(lambda c=c: nc.tensor.transpose(b.t[:, 2 * c:2 * c + 2], ctm.t[0:2, 0, c * 128:(c + 1) * 128],
                                                self.ident.t[0:2, 0, 0:2])) for c in range(16)]
        k.group(k.pe, fns, reads=[ctm.r[0], self.ident.r[0]], writes=[b.r])
        k.op(k.dve, lambda: nc.vector.tensor_copy(out=csT.t[:, 0, :], in_=b.t[:, 0:32]), reads=[b.r], writes=[csT.r[0]])
        ng = self.din["norm_g"]
        k.dma(k.sp, gtm.t[0:12, 0, :], ng[:, :], writes=[gtm.r[0]])
        b = k.psum()
        fns = [(lambda c=c: nc.tensor.transpose(b.t[:, 12 * c:12 * c + 12], gtm.t[0:12, 0, c * 128:(c + 1) * 128],
                                                self.ident.t[0:12, 0, 0:12])) for c in range(16)]
        k.group(k.pe, fns, reads=[gtm.r[0], self.ident.r[0]], writes=[b.r])
        k.op(k.dve, lambda: nc.vector.tensor_copy(
            out=self.gfm.t[:, 0, :].rearrange("p (r c) -> p r c", c=16),
            in_=b.t[:, 0:192].rearrange("p (c r) -> p r c", r=12)), reads=[b.r], writes=[self.gfm.r[0]])
        self.cos = [None] * nlayers
        if "mix" in self.cfg.get("parts", ("ffn0", "mix", "ffn1")) and 0 in self.cfg.get("layers", range(nlayers)):
            for sfx, L_ in (("S", LS), ("P", LP)):
                for cb in range(2):
                    self.bg.append(self.hyena_filters_gen(0, sfx, L_, cb))
        for _ in self.mods_layer_gen(0, csT, btm, mtm):
            self.bg_step(2)
        self.bg_drain()
        self.end_phase(es)
        for l in range(1, nlayers):
            self.bg.append(self.mods_layer_gen(l, csT, btm, mtm))
        if not self.cfg.get("bg_mods", True):
            self.bg_drain()

    def mod_coeffs(self, l, col):
        k, nc = self.k, self.nc
        m = self.mods[l]
        mv = m.t[:, 0, :].rearrange("p (jc two) -> p jc two", two=2)
        co = self.cotiles[l][col]
        g = self.gfm.t[:, 0, :].rearrange("p (r c) -> p r c", c=16)
        for j in range(3):
            shift = mv[:, (3 * j) * 16:(3 * j) * 16 + 16, col]
            scale = mv[:, (3 * j + 1) * 16:(3 * j + 1) * 16 + 16, col]
            gate = mv[:, (3 * j + 2) * 16:(3 * j + 2) * 16 + 16, col]
            gpre = g[:, l * 6 + 2 * j, :]
            gpost = g[:, l * 6 + 2 * j + 1, :]
            wmac = 1.0 if j == 1 else 0.5
            k.op(k.dve, lambda j=j, scale=scale, gpre=gpre: nc.vector.scalar_tensor_tensor(
                out=co.t[:, 3 * j, :], in0=scale, scalar=1.0, in1=gpre, op0=ALU.add, op1=ALU.mult),
                reads=[m.r[0], self.gfm.r[0]], writes=[co.r[0]])
            k.op(k.dve, lambda j=j, shift=shift: nc.vector.tensor_copy(out=co.t[:, 3 * j + 1, :], in_=shift),
                 reads=[m.r[0]], writes=[co.r[0]])
            k.op(k.dve, lambda j=j, gate=gate, gpost=gpost, wmac=wmac: nc.vector.scalar_tensor_tensor(
                out=co.t[:, 3 * j + 2, :], in0=gate, scalar=wmac, in1=gpost, op0=ALU.mult, op1=ALU.mult),
                reads=[m.r[0], self.gfm.r[0]], writes=[co.r[0]])
        return co

    def xres_tile_ap(self, ti):
        return self.xres[:, :, ti * 512:(ti + 1) * 512].rearrange("c p t -> p c t")

    def load_x_all(self, srcs):
        k, nc = self.k, self.nc
        es = ExitStack()
        st = k.tile("xstage", [128, 2, D], F32, es=es)
        xt = k.tile("xt0", [128, 16, 512], F32, es=es)
        tb_g = 0
        for xd, T in srcs:
            for tb in range(T // 128):
                si = tb_g % 2
                k.dma(k.sp, st.t[:, si, :], xd[tb * 128:(tb + 1) * 128, :], writes=[st.r[si]])
                ti, to = divmod(tb_g * 128, 512)
                for c4 in range(4):
                    b = k.psum()
                    fns = [(lambda c=c, i=i: nc.tensor.transpose(b.t[:, i * 128:(i + 1) * 128], st.t[:, si, c * 128:(c + 1) * 128],
                                                                 self.ident.t[:, 0, :])) for i, c in enumerate(range(c4 * 4, c4 * 4 + 4))]
                    k.group(k.pe, fns, reads=[st.r[si], self.ident.r[0]], writes=[b.r])
                    dst = xt.t[:, c4 * 4:c4 * 4 + 4, to:to + 128]
                    src = b.t[:, 0:512].rearrange("p (i t) -> p i t", t=128)
                    if c4 % 2 == 0:
                        k.op(k.act, lambda dst=dst, src=src: nc.scalar.copy(out=dst, in_=src), reads=[b.r], writes=[xt.r[0]])
                    else:
                        k.op(k.dve, lambda dst=dst, src=src: nc.vector.tensor_copy(out=dst, in_=src), reads=[b.r], writes=[xt.r[0]])
                if to == 384:
                    k.dma(k.sp, self.xres_tile_ap(ti), xt.t[:, :, :], reads=[xt.r[0]], writes=[self.xres_r[ti]])
                tb_g += 1
        self.end_phase(es)

    def store_x_all(self, dsts):
        k, nc = self.k, self.nc
        es = ExitStack()
        st = k.tile("ystage", [128, 2, D], F32, es=es)
        xt = k.tile("xt1", [128, 16, 512], F32, es=es)
        tb_g = 0
        for yd, T in dsts:
            for tb in range(T // 128):
                si = tb_g % 2
                ti, to = divmod(tb_g * 128, 512)
                if to == 0:
                    k.dma(k.sp, xt.t[:, :, :], self.xres_tile_ap(ti), reads=[self.xres_r[ti]], writes=[xt.r[0]])
                for c4 in range(4):
                    b = k.psum()
                    fns = [(lambda c=c, i=i: nc.tensor.transpose(b.t[:, i * 128:(i + 1) * 128], xt.t[:, c, to:to + 128],
                                                                 self.ident.t[:, 0, :])) for i, c in enumerate(range(c4 * 4, c4 * 4 + 4))]
                    k.group(k.pe, fns, reads=[xt.r[0], self.ident.r[0]], writes=[b.r])
                    dst = st.t[:, si, c4 * 512:(c4 + 1) * 512]
                    if c4 % 2 == 0:
                        k.op(k.act, lambda dst=dst, b=b: nc.scalar.copy(out=dst, in_=b.t[:, 0:512]), reads=[b.r], writes=[st.r[si]])
                    else:
                        k.op(k.dve, lambda dst=dst, b=b: nc.vector.tensor_copy(out=dst, in_=b.t[:, 0:512]), reads=[b.r], writes=[st.r[si]])
                k.dma(k.sp, yd[tb * 128:(tb + 1) * 128, :], st.t[:, si, :], reads=[st.r[si]])
                tb_g += 1
        self.end_phase(es)

    def rstd_of(self, srcs, res, T, out_t):
        k, nc = self.k, self.nc
        bk = k.psum()
        sq = self.sqbuf
        for c in range(16):
            i = c % 2
            k.op(k.act, lambda c=c, i=i: nc.scalar.activation(out=sq.t[:, i, 0:T], in_=srcs[c], func=AF.Square),
                 reads=[res[c]], writes=[sq.r[i]])
            k.op(k.pe, k.mm(bk.t[:, 0:T], self.ones_bf.t[:, 0, :], sq.t[:, i, 0:T], c == 0, c == 15),
                 reads=[sq.r[i], self.ones_bf.r[0]], writes=[bk.r])
        k.op(k.act, lambda: nc.scalar.activation(out=out_t.t[:, 0, 0:T], in_=bk.t[:, 0:T], func=AF.Sqrt,
                                                 scale=1.0 / D, bias=self.epsT.t[:, 0, :]),
             reads=[bk.r, self.epsT.r[0]], writes=[out_t.r[0]])
        k.op(k.dve, lambda: nc.vector.reciprocal(out=out_t.t[:, 0, 0:T], in_=out_t.t[:, 0, 0:T]),
             reads=[out_t.r[0]], writes=[out_t.r[0]])

    def modulate(self, xt, T, co, j, h, ho=0, hcol=0):
        k, nc = self.k, self.nc
        srcs = [xt.t[:, c, 0:T] for c in range(16)]
        res = [xt.r[0]] * 16
        self.rstd_of(srcs, res, T, self.rstd)
        tmp = self.ntmp
        for c in range(16):
            i = c % 2
            k.op(k.dve, lambda c=c, i=i: nc.vector.tensor_tensor(out=tmp.t[:, i, 0:T], in0=srcs[c], in1=self.rstd.t[:, 0, 0:T], op=ALU.mult),
                 reads=[res[c], self.rstd.r[0]], writes=[tmp.r[i]])
            k.op(k.act, lambda c=c, i=i: nc.scalar.activation(out=h.t[:, ho + c, hcol:hcol + T], in_=tmp.t[:, i, 0:T], func=AF.Identity,
                                                              scale=co.t[:, 3 * j, c:c + 1], bias=co.t[:, 3 * j + 1, c:c + 1]),
                 reads=[tmp.r[i], co.r[0]], writes=[h.r[ho + c]])

    def residual(self, xt, T, co, j, out, oo=0, ocol=0):
        k, nc = self.k, self.nc
        srcs = [out.t[:, oo + c, ocol:ocol + T] for c in range(16)]
        res = [out.r[oo + c] for c in range(16)]
        self.rstd_of(srcs, res, T, self.rstd)
        tmp = self.ntmp
        for c in range(16):
            i = c % 2
            k.op(k.dve, lambda c=c, i=i: nc.vector.tensor_tensor(out=tmp.t[:, i, 0:T], in0=srcs[c], in1=self.rstd.t[:, 0, 0:T], op=ALU.mult),
                 reads=[res[c], self.rstd.r[0]], writes=[tmp.r[i]])
            k.op(k.dve, lambda c=c, i=i: nc.vector.scalar_tensor_tensor(
                out=xt.t[:, c, 0:T], in0=tmp.t[:, i, 0:T], scalar=co.t[:, 3 * j + 2, c:c + 1], in1=xt.t[:, c, 0:T],
                op0=ALU.mult, op1=ALU.add), reads=[tmp.r[i], co.r[0], xt.r[0]], writes=[xt.r[0]])

    def ffn_phase(self, l, fi, j, tiles):
        k, nc = self.k, self.nc
        es = ExitStack()
        T = 512
        wg = self.din["ffn_wg"][l, fi]
        wu = self.din["ffn_wu"][l, fi]
        wd = self.din["ffn_wd"][l, fi]
        xt = k.tile("xt", [128, 16, 512], F32, nchunk=1, es=es)
        hh = [k.tile("h", [128, 16, 512], BF16, es=es) for _ in range(2)]
        hid = k.tile("hid", [128, FC, 512], BF16, es=es)
        out = k.tile("fout", [128, 16, 512], BF16, es=es)
        sg = k.tile("sgt", [128, 4, 512], BF16, es=es)
        tiles = list(tiles)

        def co_of(ti):
            return self.cos[l][0 if ti < 2 else 1]

        def prenorm(ti, h):
            k.dma(k.sp, xt.t[:, :, :], self.xres_tile_ap(ti), reads=[self.xres_r[ti]], writes=[xt.r[0]])
            self.modulate(xt, T, co_of(ti), j, h)

        def resid(ti):
            k.dma(k.sp, xt.t[:, :, :], self.xres_tile_ap(ti), reads=[self.xres_r[ti]], writes=[xt.r[0]])
            self.residual(xt, T, co_of(ti), j, out)
            k.dma(k.sp, self.xres_tile_ap(ti), xt.t[:, :, :], reads=[xt.r[0]], writes=[self.xres_r[ti]])

        def phase_a(h, nb0, nb1):
            haps = [h.t[:, c, 0:T] for c in range(16)]
            for nb in range(nb0, nb1, 512):
                sg_, kc = k.wload(wg, 0, 2048, nb, 512)
                su_, _ = k.wload(wu, 0, 2048, nb, 512)
                for jj in range(4):
                    bg = k.psum()
                    k.group(k.pe, [k.mm(bg.t[:, 0:T], sg_.t[:, c, jj * 128:(jj + 1) * 128], haps[c], c == 0, c == 15) for c in range(16)],
                            reads=[sg_.r[0]] + h.r, writes=[bg.r])
                    k.op(k.act, lambda bg=bg, jj=jj: nc.scalar.activation(out=sg.t[:, jj, :], in_=bg.t[:, 0:T], func=AF.Silu),
                         reads=[bg.r], writes=[sg.r[jj]])
                for jj in range(4):
                    f = nb // 128 + jj
                    bu = k.psum()
                    k.group(k.pe, [k.mm(bu.t[:, 0:T], su_.t[:, c, jj * 128:(jj + 1) * 128], haps[c], c == 0, c == 15) for c in range(16)],
                            reads=[su_.r[0]] + h.r, writes=[bu.r])
                    k.op(k.dve, lambda bu=bu, jj=jj, f=f: nc.vector.tensor_tensor(out=hid.t[:, f, :], in0=sg.t[:, jj, :], in1=bu.t[:, 0:T], op=ALU.mult),
                         reads=[bu.r, sg.r[jj]], writes=[hid.r[f]])

        def evac(b, nch, tt, rows):
            k.op(k.act, lambda: nc.scalar.copy(out=out.t[:, nch, :], in_=b.t[:, 0:T]), reads=[b.r], writes=[out.r[nch]])

        def phase_b(n0, n1):
            k.linear_fm(wd, DFF, n0, n1, [([hid.t[:, f, :] for f in range(FC)], hid.r, T)], evac, hook=lambda: self.bg_step(1))

        prenorm(tiles[0], hh[0])
        for i, ti in enumerate(tiles):
            h = hh[i % 2]
            phase_a(h, 0, 1024)
            if i > 0:
                resid(tiles[i - 1])
            phase_a(h, 1024, DFF)
            phase_b(0, 1024)
            if i + 1 < len(tiles):
                prenorm(tiles[i + 1], hh[(i + 1) % 2])
            phase_b(1024, D)
        resid(tiles[-1])
        self.bg_drain()
        self.end_phase(es)

    def ev(self, i):
        return self.k.act if i % 2 == 0 else self.k.dve

    def copy_op(self, eng, out, in_, reads, writes):
        k, nc = self.k, self.nc
        if eng is k.act:
            k.op(eng, lambda: nc.scalar.copy(out=out, in_=in_), reads=reads, writes=writes)
        else:
            k.op(eng, lambda: nc.vector.tensor_copy(out=out, in_=in_), reads=reads, writes=writes)

    def bcast_row(self, dst_ap, dst_res, src_dram, n):
        k, nc = self.k, self.nc
        row = self.rowtmp
        k.dma(k.sp, row.t[0:1, 0, 0:n], src_dram, writes=[row.r[0]])
        b = k.psum()
        k.op(k.pe, k.mm(b.t[:, 0:n], self.ones_row.t[0:1, 0, :], row.t[0:1, 0, 0:n], True, True),
             reads=[row.r[0], self.ones_row.r[0]], writes=[b.r])
        self.copy_op(k.dve, dst_ap, b.t[:, 0:n], [b.r], [dst_res])

    def load_rows_fm(self, dst_ap, dst_res, src2d, nrows):
        k, nc = self.k, self.nc
        tmp = self.rows_t
        k.dma(k.sp, tmp.t[0:nrows, 0, :], src2d, writes=[tmp.r[0]])
        b = k.psum()
        k.op(k.pe, lambda: nc.tensor.transpose(b.t[:, 0:nrows], tmp.t[0:nrows, 0, :], self.ident.t[0:nrows, 0, 0:nrows]),
             reads=[tmp.r[0], self.ident.r[0]], writes=[b.r])
        self.copy_op(k.dve, dst_ap, b.t[:, 0:nrows], [b.r], [dst_res])

    def prenorm_unit(self, unit, co, j, h, es):
        k = self.k
        key = (id(co), unit["t0"])
        t0, Tu = unit["t0"], unit["Tu"]
        hd_ap = self.h_d[:, :, t0:t0 + Tu].rearrange("c p t -> p c t")
        if self.h_key == key:
            k.dma(k.sp, h.t[:, :, :], hd_ap, reads=[self.h_r], writes=h.r)
            return
        xt = k.tile("xtp", [128, 16, 512], F32, nchunk=1, es=es)
        for i, ti in enumerate(unit["tiles"]):
            k.dma(k.sp, xt.t[:, :, :], self.xres_tile_ap(ti), reads=[self.xres_r[ti]], writes=[xt.r[0]])
            self.modulate(xt, 512, co, j, h, ho=0, hcol=i * 512)
        k.dma(k.sp, hd_ap, h.t[:, :, :], reads=h.r, writes=[self.h_r])
        self.h_key = key

    def h_acts(self, h, unit):
        return [([h.t[:, c, i * 512:(i + 1) * 512] for c in range(16)], h.r, 512) for i in range(len(unit["tiles"]))]

    def mo_ap(self, unit, c0, nc_):
        t0 = unit["t0"]
        return self.mo_d[c0:c0 + nc_, :, t0:t0 + unit["Tu"]].rearrange("c p t -> p c t")

    def residual_unit(self, unit, co, j, o, es):
        k = self.k
        xt = k.tile("xtr", [128, 16, 512], F32, nchunk=1, es=es)
        for i, ti in enumerate(unit["tiles"]):
            k.dma(k.sp, xt.t[:, :, :], self.xres_tile_ap(ti), reads=[self.xres_r[ti]], writes=[xt.r[0]])
            self.residual(xt, 512, co, j, o, oo=0, ocol=i * 512)
            k.dma(k.sp, self.xres_tile_ap(ti), xt.t[:, :, :], reads=[xt.r[0]], writes=[self.xres_r[ti]])

    def out_proj_unit(self, unit, W_out, co):
        k, nc = self.k, self.nc
        es = ExitStack()
        Tu = unit["Tu"]
        mo = k.tile("mo", [128, 16, Tu], BF16, es=es)
        o = k.tile("o", [128, 16, Tu], BF16, es=es)
        k.dma(k.sp, mo.t[:, :, :], self.mo_ap(unit, 0, 16), reads=self.mo_rs, writes=mo.r)
        acts = [([mo.t[:, c, i * 512:(i + 1) * 512] for c in range(16)], mo.r, 512) for i in range(Tu // 512)]

        def evac(b, nch, tt, rows):
            self.copy_op(self.ev(nch + tt), o.t[:, nch, tt * 512:(tt + 1) * 512], b.t[:, 0:512], [b.r], [o.r[nch]])
        k.linear_fm(W_out, D, 0, D, acts, evac)
        self.residual_unit(unit, co, 1, o, es)
        self.end_phase(es)

    def rope_inplace(self, x, chunks, Tu, perm, cos, sin, tmp):
        k, nc = self.k, self.nc
        for c in chunks:
            for tt in range(Tu // 512):
                sl = slice(tt * 512, (tt + 1) * 512)
                b = k.psum()
                k.op(k.pe, k.mm(b.t[:, 0:512], perm.t[:, 0, :], x.t[:, c, sl], True, True), reads=[perm.r[0], x.r[c]], writes=[b.r])
                k.op(k.dve, lambda c=c, sl=sl: nc.vector.tensor_tensor(out=tmp.t[:, 0, :], in0=x.t[:, c, sl], in1=cos.t[:, 0, sl], op=ALU.mult),
                     reads=[x.r[c], cos.r[0]], writes=[tmp.r[0]])
                k.op(k.dve, lambda b=b, sl=sl: nc.vector.tensor_tensor(out=tmp.t[:, 1, :], in0=b.t[:, 0:512], in1=sin.t[:, 0, sl], op=ALU.mult),
                     reads=[b.r, sin.r[0]], writes=[tmp.r[1]])
                k.op(k.dve, lambda c=c, sl=sl: nc.vector.tensor_tensor(out=x.t[:, c, sl], in0=tmp.t[:, 0, :], in1=tmp.t[:, 1, :], op=ALU.add),
                     reads=[tmp.r[0], tmp.r[1]], writes=[x.r[c]])

    def attn_stage_a(self, blk, wk):
        k, nc = self.k, self.nc
        qparts, segs, sink_ap, sink_res, scale = blk["qparts"], blk["segs"], blk["sink"], blk["sink_res"], blk["scale"]
        nseg = len(segs)
        p, st = wk["p"], wk["st"]
        srcs = []
        off = 0
        qres = [r for _, r in qparts]
        for i, (n, kaps, kres, mask) in enumerate(segs):
            b = k.psum()
            np_ = len(qparts)
            k.group(k.pe, [k.mm(b.t[:, 0:n], qparts[a][0], kaps[a], a == 0, a == np_ - 1) for a in range(np_)],
                    reads=qres + list(kres), writes=[b.r])
            if mask is not None:
                dst = p.t[:, 0, off:off + n]
                k.op(k.dve, lambda b=b, n=n, dst=dst, mask=mask: nc.vector.tensor_tensor(out=dst, in0=b.t[:, 0:n], in1=mask, op=ALU.add),
                     reads=[b.r, self.maskL.r[0]], writes=[p.r[0]])
                srcs.append((dst, p.r[0], off, n))
            else:
                srcs.append((b.t[:, 0:n], b.r, off, n))
            off += n
        blk["nkeys"] = off
        for i, (src, sres, o_, n) in enumerate(srcs):
            k.op(k.dve, lambda src=src, i=i: nc.vector.reduce_max(out=st.t[:, 0, i:i + 1], in_=src, axis=mybir.AxisListType.X),
                 reads=[sres], writes=[st.r[0]])
        if nseg > 1:
            k.op(k.dve, lambda: nc.vector.reduce_max(out=st.t[:, 0, 4:5], in_=st.t[:, 0, 0:nseg], axis=mybir.AxisListType.X),
                 reads=[st.r[0]], writes=[st.r[0]])
            rm = st.t[:, 0, 4:5]
        else:
            rm = st.t[:, 0, 0:1]
        negm = st.t[:, 0, 5:6]
        if sink_ap is not None:
            k.op(k.dve, lambda: nc.vector.tensor_scalar(out=negm, in0=rm, scalar1=scale, scalar2=sink_ap, op0=ALU.mult, op1=ALU.max),
                 reads=[st.r[0], sink_res], writes=[st.r[0]])
            k.op(k.dve, lambda: nc.vector.tensor_scalar(out=negm, in0=negm, scalar1=-1.0, scalar2=None, op0=ALU.mult),
                 reads=[st.r[0]], writes=[st.r[0]])
        else:
            k.op(k.dve, lambda: nc.vector.tensor_scalar(out=negm, in0=rm, scalar1=-scale, scalar2=None, op0=ALU.mult),
                 reads=[st.r[0]], writes=[st.r[0]])
        sm = wk["sm"]
        for i, (src, sres, o_, n) in enumerate(srcs):
            k.op(k.act, lambda src=src, o_=o_, n=n, i=i: nc.scalar.activation(
                out=p.t[:, 0, o_:o_ + n], in_=src, func=AF.Exp, scale=scale, bias=negm, accum_out=sm.t[:, 0, i:i + 1]),
                reads=[sres, st.r[0]], writes=[p.r[0], sm.r[0]])
        ns = nseg
        if sink_ap is not None:
            k.op(k.act, lambda: nc.scalar.activation(out=sm.t[:, 0, nseg:nseg + 1], in_=negm, func=AF.Exp, scale=1.0, bias=sink_ap),
                 reads=[st.r[0], sink_res], writes=[sm.r[0]])
            ns += 1
        blk["ns"] = ns

    def attn_stage_b(self, blk, wk):
        k, nc = self.k, self.nc
        p, pT, sm = wk["p"], wk["pT"], wk["sm"]
        nkeys, ns, vblocks, outbank, outcol = blk["nkeys"], blk["ns"], blk["vbl"], blk["ob"], blk["outcol"]
        den = sm.t[:, 0, 6:7]
        k.op(k.dve, lambda: nc.vector.reduce_sum(out=den, in_=sm.t[:, 0, 0:ns], axis=mybir.AxisListType.X),
             reads=[sm.r[0]], writes=[sm.r[0]])
        k.op(k.dve, lambda: nc.vector.reciprocal(out=den, in_=den), reads=[sm.r[0]], writes=[sm.r[0]])
        k.op(k.dve, lambda: nc.vector.tensor_scalar(out=p.t[:, 0, 0:nkeys], in0=p.t[:, 0, 0:nkeys], scalar1=den, scalar2=None, op0=ALU.mult),
             reads=[p.r[0], sm.r[0]], writes=[p.r[0]])
        nkb = nkeys // 128
        for g0 in range(0, nkb, 4):
            gn = min(4, nkb - g0)
            b = k.psum()
            k.group(k.pe, [(lambda i=i: nc.tensor.transpose(b.t[:, i * 128:(i + 1) * 128], p.t[:, 0, (g0 + i) * 128:(g0 + i + 1) * 128], self.ident.t[:, 0, :]))
                           for i in range(gn)], reads=[p.r[0], self.ident.r[0]], writes=[b.r])
            self.copy_op(self.ev(g0 // 4), pT.t[:, g0:g0 + gn, :], b.t[:, 0:gn * 128].rearrange("p (i t) -> p i t", t=128), [b.r], [pT.r[0]])
        k.group(k.pe, [k.mm(outbank.t[:, outcol:outcol + 128], vblocks[kb][0], pT.t[:, kb, :], kb == 0, kb == nkb - 1) for kb in range(nkb)],
                reads=[pT.r[0]] + [r for _, r in vblocks], writes=[outbank.r])
        if blk.get("post"):
            blk["post"]()

    def attn_run(self, blocks, wks):
        n = len(blocks)
        nw = len(wks)
        if n == 0:
            return
        self.attn_stage_a(blocks[0], wks[0])
        for i in range(n):
            if i + 1 < n:
                self.attn_stage_a(blocks[i + 1], wks[(i + 1) % nw])
            self.attn_stage_b(blocks[i], wks[i % nw])

    def attn_wks(self, es, pw, npt, n=3):
        k = self.k
        return [{"p": k.tile("p", [128, 1, pw], F32, es=es), "pT": k.tile("pT", [128, npt, 128], BF16, nchunk=1, es=es),
                 "st": k.tile("st", [128, 1, 8], F32, es=es), "sm": k.tile("sm", [128, 1, 8], F32, es=es)} for _ in range(n)]

    def even_attention(self, l, unit):
        k, nc = self.k, self.nc
        e = l // 2
        Tu, col, isS = unit["Tu"], unit["col"], unit["isS"]
        co = self.cos[l][col]
        W_in = self.din["ev_w_in"][e]
        ntb = Tu // 128
        es = ExitStack()
        qT = k.tile("qT", [128, 8, Tu], BF16, es=es)
        kT = k.tile("kT", [128, 2, Tu], BF16, es=es)
        kv_tm = k.tile("kv_tm", [128, 1 if isS else ntb, 512], F32, es=es)
        v_bf = k.tile("v_bf", [128, ntb, 256], BF16, es=es)
        sinkb = k.tile("sinkb", [128, 1, 8], F32, es=es)
        ostage = k.tile("ostage", [128, 2, Tu], BF16, es=es)
        wk = self.attn_wks(es, 896 if isS else 256, 7 if isS else 2)
        self.bcast_row(sinkb.t[:, 0, :], sinkb.r[0], self.din["attn_sink"][e:e + 1, :], 8)
        if isS:
            cosT = k.tile("cosT", [128, 1, LS], F32, es=es)
            sinT = k.tile("sinT", [128, 1, LS], F32, es=es)
            perm = k.tile("perm", [128, 1, 128], BF16, es=es)
            k.dma(k.sp, cosT.t[:, 0, :], self.din["ropeE_cos"][:, :], writes=[cosT.r[0]])
            k.dma(k.sp, sinT.t[:, 0, :], self.din["ropeE_sin"][:, :], writes=[sinT.r[0]])
            k.dma(k.sp, perm.t[:, 0, :], self.din["permE"][:, :], writes=[perm.r[0]])
            rtmp = k.tile("rtmp", [128, 2, 512], F32, es=es)
            kcT = k.tile("kcT", [128, 2, 512], BF16, es=es)
            vc = k.tile("vc", [128, 8, 128], BF16, nchunk=2, es=es)
            ctmp = k.tile("ctmp", [128, 4, 128], F32, nchunk=1, es=es)
        self.stop(1)
        es_h = ExitStack()
        h = k.tile("hA", [128, 16, Tu], BF16, es=es_h)
        self.prenorm_unit(unit, co, 1, h, es_h)
        acts = self.h_acts(h, unit)

        def evac_q(b, nch, tt, rows):
            self.copy_op(self.ev(nch + tt), qT.t[:, nch - 24, tt * 512:(tt + 1) * 512], b.t[:, 0:512], [b.r], [qT.r[nch - 24]])
        self.stop(21)
        k.linear_fm(W_in, D, 3072, 4096, acts, evac_q)
        self.stop(22)
        s, kc = k.wload(W_in, 0, D, 4096, 512)
        for tt, (aps, res, T) in enumerate(acts):
            for j in range(2):
                b = k.psum()
                k.group(k.pe, [k.mm(b.t[:, 0:T], s.t[:, c, j * 128:(j + 1) * 128], aps[c], c == 0, c == 15) for c in range(16)],
                        reads=[s.r[0]] + list(res), writes=[b.r])
                self.copy_op(self.ev(j), kT.t[:, j, tt * 512:(tt + 1) * 512], b.t[:, 0:512], [b.r], [kT.r[j]])
        self.stop(23)
        for tb in range(ntb):
            b = k.psum()
            k.group(k.pe, [k.mm(b.t[:, 0:512], h.t[:, c, tb * 128:(tb + 1) * 128], s.t[:, c, 0:512], c == 0, c == 15) for c in range(16)],
                    reads=[s.r[0]] + h.r, writes=[b.r])
            if not isS:
                self.copy_op(k.act, kv_tm.t[:, tb, :], b.t[:, 0:512], [b.r], [kv_tm.r[tb]])
            self.copy_op(k.dve, v_bf.t[:, tb, :], b.t[:, 256:512], [b.r], [v_bf.r[tb]])
        self.stop(24)
        self.end_phase(es_h)
        self.stop(2)
        if not isS:
            for bi in range(NP_):
                for hk in range(2):
                    for which, dn in ((0, "nk"), (1, "nv")):
                        dst = self.dout[dn][bi, hk].rearrange("(tb p) d -> p tb d", p=128)
                        src = kv_tm.t[:, bi * 2:bi * 2 + 2, which * 256 + hk * 128:which * 256 + (hk + 1) * 128]
                        k.dma(k.sp, dst, src, reads=[kv_tm.r[bi * 2], kv_tm.r[bi * 2 + 1]])
        else:
            self.rope_inplace(qT, range(8), Tu, perm, cosT, sinT, rtmp)
            self.rope_inplace(kT, range(2), Tu, perm, cosT, sinT, rtmp)
            for hk in range(2):
                k.dma(k.sp, ctmp.t[:, :, :], self.din["cache_k"][hk].rearrange("(kb p) d -> p kb d", p=128), writes=[ctmp.r[0]])
                b = k.psum()
                k.group(k.pe, [(lambda i=i: nc.tensor.transpose(b.t[:, i * 128:(i + 1) * 128], ctmp.t[:, i, :], self.ident.t[:, 0, :])) for i in range(4)],
                        reads=[ctmp.r[0], self.ident.r[0]], writes=[b.r])
                self.copy_op(k.act, kcT.t[:, hk, :], b.t[:, 0:512], [b.r], [kcT.r[hk]])
                k.barrier(engines=(k.pool,))
                k.dma(k.pool, vc.t[:, hk * 4:(hk + 1) * 4, :], self.din["cache_v"][hk].rearrange("(kb p) d -> p kb d", p=128), writes=[vc.r[hk]])
        self.stop(3)
        SC = 128 ** -0.5
        blocks = []
        for hh in range(8):
            hk = hh // 4
            osl = hh % 2
            for qg in range(Tu // 512):
                ob = k.psum_hold(hh * 2 + qg)
                for qi in range(4):
                    qb = qg * 4 + qi
                    qparts = [(qT.t[:, hh, qb * 128:(qb + 1) * 128], qT.r[hh])]
                    if isS:
                        lo, hi = max(0, qb - 1), min(7, qb + 1)
                        n = (hi - lo + 1) * 128
                        m0 = 128 if qb == 0 else 0
                        segs = [(n, [kT.t[:, hk, lo * 128:lo * 128 + n]], [kT.r[hk]], self.maskL.t[:, 0, m0:m0 + n]),
                                (512, [kcT.t[:, hk, :]], [kcT.r[hk]], None)]
                        vbl = [(v_bf.t[:, kb, hk * 128:(hk + 1) * 128], v_bf.r[kb]) for kb in range(lo, hi + 1)]
                        vbl += [(vc.t[:, hk * 4 + kb, :], vc.r[hk]) for kb in range(4)]
                    else:
                        bi = qb // 2
                        segs = [(256, [kT.t[:, hk, bi * 256:(bi + 1) * 256]], [kT.r[hk]], None)]
                        vbl = [(v_bf.t[:, bi * 2 + kb, hk * 128:(hk + 1) * 128], v_bf.r[bi * 2 + kb]) for kb in range(2)]
                    blk = dict(qparts=qparts, segs=segs, vbl=vbl, sink=sinkb.t[:, 0, hh:hh + 1], sink_res=sinkb.r[0], scale=SC, ob=ob, outcol=qi * 128)
                    if qi == 3:
                        def post(hh=hh, qg=qg, ob=ob, osl=osl, last=(qg == Tu // 512 - 1)):
                            self.copy_op(self.ev(qg), ostage.t[:, osl, qg * 512:(qg + 1) * 512], ob.t[:, 0:512], [ob.r], [ostage.r[osl]])
                            if last:
                                k.dma(k.sp, self.mo_ap(unit, 8 + hh, 1), ostage.t[:, osl:osl + 1, :], reads=[ostage.r[osl]], writes=[self.mo_rs[8 + hh]])
                        blk["post"] = post
                    blocks.append(blk)
        self.attn_run(blocks, wk)
        self.end_phase(es)
    def hyena_filters_gen(self, e, sfx, L, cb):
        k, nc = self.k, self.nc
        nt = L // 128
        CF, SF = self.din["CF" + sfx], self.din["SF" + sfx]
        es_f = ExitStack()
        KA = k.tile("KAf", [128, 2 * nt, 512], BF16, es=es_f)
        KB = k.tile("KBf", [128, 2 * nt, 512], BF16, es=es_f)
        wf0 = k.tile("wf0f", [128, 1, 1], F32, es=es_f)
        tmpA = k.tile("tmpAf", [128, 2, 512], F32, es=es_f)
        k.dma(k.sp, wf0.t[:, 0, :], self.din["wf0" + sfx][:, :], writes=[wf0.r[0]])
        pw = [k.tile("pwf", [128, 8, 512], BF16, nchunk=1, es=es_f) for _ in range(2)]
        w3st = k.tile("w3st", [64, 2, 512], F32, es=es_f)

        def pload(i, src, kn, nn):
            k.dma(k.sp, pw[i].t[:, 0:kn // 128, 0:nn], src.rearrange("(c p) n -> p c n", p=128), writes=[pw[i].r[0]])
            return pw[i], kn // 128

        def pload_w3(i, src):
            k.dma(k.sp, w3st.t[0:64, i, :], src, writes=[w3st.r[i]])
            self.copy_op(k.act, pw[i].t[0:64, 0, 0:512], w3st.t[0:64, i, :], [w3st.r[i]], [pw[i].r[0]])
            return pw[i], 1
        zT = k.tile("zT", [33, 1, L], F32, es=es_f)
        w1 = k.tile("w1", [33, 1, 64], F32, es=es_f)
        w2 = k.tile("w2", [64, 1, 64], F32, es=es_f)
        fv = k.tile("fv", [64, 1, 8], F32, es=es_f)
        h1T = k.tile("h1T", [64, 1, L], F32, es=es_f)
        h2T = k.tile("h2T", [64, 1, L], F32, es=es_f)
        h2b = k.tile("h2b", [64, 1, L], BF16, es=es_f)
        st_ = k.tile("sintmp", [64, 2, 512], F32, es=es_f)
        decay = k.tile("decay", [128, nt, 512], F32, nchunk=1, es=es_f)
        hs = k.tile("hs", [128, nt, 512], BF16, es=es_f)
        hd = k.tile("hd", [128, nt, 512], BF16, es=es_f)
        k.dma(k.sp, zT.t[0:33, 0, :], self.din["zT" + sfx][:, :], writes=[zT.r[0]])
        k.dma(k.sp, w1.t[0:33, 0, :], self.din["hy_f_w1"][e], writes=[w1.r[0]])
        k.dma(k.sp, w2.t[0:64, 0, :], self.din["hy_f_w2"][e], writes=[w2.r[0]])
        k.dma(k.sp, decay.t[:, :, :], self.din["decay" + sfx][:, cb * 512:(cb + 1) * 512].rearrange("(tb p) c -> p tb c", p=128), writes=[decay.r[0]])
        rows = self.rows_t
        for i, nm in enumerate(("hy_f_b1", "hy_f_b2", "hy_f_freq")):
            k.dma(k.sp, rows.t[i:i + 1, 0, 0:64], self.din[nm][e:e + 1, :], writes=[rows.r[0]])
        b = k.psum()
        k.op(k.pe, lambda: nc.tensor.transpose(b.t[0:64, 0:3], rows.t[0:3, 0, 0:64], self.ident.t[0:3, 0, 0:3]),
             reads=[rows.r[0], self.ident.r[0]], writes=[b.r])
        self.copy_op(k.dve, fv.t[0:64, 0, 0:3], b.t[0:64, 0:3], [b.r], [fv.r[0]])
        k.op(k.dve, lambda: nc.vector.tensor_scalar(out=fv.t[0:64, 0, 3:4], in0=fv.t[0:64, 0, 2:3], scalar1=1.0 / 3.0, scalar2=None, op0=ALU.mult),
             reads=[fv.r[0]], writes=[fv.r[0]])
        for i in range(2):
            k.op(k.dve, lambda i=i: nc.vector.tensor_tensor(out=fv.t[0:64, 0, 4 + i:5 + i], in0=fv.t[0:64, 0, 3:4], in1=fv.t[0:64, 0, i:i + 1], op=ALU.mult),
                 reads=[fv.r[0]], writes=[fv.r[0]])

        def sin_layer(wt, kp, src, dst, bi):
            for c0 in range(0, L, 512):
                n = min(512, L - c0)
                bk = k.psum()
                k.op(k.pe, k.mm(bk.t[0:64, 0:n], wt.t[0:kp, 0, :], src.t[0:kp, 0, c0:c0 + n], True, True), reads=[wt.r[0], src.r[0]], writes=[bk.r])
                s_ = st_.t[0:64, 0, 0:n]
                t_ = st_.t[0:64, 1, 0:n]
                k.op(k.act, lambda bk=bk, n=n, s_=s_: nc.scalar.activation(out=s_, in_=bk.t[0:64, 0:n], func=AF.Sin, scale=fv.t[0:64, 0, 3:4], bias=fv.t[0:64, 0, 4 + bi:5 + bi]),
                     reads=[bk.r, fv.r[0]], writes=[st_.r[0]])
                k.op(k.dve, lambda s_=s_, t_=t_: nc.vector.tensor_tensor(out=t_, in0=s_, in1=s_, op=ALU.mult), reads=[st_.r[0]], writes=[st_.r[1]])
                k.op(k.dve, lambda t_=t_: nc.vector.tensor_scalar(out=t_, in0=t_, scalar1=-4.0, scalar2=3.0, op0=ALU.mult, op1=ALU.add), reads=[st_.r[1]], writes=[st_.r[1]])
                k.op(k.dve, lambda s_=s_, t_=t_, c0=c0, n=n: nc.vector.tensor_tensor(out=dst.t[0:64, 0, c0:c0 + n], in0=s_, in1=t_, op=ALU.mult),
                     reads=[st_.r[0], st_.r[1]], writes=[dst.r[0]])
        sin_layer(w1, 33, zT, h1T, 0)
        yield
        sin_layer(w2, 64, h1T, h2T, 1)
        yield
        self.copy_op(k.act, h2b.t[0:64, 0, :], h2T.t[0:64, 0, :], [h2T.r[0]], [h2b.r[0]])
        w3 = self.din["hy_f_w3"][e]
        for n in range(2):
            sf_, _ = pload_w3(0, w3[:, (n * 2 + 0) * 1024 + cb * 512:(n * 2 + 0) * 1024 + cb * 512 + 512])
            sb_, _ = pload_w3(1, w3[:, (n * 2 + 1) * 1024 + cb * 512:(n * 2 + 1) * 1024 + cb * 512 + 512])
            for tb in range(nt):
                bf_ = k.psum()
                bb_ = k.psum()
                k.op(k.pe, k.mm(bf_.t[:, 0:512], h2b.t[0:64, 0, tb * 128:(tb + 1) * 128], sf_.t[0:64, 0, 0:512], True, True), reads=[h2b.r[0], sf_.r[0]], writes=[bf_.r])
                k.op(k.pe, k.mm(bb_.t[:, 0:512], h2b.t[0:64, 0, tb * 128:(tb + 1) * 128], sb_.t[0:64, 0, 0:512], True, True), reads=[h2b.r[0], sb_.r[0]], writes=[bb_.r])
                k.op(k.dve, lambda bf_=bf_, tb=tb: nc.vector.tensor_tensor(out=tmpA.t[:, 0, :], in0=bf_.t[:, 0:512], in1=decay.t[:, tb, :], op=ALU.mult),
                     reads=[bf_.r, decay.r[0]], writes=[tmpA.r[0]])
                k.op(k.dve, lambda bb_=bb_, tb=tb: nc.vector.tensor_tensor(out=tmpA.t[:, 1, :], in0=bb_.t[:, 0:512], in1=decay.t[:, tb, :], op=ALU.mult),
                     reads=[bb_.r, decay.r[0]], writes=[tmpA.r[1]])
                if tb == 0:
                    k.op(k.dve, lambda: nc.vector.memset(tmpA.t[0:1, 1, :], 0.0), writes=[tmpA.r[1]])
                k.op(k.dve, lambda tb=tb: nc.vector.tensor_tensor(out=hs.t[:, tb, :], in0=tmpA.t[:, 0, :], in1=tmpA.t[:, 1, :], op=ALU.add),
                     reads=[tmpA.r[0], tmpA.r[1]], writes=[hs.r[tb]])
                k.op(k.dve, lambda tb=tb: nc.vector.tensor_tensor(out=hd.t[:, tb, :], in0=tmpA.t[:, 0, :], in1=tmpA.t[:, 1, :], op=ALU.subtract),
                     reads=[tmpA.r[0], tmpA.r[1]], writes=[hd.r[tb]])
                yield
            for f0 in range(0, L, 512):
                fn = min(512, L - f0)
                sc_, _ = pload(0, CF[:, f0:f0 + fn], L, fn)
                ss_, _ = pload(1, SF[:, f0:f0 + fn], L, fn)
                for j in range(fn // 128):
                    fc = f0 // 128 + j
                    ba = k.psum()
                    bb = k.psum()
                    k.group(k.pe, [k.mm(ba.t[:, 0:512], sc_.t[:, tb, j * 128:(j + 1) * 128], hs.t[:, tb, :], tb == 0, tb == nt - 1) for tb in range(nt)],
                            reads=[sc_.r[0]] + hs.r, writes=[ba.r])
                    k.group(k.pe, [k.mm(bb.t[:, 0:512], ss_.t[:, tb, j * 128:(j + 1) * 128], hd.t[:, tb, :], tb == 0, tb == nt - 1) for tb in range(nt)],
                            reads=[ss_.r[0]] + hd.r, writes=[bb.r])
                    wsc = wf0.t[:, 0, :] if fc == 0 else 1.0 / L
                    k.op(k.dve, lambda ba=ba, fc=fc, wsc=wsc, n=n: nc.vector.tensor_scalar(out=KA.t[:, n * nt + fc, :], in0=ba.t[:, 0:512], scalar1=wsc, scalar2=None, op0=ALU.mult),
                         reads=[ba.r, wf0.r[0]], writes=[KA.r[n * nt + fc]])
                    k.op(k.dve, lambda bb=bb, fc=fc, wsc=wsc, n=n: nc.vector.tensor_scalar(out=KB.t[:, n * nt + fc, :], in0=bb.t[:, 0:512], scalar1=wsc, scalar2=None, op0=ALU.mult),
                         reads=[bb.r, wf0.r[0]], writes=[KB.r[n * nt + fc]])
                    if fc == 0:
                        bn = k.psum()
                        k.group(k.pe, [k.mm(bn.t[0:1, 0:512], ss_.t[:, tb, 0:1], hs.t[:, tb, :], tb == 0, tb == nt - 1) for tb in range(nt)],
                                reads=[ss_.r[0]] + hs.r, writes=[bn.r])
                        k.op(k.dve, lambda bn=bn, n=n: nc.vector.tensor_scalar(out=KB.t[0:1, n * nt, :], in0=bn.t[0:1, 0:512], scalar1=0.5 / L, scalar2=None, op0=ALU.mult),
                             reads=[bn.r], writes=[KB.r[n * nt]])
                    yield
        dA, dB, dr = self.kab_d[(sfx, cb)]
        k.dma(k.sp, dA.rearrange("c p n -> p c n"), KA.t[:, :, :], reads=KA.r, writes=[dr])
        k.dma(k.sp, dB.rearrange("c p n -> p c n"), KB.t[:, :, :], reads=KB.r, writes=[dr])
        self.end_phase(es_f)
        yield

    def bg_step(self, n=1):
        for _ in range(n):
            while self.bg:
                try:
                    next(self.bg[0])
                    break
                except StopIteration:
                    self.bg.pop(0)

    def bg_drain(self):
        while self.bg:
            self.bg_step()

    def hyena_pass(self, l, unit, cb):
        k, nc = self.k, self.nc
        e = l // 2
        Tu, col = unit["Tu"], unit["col"]
        co = self.cos[l][col]
        W_in = self.din["ev_w_in"][e]
        L = unit["seqs"][0][1]
        nt = L // 128
        sfx = "S" if L == LS else "P"
        CF, SF, SI = self.din["CF" + sfx], self.din["SF" + sfx], self.din["SI" + sfx]
        es = ExitStack()
        u = k.tile("u", [128, 12, Tu], BF16, es=es)
        KA = k.tile("KA", [128, 2 * nt, 512], BF16, es=es)
        KB = k.tile("KB", [128, 2 * nt, 512], BF16, es=es)
        cwb = k.tile("cwb", [128, 1, 96], F32, es=es)
        bias_bc = k.tile("bias_bc", [128, 2, 512], F32, es=es)
        tmpA = k.tile("tmpA", [128, 2, 512], F32, es=es)
        tmpB = k.tile("tmpB", [128, 2, 512], F32, es=es)
        self.load_rows_fm(cwb.t[:, 0, 0:72], cwb.r[0], self.din["hy_conv_w"][e].rearrange("r (c p) -> (r c) p", p=128), 72)
        self.load_rows_fm(cwb.t[:, 0, 72:96], cwb.r[0], self.din["hy_conv_b"][e:e + 1, :].rearrange("o (c p) -> (o c) p", p=128), 24)
        for n in range(2):
            self.bcast_row(bias_bc.t[:, n, :], bias_bc.r[n], self.din["hy_bias"][e, n:n + 1, cb * 512:(cb + 1) * 512], 512)
        es_h = ExitStack()
        h = k.tile("hH", [128, 16, Tu], BF16, es=es_h)
        self.prenorm_unit(unit, co, 1, h, es_h)
        acts = self.h_acts(h, unit)

        def evac_u(b, nch, tt, rows):
            a, j = nch // 8, nch % 8 - cb * 4
            self.copy_op(self.ev(nch + tt), u.t[:, a * 4 + j, tt * 512:(tt + 1) * 512], b.t[:, 0:512], [b.r], [u.r[a * 4 + j]])
        for a in range(3):
            k.linear_fm(W_in, D, a * 1024 + cb * 512, a * 1024 + cb * 512 + 512, acts, evac_u)
        self.end_phase(es_h)
        dA, dB, dr = self.kab_d[(sfx, cb)]
        k.dma(k.sp, KA.t[:, :, :], dA.rearrange("c p n -> p c n"), reads=[dr], writes=KA.r)
        k.dma(k.sp, KB.t[:, :, :], dB.rearrange("c p n -> p c n"), reads=[dr], writes=KB.r)
        v_tm = k.tile("v_tm", [128, nt, 512], BF16, es=es)
        g_tm = [k.tile(f"g{i}_tm", [128, nt, 512], BF16, es=es) for i in range(2)]
        for (off, _L) in unit["seqs"]:
            es_a = ExitStack()
            ucb = k.tile("ucb", [128, 4, L], F32, es=es_a)
            for a in range(3):
                dst_tm = v_tm if a == 0 else g_tm[a - 1]
                for i in range(4):
                    gc = a * 8 + cb * 4 + i
                    ch = a * 4 + i
                    k.op(k.dve, lambda i=i, gc=gc, ch=ch: nc.vector.tensor_scalar(
                        out=ucb.t[:, i, :], in0=u.t[:, ch, off:off + L], scalar1=cwb.t[:, 0, 24 + gc:25 + gc], scalar2=cwb.t[:, 0, 72 + gc:73 + gc],
                        op0=ALU.mult, op1=ALU.add), reads=[u.r[ch], cwb.r[0]], writes=[ucb.r[i]])
                    k.op(k.dve, lambda i=i, gc=gc, ch=ch: nc.vector.scalar_tensor_tensor(
                        out=ucb.t[:, i, 1:L], in0=u.t[:, ch, off:off + L - 1], scalar=cwb.t[:, 0, gc:gc + 1], in1=ucb.t[:, i, 1:L],
                        op0=ALU.mult, op1=ALU.add), reads=[u.r[ch], cwb.r[0], ucb.r[i]], writes=[ucb.r[i]])
                    k.op(k.dve, lambda i=i, gc=gc, ch=ch: nc.vector.scalar_tensor_tensor(
                        out=ucb.t[:, i, 0:L - 1], in0=u.t[:, ch, off + 1:off + L], scalar=cwb.t[:, 0, 48 + gc:49 + gc], in1=ucb.t[:, i, 0:L - 1],
                        op0=ALU.mult, op1=ALU.add), reads=[u.r[ch], cwb.r[0], ucb.r[i]], writes=[ucb.r[i]])
                for tb in range(nt):
                    bk = k.psum()
                    k.group(k.pe, [(lambda i=i: nc.tensor.transpose(bk.t[:, i * 128:(i + 1) * 128], ucb.t[:, i, tb * 128:(tb + 1) * 128], self.ident.t[:, 0, :])) for i in range(4)],
                            reads=ucb.r + [self.ident.r[0]], writes=[bk.r])
                    self.copy_op(self.ev(tb), dst_tm.t[:, tb, :], bk.t[:, 0:512], [bk.r], [dst_tm.r[tb]])
            self.end_phase(es_a)
            es_b = ExitStack()
            P_ = k.tile("P_", [128, nt, 512], BF16, es=es_b)
            Q_ = k.tile("Q_", [128, nt, 512], BF16, es=es_b)
            z2f = k.tile("z2f", [128, nt, 512], F32, es=es_b)
            hstage = k.tile("hstage", [128, 4, L], BF16, nchunk=1, es=es_b)
            for n in range(2):
                z = v_tm if n == 0 else g_tm[0]
                gate = g_tm[n]
                for f0 in range(0, L, 512):
                    fn = min(512, L - f0)
                    sc_, _ = k.wload_ap(CF[:, f0:f0 + fn], L, fn)
                    ss_, _ = k.wload_ap(SF[:, f0:f0 + fn], L, fn)
                    for j in range(fn // 128):
                        fc = f0 // 128 + j
                        ba = k.psum()
                        bb = k.psum()
                        k.group(k.pe, [k.mm(ba.t[:, 0:512], sc_.t[:, tb, j * 128:(j + 1) * 128], z.t[:, tb, :], tb == 0, tb == nt - 1) for tb in range(nt)],
                                reads=[sc_.r[0]] + z.r, writes=[ba.r])
                        k.group(k.pe, [k.mm(bb.t[:, 0:512], ss_.t[:, tb, j * 128:(j + 1) * 128], z.t[:, tb, :], tb == 0, tb == nt - 1) for tb in range(nt)],
                                reads=[ss_.r[0]] + z.r, writes=[bb.r])
                        ka, kb_ = KA.t[:, n * nt + fc, :], KB.t[:, n * nt + fc, :]
                        kres = [KA.r[n * nt + fc], KB.r[n * nt + fc]]
                        k.op(k.dve, lambda ba=ba, ka=ka: nc.vector.tensor_tensor(out=tmpA.t[:, 0, :], in0=ba.t[:, 0:512], in1=ka, op=ALU.mult), reads=[ba.r] + kres, writes=[tmpA.r[0]])
                        k.op(k.dve, lambda bb=bb, kb_=kb_: nc.vector.tensor_tensor(out=tmpA.t[:, 1, :], in0=bb.t[:, 0:512], in1=kb_, op=ALU.mult), reads=[bb.r] + kres, writes=[tmpA.r[1]])
                        k.op(k.dve, lambda fc=fc: nc.vector.tensor_tensor(out=P_.t[:, fc, :], in0=tmpA.t[:, 0, :], in1=tmpA.t[:, 1, :], op=ALU.subtract),
                             reads=[tmpA.r[0], tmpA.r[1]], writes=[P_.r[fc]])
                        k.op(k.dve, lambda ba=ba, kb_=kb_: nc.vector.tensor_tensor(out=tmpB.t[:, 0, :], in0=ba.t[:, 0:512], in1=kb_, op=ALU.mult), reads=[ba.r] + kres, writes=[tmpB.r[0]])
                        k.op(k.dve, lambda bb=bb, ka=ka: nc.vector.tensor_tensor(out=tmpB.t[:, 1, :], in0=bb.t[:, 0:512], in1=ka, op=ALU.mult), reads=[bb.r] + kres, writes=[tmpB.r[1]])
                        k.op(k.dve, lambda fc=fc: nc.vector.tensor_tensor(out=Q_.t[:, fc, :], in0=tmpB.t[:, 0, :], in1=tmpB.t[:, 1, :], op=ALU.add),
                             reads=[tmpB.r[0], tmpB.r[1]], writes=[Q_.r[fc]])
                        if fc == 0:
                            k.op(k.dve, lambda ba=ba, n=n: nc.vector.tensor_tensor(out=P_.t[0:1, 0, :], in0=ba.t[0:1, 0:512], in1=KA.t[0:1, n * nt, :], op=ALU.mult),
                                 reads=[ba.r] + kres, writes=[P_.r[0]])
                            k.op(k.dve, lambda bb=bb, n=n: nc.vector.tensor_tensor(out=Q_.t[0:1, 0, :], in0=bb.t[0:1, 0:512], in1=KB.t[0:1, n * nt, :], op=ALU.mult),
                                 reads=[bb.r] + kres, writes=[Q_.r[0]])
                for t0 in range(0, L, 512):
                    tn = min(512, L - t0)
                    sc_, _ = k.wload_ap(CF[:, t0:t0 + tn], L, tn)
                    si_, _ = k.wload_ap(SI[:, t0:t0 + tn], L, tn)
                    for j in range(tn // 128):
                        tb = t0 // 128 + j
                        by = k.psum()
                        fns = []
                        for fc in range(nt):
                            fns.append(k.mm(by.t[:, 0:512], sc_.t[:, fc, j * 128:(j + 1) * 128], P_.t[:, fc, :], fc == 0, False))
                            fns.append(k.mm(by.t[:, 0:512], si_.t[:, fc, j * 128:(j + 1) * 128], Q_.t[:, fc, :], False, fc == nt - 1))
                        k.group(k.pe, fns, reads=[sc_.r[0], si_.r[0]] + P_.r + Q_.r, writes=[by.r])
                        k.op(k.dve, lambda tb=tb, n=n, z=z: nc.vector.tensor_tensor(out=tmpA.t[:, 0, :], in0=z.t[:, tb, :], in1=bias_bc.t[:, n, :], op=ALU.mult),
                             reads=[z.r[tb], bias_bc.r[n]], writes=[tmpA.r[0]])
                        k.op(k.dve, lambda by=by: nc.vector.tensor_tensor(out=tmpA.t[:, 1, :], in0=by.t[:, 0:512], in1=tmpA.t[:, 0, :], op=ALU.add),
                             reads=[by.r, tmpA.r[0]], writes=[tmpA.r[1]])
                        dst = gate.t[:, tb, :] if n == 0 else z2f.t[:, tb, :]
                        dres = gate.r[tb] if n == 0 else z2f.r[tb]
                        k.op(k.dve, lambda dst=dst, gate=gate, tb=tb: nc.vector.tensor_tensor(out=dst, in0=tmpA.t[:, 1, :], in1=gate.t[:, tb, :], op=ALU.mult),
                             reads=[tmpA.r[1], gate.r[tb]], writes=[dres])
            for i in range(4):
                for t0 in range(0, nt, 4):
                    tn = min(4, nt - t0)
                    bk = k.psum()
                    k.group(k.pe, [(lambda q=q: nc.tensor.transpose(bk.t[:, q * 128:(q + 1) * 128], z2f.t[:, t0 + q, i * 128:(i + 1) * 128], self.ident.t[:, 0, :])) for q in range(tn)],
                            reads=z2f.r + [self.ident.r[0]], writes=[bk.r])
                    self.copy_op(self.ev(i), hstage.t[:, i, t0 * 128:(t0 + tn) * 128], bk.t[:, 0:tn * 128], [bk.r], [hstage.r[0]])
            t0g = unit["t0"] + off
            k.dma(k.sp, self.mo_d[cb * 4:cb * 4 + 4, :, t0g:t0g + L].rearrange("c p t -> p c t"), hstage.t[:, :, :],
                  reads=[hstage.r[0]], writes=[self.mo_rs[cb * 4 + i] for i in range(4)])
            self.end_phase(es_b)
        self.end_phase(es)

    def even_mixer(self, l, unit):
        mp = self.cfg.get("mixparts", ("att", "hy", "out"))
        if "att" in mp:
            self.even_attention(l, unit)
        if "hy" in mp:
            for cb in range(2):
                self.hyena_pass(l, unit, cb)
        if "out" in mp:
            self.out_proj_unit(unit, self.din["ev_w_out"][l // 2], self.cos[l][unit["col"]])

    def rstd_gen(self, srcs, res, T, out_t, Dn):
        k, nc = self.k, self.nc
        bk = k.psum()
        sq = self.sqbuf
        n = len(srcs)
        for c in range(n):
            i = c % 2
            k.op(k.act, lambda c=c, i=i: nc.scalar.activation(out=sq.t[:, i, 0:T], in_=srcs[c], func=AF.Square),
                 reads=[res[c]], writes=[sq.r[i]])
            k.op(k.pe, k.mm(bk.t[:, 0:T], self.ones_bf.t[:, 0, :], sq.t[:, i, 0:T], c == 0, c == n - 1),
                 reads=[sq.r[i], self.ones_bf.r[0]], writes=[bk.r])
        k.op(k.act, lambda: nc.scalar.activation(out=out_t.t[:, 0, 0:T], in_=bk.t[:, 0:T], func=AF.Sqrt,
                                                 scale=1.0 / Dn, bias=self.epsT.t[:, 0, :]),
             reads=[bk.r, self.epsT.r[0]], writes=[out_t.r[0]])
        k.op(k.dve, lambda: nc.vector.reciprocal(out=out_t.t[:, 0, 0:T], in_=out_t.t[:, 0, 0:T]),
             reads=[out_t.r[0]], writes=[out_t.r[0]])

    def mla_pass(self, l, unit):
        k, nc = self.k, self.nc
        o_ = l // 2
        Tu, col, isS = unit["Tu"], unit["col"], unit["isS"]
        co = self.cos[l][col]
        W_in = self.din["od_w_in"][o_]
        w_qb = self.din["mla_w_qb"][o_]
        w_kvb = self.din["mla_w_kvb"][o_]
        Tk = Tu + (512 if isS else 0)
        ntt = Tu // 512
        es = ExitStack()
        qn = k.tile("qn", [128, 4, Tu], BF16, es=es)
        ckvT = k.tile("ckvT", [128, 2, Tk], BF16, es=es)
        krT = k.tile("krT", [128, 1, Tk], BF16, es=es)
        gains = k.tile("gains", [128, 1, 8], F32, es=es)
        self.load_rows_fm(gains.t[:, 0, 0:4], gains.r[0], self.din["mla_q_norm"][o_:o_ + 1, :].rearrange("o (c p) -> (o c) p", p=128), 4)
        self.load_rows_fm(gains.t[:, 0, 4:6], gains.r[0], self.din["mla_kv_norm"][o_:o_ + 1, :].rearrange("o (c p) -> (o c) p", p=128), 2)
        if isS:
            cosT = k.tile("cosO", [128, 1, LS], F32, es=es)
            sinT = k.tile("sinO", [128, 1, LS], F32, es=es)
            perm = k.tile("permO", [128, 1, 128], BF16, es=es)
            k.dma(k.sp, cosT.t[:, 0, :], self.din["ropeO_cos"][:, :], writes=[cosT.r[0]])
            k.dma(k.sp, sinT.t[:, 0, :], self.din["ropeO_sin"][:, :], writes=[sinT.r[0]])
            k.dma(k.sp, perm.t[:, 0, :], self.din["permO"][:, :], writes=[perm.r[0]])
            rtmp = k.tile("rtmpO", [128, 2, 512], F32, es=es)
        es_h = ExitStack()
        h = k.tile("hM", [128, 16, Tu], BF16, es=es_h)
        lat = k.tile("lat", [128, 7, Tu], F32, es=es_h)
        self.prenorm_unit(unit, co, 1, h, es_h)
        acts = self.h_acts(h, unit)

        def evac_lat(b, nch, tt, rows):
            self.copy_op(self.ev(nch + tt), lat.t[:, nch - 40, tt * 512:(tt + 1) * 512], b.t[:, 0:512], [b.r], [lat.r[nch - 40]])
        k.linear_fm(W_in, D, 5120, 5888, acts, evac_lat)
        s = k.wslot()
        for half in range(2):
            k.dma(k.pool, s.t[:, 0:16, half * 64:(half + 1) * 64], W_in[:, 5888:5952].rearrange("(c p) n -> p c n", p=128), writes=[s.r[0]])
        for tt, (aps, res, T) in enumerate(acts):
            b = k.psum()
            k.group(k.pe, [k.mm(b.t[:, 0:T], s.t[:, c, 0:128], aps[c], c == 0, c == 15) for c in range(16)],
                    reads=[s.r[0]] + list(res), writes=[b.r])
            self.copy_op(self.ev(tt), lat.t[:, 6, tt * 512:(tt + 1) * 512], b.t[:, 0:512], [b.r], [lat.r[6]])
        ckv32 = k.tile("ckv32", [128, 2, Tu], F32, es=es_h) if not isS else None
        for tt in range(ntt):
            sl = slice(tt * 512, (tt + 1) * 512)
            self.rstd_gen([lat.t[:, c, sl] for c in range(4)], [lat.r[c] for c in range(4)], 512, self.rstd, 512.0)
            for c in range(4):
                k.op(k.dve, lambda c=c, sl=sl: nc.vector.scalar_tensor_tensor(
                    out=qn.t[:, c, sl], in0=lat.t[:, c, sl], scalar=gains.t[:, 0, c:c + 1], in1=self.rstd.t[:, 0, 0:512],
                    op0=ALU.mult, op1=ALU.mult), reads=[lat.r[c], gains.r[0], self.rstd.r[0]], writes=[qn.r[c]])
            self.rstd_gen([lat.t[:, 4 + c, sl] for c in range(2)], [lat.r[4 + c] for c in range(2)], 512, self.rstd, 256.0)
            for c in range(2):
                if isS:
                    k.op(k.dve, lambda c=c, sl=sl: nc.vector.scalar_tensor_tensor(
                        out=ckvT.t[:, c, sl], in0=lat.t[:, 4 + c, sl], scalar=gains.t[:, 0, 4 + c:5 + c], in1=self.rstd.t[:, 0, 0:512],
                        op0=ALU.mult, op1=ALU.mult), reads=[lat.r[4 + c], gains.r[0], self.rstd.r[0]], writes=[ckvT.r[c]])
                else:
                    k.op(k.dve, lambda c=c, sl=sl: nc.vector.scalar_tensor_tensor(
                        out=ckv32.t[:, c, sl], in0=lat.t[:, 4 + c, sl], scalar=gains.t[:, 0, 4 + c:5 + c], in1=self.rstd.t[:, 0, 0:512],
                        op0=ALU.mult, op1=ALU.mult), reads=[lat.r[4 + c], gains.r[0], self.rstd.r[0]], writes=[ckv32.r[c]])
                    self.copy_op(k.act, ckvT.t[:, c, sl], ckv32.t[:, c, sl], [ckv32.r[c]], [ckvT.r[c]])
        self.copy_op(k.act, krT.t[:, 0, 0:Tu], lat.t[:, 6, 0:Tu], [lat.r[6]], [krT.r[0]])
        if not isS:
            ost = k.tile("ost", [128, 2, 320], F32, es=es_h)
            for tb in range(Tu // 128):
                bi, tq = divmod(tb, 2)
                b = k.psum()
                fns = [(lambda c=c: nc.tensor.transpose(b.t[:, c * 128:(c + 1) * 128], ckv32.t[:, c, tb * 128:(tb + 1) * 128], self.ident.t[:, 0, :])) for c in range(2)]
                fns.append(lambda: nc.tensor.transpose(b.t[:, 256:320], lat.t[0:64, 6, tb * 128:(tb + 1) * 128], self.ident.t[0:64, 0, 0:64]))
                k.group(k.pe, fns, reads=[ckv32.r[0], ckv32.r[1], lat.r[6], self.ident.r[0]], writes=[b.r])
                si = tb % 2
                self.copy_op(self.ev(tb), ost.t[:, si, :], b.t[:, 0:320], [b.r], [ost.r[si]])
                k.dma(k.sp, self.dout["nckv"][bi, tq * 128:(tq + 1) * 128, :], ost.t[:, si, 0:256], reads=[ost.r[si]])
                k.dma(k.sp, self.dout["nkr"][bi, tq * 128:(tq + 1) * 128, :], ost.t[:, si, 256:320], reads=[ost.r[si]])
        else:
            ctmp = k.tile("ctmpO", [128, 4, 384], F32, nchunk=1, es=es_h)
            k.dma(k.sp, ctmp.t[:, :, 0:256], self.din["cache_ckv"].rearrange("(kb p) d -> p kb d", p=128), writes=[ctmp.r[0]])
            for half in range(2):
                k.dma(k.sp, ctmp.t[:, :, 256 + half * 64:320 + half * 64], self.din["cache_kr"].rearrange("(kb p) d -> p kb d", p=128), writes=[ctmp.r[0]])
            for c in range(3):
                b = k.psum()
                k.group(k.pe, [(lambda kb=kb: nc.tensor.transpose(b.t[:, kb * 128:(kb + 1) * 128], ctmp.t[:, kb, c * 128:(c + 1) * 128], self.ident.t[:, 0, :])) for kb in range(4)],
                        reads=[ctmp.r[0], self.ident.r[0]], writes=[b.r])
                if c < 2:
                    self.copy_op(self.ev(c), ckvT.t[:, c, Tu:Tu + 512], b.t[:, 0:512], [b.r], [ckvT.r[c]])
                else:
                    self.copy_op(k.act, krT.t[:, 0, Tu:Tu + 512], b.t[:, 0:512], [b.r], [krT.r[0]])
            self.rope_inplace_cols(krT, 0, Tu, perm, cosT, sinT, rtmp)
        self.end_phase(es_h)
        qnT = k.tile("qnT", [128, 8, Tu], BF16, es=es)
        qrT = k.tile("qrT", [128, 4, Tu], BF16, es=es)
        knT = k.tile("knT", [128, 8, Tk], BF16, es=es)
        V_tm = k.tile("V_tm", [128, Tk // 128, 1024], BF16, es=es)
        ostage = k.tile("ostageM", [128, 2, Tu], BF16, es=es)
        wk = self.attn_wks(es, 1536 if isS else 256, 12 if isS else 2)
        qacts = [([qn.t[:, c, i * 512:(i + 1) * 512] for c in range(4)], qn.r, 512) for i in range(ntt)]
        wq3 = w_qb.rearrange("k (h n) -> k h n", n=192)
        for hb in range(2):
            s = k.wslot()
            for j in range(4):
                k.dma(k.pool, s.t[:, 0:4, j * 128:(j + 1) * 128], wq3[:, hb * 4 + j, 0:128].rearrange("(c p) n -> p c n", p=128), writes=[s.r[0]])
            for tt, (aps, res, T) in enumerate(qacts):
                for j in range(4):
                    b = k.psum()
                    k.group(k.pe, [k.mm(b.t[:, 0:T], s.t[:, c, j * 128:(j + 1) * 128], aps[c], c == 0, c == 3) for c in range(4)],
                            reads=[s.r[0]] + list(res), writes=[b.r])
                    self.copy_op(self.ev(j), qnT.t[:, hb * 4 + j, tt * 512:(tt + 1) * 512], b.t[:, 0:512], [b.r], [qnT.r[hb * 4 + j]])
        s = k.wslot()
        for j in range(8):
            k.dma(k.pool, s.t[:, 0:4, j * 64:(j + 1) * 64], wq3[:, j, 128:192].rearrange("(c p) n -> p c n", p=128), writes=[s.r[0]])
        for tt, (aps, res, T) in enumerate(qacts):
            for j in range(4):
                b = k.psum()
                k.group(k.pe, [k.mm(b.t[:, 0:T], s.t[:, c, j * 128:(j + 1) * 128], aps[c], c == 0, c == 3) for c in range(4)],
                        reads=[s.r[0]] + list(res), writes=[b.r])
                self.copy_op(self.ev(j), qrT.t[:, j, tt * 512:(tt + 1) * 512], b.t[:, 0:512], [b.r], [qrT.r[j]])
        if isS:
            self.rope_inplace(qrT, range(4), Tu, perm, cosT, sinT, rtmp)
        wkv3 = w_kvb.rearrange("k (h n) -> k h n", n=256)
        ntk = (Tk + 511) // 512
        for hb in range(2):
            s = k.wslot()
            for j in range(4):
                k.dma(k.pool, s.t[:, 0:2, j * 128:(j + 1) * 128], wkv3[:, hb * 4 + j, 0:128].rearrange("(c p) n -> p c n", p=128), writes=[s.r[0]])
            for tt in range(ntk):
                sl = slice(tt * 512, (tt + 1) * 512)
                for j in range(4):
                    b = k.psum()
                    k.group(k.pe, [k.mm(b.t[:, 0:512], s.t[:, c, j * 128:(j + 1) * 128], ckvT.t[:, c, sl], c == 0, c == 1) for c in range(2)],
                            reads=[s.r[0]] + ckvT.r, writes=[b.r])
                    self.copy_op(self.ev(j), knT.t[:, hb * 4 + j, sl], b.t[:, 0:512], [b.r], [knT.r[hb * 4 + j]])
            s = k.wslot()
            for j in range(4):
                k.dma(k.pool, s.t[:, 0:2, j * 128:(j + 1) * 128], wkv3[:, hb * 4 + j, 128:256].rearrange("(c p) n -> p c n", p=128), writes=[s.r[0]])
            for tb in range(Tk // 128):
                b = k.psum()
                k.group(k.pe, [k.mm(b.t[:, 0:512], ckvT.t[:, c, tb * 128:(tb + 1) * 128], s.t[:, c, 0:512], c == 0, c == 1) for c in range(2)],
                        reads=[s.r[0]] + ckvT.r, writes=[b.r])
                self.copy_op(self.ev(tb), V_tm.t[:, tb, hb * 512:(hb + 1) * 512], b.t[:, 0:512], [b.r], [V_tm.r[tb]])
        SC = 192 ** -0.5
        blocks = []
        for hh in range(8):
            osl = hh % 2
            pb = (hh % 2) * 64
            for qg in range(ntt):
                ob = k.psum_hold(hh * 2 + qg)
                for qi in range(4):
                    qb = qg * 4 + qi
                    qparts = [(qnT.t[:, hh, qb * 128:(qb + 1) * 128], qnT.r[hh]),
                              (qrT.t[pb:pb + 64, hh // 2, qb * 128:(qb + 1) * 128], qrT.r[hh // 2])]
                    if isS:
                        kranges = [(i * 512, 512) for i in range(3)]
                    else:
                        kranges = [((qb // 2) * 256, 256)]
                    segs = [(n, [knT.t[:, hh, k0:k0 + n], krT.t[pb:pb + 64, 0, k0:k0 + n]], [knT.r[hh], krT.r[0]], None) for k0, n in kranges]
                    vbl = []
                    for k0, n in kranges:
                        vbl += [(V_tm.t[:, k0 // 128 + kb, hh * 128:(hh + 1) * 128], V_tm.r[k0 // 128 + kb]) for kb in range(n // 128)]
                    blk = dict(qparts=qparts, segs=segs, vbl=vbl, sink=None, sink_res=None, scale=SC, ob=ob, outcol=qi * 128)
                    if qi == 3:
                        def post(hh=hh, qg=qg, ob=ob, osl=osl, last=(qg == ntt - 1)):
                            self.copy_op(self.ev(qg), ostage.t[:, osl, qg * 512:(qg + 1) * 512], ob.t[:, 0:512], [ob.r], [ostage.r[osl]])
                            if last:
                                k.dma(k.sp, self.mo_ap(unit, 8 + hh, 1), ostage.t[:, osl:osl + 1, :], reads=[ostage.r[osl]], writes=[self.mo_rs[8 + hh]])
                        blk["post"] = post
                    blocks.append(blk)
        self.attn_run(blocks, wk)
        self.end_phase(es)

    def rope_inplace_cols(self, x, c, Tu, perm, cos, sin, tmp):
        self.rope_inplace(x, [c], Tu, perm, cos, sin, tmp)

    def end_phase_keep(self, es, keep=False):
        pass

    def hgrn_pass(self, l, unit, hg, nh):
        k, nc = self.k, self.nc
        o_ = l // 2
        Tu, col, isS = unit["Tu"], unit["col"], unit["isS"]
        co = self.cos[l][col]
        W_in = self.din["od_w_in"][o_]
        ntb = Tu // 128
        heads = [nh * hg + i for i in range(nh)]
        es = ExitStack()
        pj = k.tile("pj5", [128, 5 * nh, Tu], BF16, es=es)
        v_tm = k.tile("hv_tm", [128, ntb, 128 * nh], BF16, es=es)
        lbt = k.tile("lbt", [128, 1, 64], F32, es=es)
        gn = k.tile("gn", [128, 1, 1], F32, es=es)
        oacc = k.tile("oacc", [128, nh, Tu], F32, es=es)
        Sst = k.tile("Sst", [128, 2 * nh, 128], F32, es=es)
        Sbf = k.tile("Sbf", [128, 2 * nh, 128], BF16, es=es)
        ostage = k.tile("ostageH", [128, nh, Tu], BF16, es=es)
        self.load_rows_fm(lbt.t[:, 0, 0:32], lbt.r[0], self.din["hg_lb"].rearrange("l d (h p) -> (l d h) p", p=128), 32)
        k.op(k.dve, lambda: nc.vector.tensor_tensor(out=lbt.t[:, 0, 32:48], in0=lbt.t[:, 0, 16:32], in1=lbt.t[:, 0, 0:16], op=ALU.subtract),
             reads=[lbt.r[0]], writes=[lbt.r[0]])
        k.op(k.act, lambda: nc.scalar.activation(out=lbt.t[:, 0, 32:48], in_=lbt.t[:, 0, 32:48], func=AF.Sigmoid), reads=[lbt.r[0]], writes=[lbt.r[0]])
        k.op(k.dve, lambda: nc.vector.tensor_scalar(out=lbt.t[:, 0, 48:64], in0=lbt.t[:, 0, 32:48], scalar1=-1.0, scalar2=1.0, op0=ALU.mult, op1=ALU.add),
             reads=[lbt.r[0]], writes=[lbt.r[0]])
        self.load_rows_fm(gn.t[:, 0, 0:1], gn.r[0], self.din["hg_norm"][o_:o_ + 1, :], 1)
        es_h = ExitStack()
        h = k.tile("hG", [128, 16, Tu], BF16, es=es_h)
        self.prenorm_unit(unit, co, 1, h, es_h)
        acts = self.h_acts(h, unit)
        for a in range(5):
            s, kc = k.wload(W_in, 0, D, a * 1024 + hg * 128 * nh, 128 * nh)
            for tt, (aps, res, T) in enumerate(acts):
                for j in range(nh):
                    b = k.psum()
                    k.group(k.pe, [k.mm(b.t[:, 0:T], s.t[:, c, j * 128:(j + 1) * 128], aps[c], c == 0, c == 15) for c in range(16)],
                            reads=[s.r[0]] + list(res), writes=[b.r])
                    self.copy_op(self.ev(j), pj.t[:, a * nh + j, tt * 512:(tt + 1) * 512], b.t[:, 0:512], [b.r], [pj.r[a * nh + j]])
            if a == 3:
                for tb in range(ntb):
                    b = k.psum()
                    k.group(k.pe, [k.mm(b.t[:, 0:128 * nh], h.t[:, c, tb * 128:(tb + 1) * 128], s.t[:, c, 0:128 * nh], c == 0, c == 15) for c in range(16)],
                            reads=[s.r[0]] + h.r, writes=[b.r])
                    self.copy_op(self.ev(tb), v_tm.t[:, tb, :], b.t[:, 0:128 * nh], [b.r], [v_tm.r[tb]])
        self.end_phase(es_h)
        es_w = ExitStack()
        L = unit["seqs"][0][1]
        nseq = len(unit["seqs"])
        nch = Tu // 64
        w32 = k.tile("w32", [128, 5, Tu], F32, es=es_w)
        cmask = k.tile("cmask", [128, 1, Tu], F32, es=es_w)
        k.op(k.dve, lambda: nc.vector.memset(cmask.t[:, 0, :], 1.0), writes=[cmask.r[0]])
        k.op(k.dve, lambda: nc.vector.memset(cmask.t[:, 0, :].rearrange("p (c s) -> p c s", s=64)[:, :, 0:1], 0.0), writes=[cmask.r[0]])
        tri = k.tile("tri", [128, 2, 128], F32, es=es_w)
        k.dma(k.sp, tri.t[:, :, :], self.din["hg_tri"].rearrange("d s t -> s d t"), writes=tri.r)
        QI = k.tile("QI", [128, 2 * nh, Tu], BF16, es=es_w)
        KI = k.tile("KI", [128, 2 * nh, Tu], BF16, es=es_w)
        QX = k.tile("QX", [128, nh, Tu], BF16, es=es_w)
        KS = k.tile("KS", [128, 2 * nh * ntb, 128], BF16, nchunk=2 * nh, es=es_w)
        dec = k.tile("dec", [128, 2 * nh, nch], F32, es=es_w)
        ATs = k.tile("ATs", [128, 2 * nh, 128], BF16, es=es_w)
        for hi, hd_ in enumerate(heads):
            k.op(k.act, lambda hi=hi: nc.scalar.activation(out=w32.t[:, 4, :], in_=pj.t[:, 0 * nh + hi, :], func=AF.Silu), reads=[pj.r[hi]], writes=[w32.r[4]])
            for d in range(2):
                ch = hi * 2 + d
                lcol = d * 8 + hd_
                fsrc = pj.t[:, (1 + d) * nh + hi, :]
                fres = pj.r[(1 + d) * nh + hi]
                k.op(k.act, lambda fsrc=fsrc: nc.scalar.activation(out=w32.t[:, 0, :], in_=fsrc, func=AF.Sigmoid), reads=[fres], writes=[w32.r[0]])
                k.op(k.dve, lambda lcol=lcol: nc.vector.tensor_scalar(out=w32.t[:, 0, :], in0=w32.t[:, 0, :], scalar1=lbt.t[:, 0, 48 + lcol:49 + lcol],
                                                                      scalar2=lbt.t[:, 0, 32 + lcol:33 + lcol], op0=ALU.mult, op1=ALU.add),
                     reads=[w32.r[0], lbt.r[0]], writes=[w32.r[0]])
                k.op(k.act, lambda: nc.scalar.activation(out=w32.t[:, 1, :], in_=w32.t[:, 0, :], func=AF.Ln), reads=[w32.r[0]], writes=[w32.r[1]])
                k.op(k.dve, lambda: nc.vector.tensor_scalar(out=w32.t[:, 0, :], in0=w32.t[:, 0, :], scalar1=-1.0, scalar2=1.0, op0=ALU.mult, op1=ALU.add),
                     reads=[w32.r[0]], writes=[w32.r[0]])
                k.op(k.dve, lambda: nc.vector.tensor_tensor_scan(out=w32.t[:, 2, :], data0=cmask.t[:, 0, :], data1=w32.t[:, 1, :], initial=0.0,
                                                                 op0=ALU.mult, op1=ALU.add), reads=[cmask.r[0], w32.r[1]], writes=[w32.r[2]])
                b3 = w32.t[:, 2, :].rearrange("p (c s) -> p c s", s=64)
                k.op(k.act, lambda ch=ch, b3=b3: nc.scalar.activation(out=dec.t[:, ch, :], in_=b3[:, :, 63], func=AF.Exp), reads=[w32.r[2]], writes=[dec.r[ch]])
                if d == 0:
                    k.op(k.act, lambda: nc.scalar.activation(out=w32.t[:, 3, :], in_=w32.t[:, 2, :], func=AF.Exp), reads=[w32.r[2]], writes=[w32.r[3]])
                    k.op(k.dve, lambda ch=ch: nc.vector.tensor_tensor(out=QI.t[:, ch, :], in0=w32.t[:, 4, :], in1=w32.t[:, 3, :], op=ALU.mult),
                         reads=[w32.r[4], w32.r[3]], writes=[QI.r[ch]])
                    k.op(k.act, lambda: nc.scalar.activation(out=w32.t[:, 3, :], in_=w32.t[:, 2, :], func=AF.Exp, scale=-1.0), reads=[w32.r[2]], writes=[w32.r[3]])
                    k.op(k.dve, lambda ch=ch: nc.vector.tensor_tensor(out=KI.t[:, ch, :], in0=w32.t[:, 0, :], in1=w32.t[:, 3, :], op=ALU.mult),
                         reads=[w32.r[0], w32.r[3]], writes=[KI.r[ch]])
                    k.op(k.dve, lambda b3=b3: nc.vector.tensor_tensor(out=w32.t[:, 3, :].rearrange("p (c s) -> p c s", s=64),
                                                                      in0=b3[:, :, 63:64].to_broadcast([128, nch, 64]), in1=b3, op=ALU.subtract),
                         reads=[w32.r[2]], writes=[w32.r[3]])
                else:
                    k.op(k.dve, lambda: nc.vector.tensor_tensor(out=w32.t[:, 1, :], in0=w32.t[:, 2, :], in1=w32.t[:, 1, :], op=ALU.subtract),
                         reads=[w32.r[2], w32.r[1]], writes=[w32.r[1]])
                    k.op(k.act, lambda: nc.scalar.activation(out=w32.t[:, 3, :], in_=w32.t[:, 1, :], func=AF.Exp, scale=-1.0), reads=[w32.r[1]], writes=[w32.r[3]])
                    k.op(k.dve, lambda ch=ch: nc.vector.tensor_tensor(out=QI.t[:, ch, :], in0=w32.t[:, 4, :], in1=w32.t[:, 3, :], op=ALU.mult),
                         reads=[w32.r[4], w32.r[3]], writes=[QI.r[ch]])
                    k.op(k.dve, lambda b3=b3: nc.vector.tensor_tensor(out=w32.t[:, 3, :].rearrange("p (c s) -> p c s", s=64),
                                                                      in0=b3[:, :, 63:64].to_broadcast([128, nch, 64]),
                                                                      in1=w32.t[:, 1, :].rearrange("p (c s) -> p c s", s=64), op=ALU.subtract),
                         reads=[w32.r[2], w32.r[1]], writes=[w32.r[3]])
                    k.op(k.act, lambda: nc.scalar.activation(out=w32.t[:, 3, :], in_=w32.t[:, 3, :], func=AF.Exp), reads=[w32.r[3]], writes=[w32.r[3]])
                    k.op(k.dve, lambda hi=hi: nc.vector.tensor_tensor(out=QX.t[:, hi, :], in0=w32.t[:, 4, :], in1=w32.t[:, 3, :], op=ALU.mult),
                         reads=[w32.r[4], w32.r[3]], writes=[QX.r[hi]])
                    self.copy_op(k.dve, w32.t[:, 3, :], w32.t[:, 1, :], [w32.r[1]], [w32.r[3]])
                k.op(k.act, lambda: nc.scalar.activation(out=w32.t[:, 3, :], in_=w32.t[:, 3, :], func=AF.Exp), reads=[w32.r[3]], writes=[w32.r[3]])
                k.op(k.dve, lambda: nc.vector.tensor_tensor(out=w32.t[:, 3, :], in0=w32.t[:, 0, :], in1=w32.t[:, 3, :], op=ALU.mult),
                     reads=[w32.r[0], w32.r[3]], writes=[w32.r[3]])
                if d == 1:
                    self.copy_op(k.act, KI.t[:, ch, :], w32.t[:, 3, :], [w32.r[3]], [KI.r[ch]])
                for t0 in range(0, ntb, 4):
                    tn = min(4, ntb - t0)
                    b = k.psum()
                    k.group(k.pe, [(lambda q=q: nc.tensor.transpose(b.t[:, q * 128:(q + 1) * 128], w32.t[:, 3, (t0 + q) * 128:(t0 + q + 1) * 128], self.ident.t[:, 0, :])) for q in range(tn)],
                            reads=[w32.r[3], self.ident.r[0]], writes=[b.r])
                    self.copy_op(self.ev(t0 // 4), KS.t[:, ch * ntb + t0:ch * ntb + t0 + tn, :], b.t[:, 0:tn * 128].rearrange("p (q t) -> p q t", t=128), [b.r], [KS.r[ch]])
        k.op(k.dve, lambda: nc.vector.memset(oacc.t[:, :, :], 0.0), writes=oacc.r)
        for (off, _L) in unit["seqs"]:
            bi = off // L
            nblk_ = L // 128
            for hi, hd_ in enumerate(heads):
                for d in range(2):
                    ch = hi * 2 + d
                    if isS:
                        k.dma(k.sp, Sst.t[:, ch, :], self.din["state"][d, hd_], writes=[Sst.r[ch]])
                    else:
                        k.op(k.dve, lambda ch=ch: nc.vector.memset(Sst.t[:, ch, :], 0.0), writes=[Sst.r[ch]])
                    self.copy_op(k.act, Sbf.t[:, ch, :], Sst.t[:, ch, :], [Sst.r[ch]], [Sbf.r[ch]])
            for step in range(nblk_):
                for hi, hd_ in enumerate(heads):
                    for d in range(2):
                        ch = hi * 2 + d
                        blk = step if d == 0 else nblk_ - 1 - step
                        tbg = off // 128 + blk
                        c0 = tbg * 128
                        vcols = slice(hi * 128, (hi + 1) * 128)
                        ba = k.psum()
                        k.op(k.pe, k.mm(ba.t[:, 0:128], KI.t[:, ch, c0:c0 + 128], QI.t[:, ch, c0:c0 + 128], True, True),
                             reads=[KI.r[ch], QI.r[ch]], writes=[ba.r])
                        k.op(k.dve, lambda ba=ba, ch=ch, d=d: nc.vector.tensor_tensor(out=ATs.t[:, ch, :], in0=ba.t[:, 0:128], in1=tri.t[:, d, :], op=ALU.mult),
                             reads=[ba.r, tri.r[d]], writes=[ATs.r[ch]])
                        bo = k.psum()
                        k.op(k.pe, k.mm(bo.t[:, 0:128], v_tm.t[:, tbg, vcols], ATs.t[:, ch, :], True, True),
                             reads=[v_tm.r[tbg], ATs.r[ch]], writes=[bo.r])
                        k.op(k.dve, lambda bo=bo, hi=hi, c0=c0: nc.vector.tensor_tensor(out=oacc.t[:, hi, c0:c0 + 128], in0=bo.t[:, 0:128], in1=oacc.t[:, hi, c0:c0 + 128], op=ALU.add),
                             reads=[bo.r, oacc.r[hi]], writes=[oacc.r[hi]])
                        qx = QI if d == 0 else QX
                        qxi = ch if d == 0 else hi
                        for cc in ((0, 1) if d == 0 else (1, 0)):
                            t0 = c0 + cc * 64
                            gch = t0 // 64
                            bi_ = k.psum()
                            k.op(k.pe, k.mm(bi_.t[:, 0:64], Sbf.t[:, ch, :], qx.t[:, qxi, t0:t0 + 64], True, True),
                                 reads=[Sbf.r[ch], qx.r[qxi]], writes=[bi_.r])
                            k.op(k.dve, lambda bi_=bi_, hi=hi, t0=t0: nc.vector.tensor_tensor(out=oacc.t[:, hi, t0:t0 + 64], in0=bi_.t[:, 0:64], in1=oacc.t[:, hi, t0:t0 + 64], op=ALU.add),
                                 reads=[bi_.r, oacc.r[hi]], writes=[oacc.r[hi]])
                            bs = k.psum()
                            pb = cc * 64
                            k.op(k.pe, k.mm(bs.t[:, 0:128], KS.t[pb:pb + 64, ch * ntb + tbg, :], v_tm.t[pb:pb + 64, tbg, vcols], True, True),
                                 reads=[KS.r[ch], v_tm.r[tbg]], writes=[bs.r])
                            k.op(k.dve, lambda bs=bs, ch=ch, gch=gch: nc.vector.scalar_tensor_tensor(
                                out=Sst.t[:, ch, :], in0=Sst.t[:, ch, :], scalar=dec.t[:, ch, gch:gch + 1], in1=bs.t[:, 0:128], op0=ALU.mult, op1=ALU.add),
                                reads=[Sst.r[ch], dec.r[ch], bs.r], writes=[Sst.r[ch]])
                            self.copy_op(k.act, Sbf.t[:, ch, :], Sst.t[:, ch, :], [Sst.r[ch]], [Sbf.r[ch]])
            if not isS:
                for hi, hd_ in enumerate(heads):
                    for d in range(2):
                        k.dma(k.sp, self.dout["ns"][bi, d, hd_], Sst.t[:, hi * 2 + d, :], reads=[Sst.r[hi * 2 + d]])
        for hi, hd_ in enumerate(heads):
            for tt in range(Tu // 512):
                sl = slice(tt * 512, (tt + 1) * 512)
                self.rstd_gen([oacc.t[:, hi, sl]], [oacc.r[hi]], 512, self.rstd, 128.0)
                k.op(k.act, lambda hi=hi, sl=sl: nc.scalar.activation(out=self.ntmp.t[:, 0, :], in_=pj.t[:, 4 * nh + hi, sl], func=AF.Silu), reads=[pj.r[4 * nh + hi]], writes=[self.ntmp.r[0]])
                k.op(k.dve, lambda hi=hi, sl=sl: nc.vector.scalar_tensor_tensor(
                    out=self.ntmp.t[:, 1, :], in0=oacc.t[:, hi, sl], scalar=gn.t[:, 0, 0:1], in1=self.rstd.t[:, 0, 0:512], op0=ALU.mult, op1=ALU.mult),
                    reads=[oacc.r[hi], gn.r[0], self.rstd.r[0]], writes=[self.ntmp.r[1]])
                k.op(k.dve, lambda hi=hi, sl=sl: nc.vector.tensor_tensor(out=ostage.t[:, hi, sl], in0=self.ntmp.t[:, 0, :], in1=self.ntmp.t[:, 1, :], op=ALU.mult),
                     reads=[self.ntmp.r[0], self.ntmp.r[1]], writes=[ostage.r[hi]])
            k.dma(k.sp, self.mo_ap(unit, hd_, 1), ostage.t[:, hi:hi + 1, :], reads=[ostage.r[hi]], writes=[self.mo_rs[hd_]])
        self.end_phase(es_w)
        self.end_phase(es)

    def odd_mixer(self, l, unit):
        mp = self.cfg.get("mixparts", ("att", "hg", "out"))
        if "att" in mp:
            self.mla_pass(l, unit)
        if "hg" in mp:
            nh = 2 if unit["isS"] else 4
            for hg in range(8 // nh):
                self.hgrn_pass(l, unit, hg, nh)
        if "out" in mp:
            self.out_proj_unit(unit, self.din["od_w_out"][l // 2], self.cos[l][unit["col"]])

    def build(self):
        cfg = self.cfg
        k, nc = self.k, self.nc
        xs_d = self.inp("xs", [LS, D])
        xp_d = self.inp("xp", [NP_ * LP, D])
        self.inp("c2", [2, D])
        self.inp("mod_w", [2, D, 9 * D])
        self.inp("mod_b", [2, 9 * D])
        self.inp("norm_g", [12, D])
        self.inp("ffn_wg", [2, 2, D, DFF])
        self.inp("ffn_wu", [2, 2, D, DFF])
        self.inp("ffn_wd", [2, 2, DFF, D])
        self.inp("ev_w_in", [1, D, 4608])
        self.inp("ev_w_out", [1, D, D])
        self.inp("hy_conv_w", [1, 3, 3072])
        self.inp("hy_conv_b", [1, 3072])
        self.inp("hy_f_w1", [1, 33, 64])
        self.inp("hy_f_b1", [1, 64])
        self.inp("hy_f_w2", [1, 64, 64])
        self.inp("hy_f_b2", [1, 64])
        self.inp("hy_f_w3", [1, 64, 4096])
        self.inp("hy_f_freq", [1, 64])
        self.inp("hy_bias", [1, 2, 1024])
        self.inp("attn_sink", [1, 8])
        self.inp("cache_k", [2, 512, 128])
        self.inp("cache_v", [2, 512, 128])
        self.inp("od_w_in", [1, D, 5952])
        self.inp("od_w_out", [1, D, D])
        self.inp("hg_lb", [2, 2, 1024])
        self.inp("hg_norm", [1, 128])
        self.inp("mla_q_norm", [1, 512])
        self.inp("mla_w_qb", [1, 512, 1536])
        self.inp("mla_kv_norm", [1, 256])
        self.inp("mla_w_kvb", [1, 256, 2048])
        self.inp("cache_ckv", [512, 256])
        self.inp("cache_kr", [512, 64])
        self.inp("state", [2, 8, 128, 128])
        self.inp("ropeO_cos", [128, LS])
        self.inp("ropeO_sin", [128, LS])
        self.inp("permO", [128, 128], BF16)
        self.inp("hg_tri", [2, 128, 128])
        self.inp("ropeE_cos", [128, LS])
        self.inp("ropeE_sin", [128, LS])
        self.inp("permE", [128, 128], BF16)
        for sfx, L_ in (("S", LS), ("P", LP)):
            for nm in ("CF", "SF", "SI"):
                self.inp(nm + sfx, [L_, L_], BF16)
            self.inp("zT" + sfx, [33, L_])
            self.inp("decay" + sfx, [L_, 1024])
            self.inp("wf0" + sfx, [128, 1])
        ys_d = self.outp("ys", [LS, D])
        yp_d = self.outp("yp", [NP_ * LP, D])
        self.outp("nk", [NP_, 2, LP, 128])
        self.outp("nv", [NP_, 2, LP, 128])
        self.outp("nckv", [NP_, LP, 256])
        self.outp("nkr", [NP_, LP, 64])
        self.outp("ns", [NP_, 2, 8, 128, 128])
        self.xres = nc.dram_tensor("xres", [16, 128, 1536], F32).ap()
        self.xres_r = [Res() for _ in range(3)]
        if cfg.get("debug"):
            self.mo_d = self.outp("mo_d", [16, 128, 1536], BF16)
        else:
            self.mo_d = nc.dram_tensor("mo_d", [16, 128, 1536], BF16).ap()
        self.mo_rs = [Res() for _ in range(16)]
        self.h_d = nc.dram_tensor("h_d", [16, 128, 1536], BF16).ap()
        self.h_r = Res()
        self.h_key = None
        self.kab_d = {}
        for sfx, L_ in (("S", LS), ("P", LP)):
            for cb in range(2):
                nt_ = L_ // 128
                self.kab_d[(sfx, cb)] = (nc.dram_tensor(f"ka_{sfx}{cb}", [2 * nt_, 128, 512], BF16).ap(),
                                         nc.dram_tensor(f"kb_{sfx}{cb}", [2 * nt_, 128, 512], BF16).ap(), Res())
        self.bg = []
        self.units = {
            "S": dict(tiles=[0, 1], Tu=LS, t0=0, seqs=[(0, LS)], col=0, isS=True),
            "P": dict(tiles=[2], Tu=NP_ * LP, t0=LS, seqs=[(0, LP), (LP, LP)], col=1, isS=False),
        }

        self.consts()
        self.mods_all()
        self.load_x_all([(xs_d, LS), (xp_d, NP_ * LP)])
        nlayers = cfg.get("nlayers", 2)
        tiles = cfg.get("tiles", (0, 1, 2))
        parts = cfg.get("parts", ("ffn0", "mix", "ffn1"))
        try:
            self.body(cfg, nlayers, tiles, parts)
        except StopBuild:
            pass
        self.store_x_all([(ys_d, LS), (yp_d, NP_ * LP)])
        k.barrier(engines=(k.sp,))
        return nc

    def body(self, cfg, nlayers, tiles, parts):
        for l in cfg.get("layers", range(nlayers)):
            if "ffn0" in parts:
                self.ffn_phase(l, 0, 0, tiles)
            if "mix" in parts:
                for un in cfg.get("units", ("S", "P")):
                    if l % 2 == 0:
                        self.even_mixer(l, self.units[un])
                    else:
                        self.odd_mixer(l, self.units[un])
            if "ffn1" in parts:
                self.ffn_phase(l, 1, 2, tiles)


def _bf(a):
    return np.asarray(a, dtype=np.float32).astype(ml_dtypes.bfloat16)


def _rope_tables(L, rot_dim, nrep):
    half = rot_dim // 2
    inv = (10000.0 ** (-np.arange(0, half, 2, dtype=np.float32) / half)).astype(np.float32)
    nq = half // 2
    t = np.arange(L)
    row = (t // 64).astype(np.float32)
    colp = (t % 64).astype(np.float32)
    cos = np.zeros((rot_dim, L), np.float32)
    sin = np.zeros((rot_dim, L), np.float32)
    perm = np.zeros((rot_dim, rot_dim), np.float32)
    for p_ in range(rot_dim):
        first = p_ < half
        q = (p_ if first else p_ - half)
        i = q % nq
        ang = ((row if first else colp) * inv[i]).astype(np.float32)
        cos[p_] = np.cos(ang)
        sin[p_] = np.sin(ang)
        base = 0 if first else half
        if q < nq:
            perm[base + q + nq, p_] = -1.0
        else:
            perm[base + q - nq, p_] = 1.0
    cos = np.tile(cos, (nrep, 1))
    sin = np.tile(sin, (nrep, 1))
    P = np.zeros((nrep * rot_dim, nrep * rot_dim), np.float32)
    for r in range(nrep):
        P[r * rot_dim:(r + 1) * rot_dim, r * rot_dim:(r + 1) * rot_dim] = perm
    return cos, sin, P


def make_consts():
    cst = {"ident": np.eye(128, dtype=np.float32)}
    q = np.arange(128)[:, None]
    kk = np.arange(128)[None, :]
    NEG = -30000.0
    m = np.zeros((128, 384), np.float32)
    m[:, 0:128] = np.where(kk >= q, 0.0, NEG)
    m[:, 256:384] = np.where(kk <= q, 0.0, NEG)
    cst["maskL"] = m
    sI = np.arange(128)[:, None]
    tI = np.arange(128)[None, :]
    same = (sI // 64) == (tI // 64)
    cst["hg_tri"] = np.stack([(same & (tI >= sI)), (same & (tI <= sI))], 0).astype(np.float32)
    c, s_, P = _rope_tables(LS, 128, 1)
    cst["ropeE_cos"], cst["ropeE_sin"], cst["permE"] = c, s_, _bf(P)
    c, s_, P = _rope_tables(LS, 64, 2)
    cst["ropeO_cos"], cst["ropeO_sin"], cst["permO"] = c, s_, _bf(P)
    for sfx, L in (("S", LS), ("P", LP)):
        t = np.arange(L, dtype=np.float64)
        ft = np.outer(t, t) * (np.pi / L)
        CF = np.cos(ft)
        SF = np.sin(ft)
        SF[:, 0] = (-1.0) ** t
        cst["CF" + sfx] = _bf(CF)
        cst["SF" + sfx] = _bf(SF)
        cst["SI" + sfx] = _bf(SF.T.copy())
        tt = np.linspace(0.0, 1.0, L, dtype=np.float32)[:, None]
        bands = 16
        w = (2.0 * math.pi * np.arange(L, dtype=np.float32)[:, None] / L).astype(np.float32)
        fb = np.linspace(1e-4, bands - 1, bands, dtype=np.float32)[None, :]
        z = np.concatenate([tt, np.cos(fb * w), -np.sin(fb * w)], axis=-1).astype(np.float32)
        cst["zT" + sfx] = np.ascontiguousarray(z.T)
        deltas = np.abs(np.linspace(math.log(1e-2) / 1.5, math.log(1e-2) / 0.3, 1024, dtype=np.float32))
        cst["decay" + sfx] = np.exp(-tt * deltas).astype(np.float32)
        wf = np.full((128, 1), 1.0 / L, np.float32)
        wf[0, 0] = 0.5 / L
        cst["wf0" + sfx] = wf
    return cst


def core_inputs(core, inputs, consts):
    m = dict(consts)
    m["xs"] = np.ascontiguousarray(inputs["x_sample"][core])
    m["xp"] = np.ascontiguousarray(inputs["x_prompt"][2 * core:2 * core + 2].reshape(NP_ * LP, D))
    m["c2"] = np.ascontiguousarray(np.stack([inputs["c"][core], inputs["c_ctx"]], 0))
    m["norm_g"] = np.ascontiguousarray(inputs["norm_g"].reshape(12, D))
    m["cache_k"] = np.ascontiguousarray(inputs["cache_attn_k"][core, 0])
    m["cache_v"] = np.ascontiguousarray(inputs["cache_attn_v"][core, 0])
    m["cache_ckv"] = np.ascontiguousarray(inputs["cache_mla_ckv"][core, 0])
    m["cache_kr"] = np.ascontiguousarray(inputs["cache_mla_krope"][core, 0])
    m["state"] = np.ascontiguousarray(inputs["state_hgrn"][core, 0])
    for n, v in inputs.items():
        if n not in m and n not in ("x_sample", "x_prompt", "c", "c_ctx", "cache_attn_k", "cache_attn_v",
                                    "cache_mla_ckv", "cache_mla_krope", "state_hgrn"):
            m[n] = v
    return m


def run(inputs, cfg, cores=range(NCORES)):
    inputs = {k_: np.asarray(v) for k_, v in inputs.items()}
    b = Builder(cfg)
    nc = b.build()
    consts = make_consts()
    cores = list(cores)
    in_maps = [{n: m[n] for n in b.din} for m in (core_inputs(c, inputs, consts) for c in cores)]
    res = run_bass_kernel_spmd(nc, in_maps, core_ids=list(range(len(cores))))
    return b, res


def kernel(**inputs):
    b, res = run(inputs, {})
    R = res.results
    ys = np.stack([r["ys"] for r in R], 0)
    yp = np.concatenate([r["yp"].reshape(NP_, LP, D) for r in R], 0)
    nk = np.concatenate([r["nk"] for r in R], 0)[:, None]
    nv = np.concatenate([r["nv"] for r in R], 0)[:, None]
    nckv = np.concatenate([r["nckv"] for r in R], 0)[:, None]
    nkr = np.concatenate([r["nkr"] for r in R], 0)[:, None]
    ns = np.concatenate([r["ns"] for r in R], 0)[:, None]
    return (yp.astype(np.float32), ys.astype(np.float32), nk.astype(np.float32), nv.astype(np.float32),
            nckv.astype(np.float32), nkr.astype(np.float32), ns.astype(np.float32))
```

```python
import math
from contextlib import ExitStack
import numpy as np
import ml_dtypes
import concourse.bass as bass
import concourse.mybir as mybir
from concourse.bass_utils import run_bass_kernel_spmd

F32 = mybir.dt.float32
BF16 = mybir.dt.bfloat16
AF = mybir.ActivationFunctionType
ALU = mybir.AluOpType

D = 2048
DC = 16
DFF = 5632
FC = 44
LS = 1024
LP = 256
NP_ = 2
EPS = 1e-6
NCORES = 8
SKIP_SAME_ENGINE_WAITS = False


class Res:
    __slots__ = ("w", "rs", "excl")

    def __init__(self, excl=False):
        self.w = None
        self.rs = {}
        self.excl = excl


class Eng:
    def __init__(self, name, h, si):
        self.name = name
        self.h = h
        self.si = si
        self.cnt = 0
        self.seen = {}
        self.dma_sems = []
        self.dma_vals = []
        self.dma_next = 0


class PB:
    def __init__(self, t):
        self.t = t
        self.r = Res(excl=True)

    def ap(self, p=128, n=512):
        return self.t[0:p, 0:n]


class Tile:
    def __init__(self, t, nchunk):
        self.t = t
        self.r = [Res() for _ in range(nchunk)]
        self.n = nchunk

    def __getitem__(self, i):
        return self.t[:, i, :]


class K:
    def __init__(self, nc):
        self.nc = nc
        self.sems = []
        self.pe = self._eng("pe", nc.tensor)
        self.act = self._eng("act", nc.scalar)
        self.dve = self._eng("dve", nc.vector)
        self.pool = self._eng("pool", nc.gpsimd)
        self.sp = self._eng("sp", nc.sync)
        self.compute = [self.pe, self.act, self.dve]
        for q, n in ((self.sp, 24), (self.pool, 8)):
            for i in range(n):
                q.dma_sems.append(self._sem(f"{q.name}_d{i}"))
                q.dma_vals.append(0)
        self.ninst = 0
        self._names = 0
        self.skip_same = SKIP_SAME_ENGINE_WAITS
        self.banks = [PB(nc.alloc_psum_tensor(f"psb{i}", [128, 512], F32)) for i in range(8)]
        self.bank_i = 0
        self.NW = 3
        self.wslots = [Tile(nc.alloc_sbuf_tensor(f"wslot{i}", [128, 16, 512], BF16), 1) for i in range(self.NW)]
        self.w_i = 0

    def _sem(self, name):
        s = self.nc.semaphore(name).__enter__()
        self.sems.append(s)
        return len(self.sems) - 1

    def _eng(self, name, h):
        return Eng(name, h, self._sem(name))

    def name(self, p):
        self._names += 1
        return f"{p}{self._names}"

    def wait(self, eng, ev):
        si, val = ev
        if eng.seen.get(si, 0) >= val:
            return
        if si == eng.si and (eng is self.pe or self.skip_same):
            return
        eng.h.wait_ge(self.sems[si], val)
        eng.seen[si] = val
        self.ninst += 1

    def _deps(self, eng, reads, writes):
        ex = [r for r in reads if r.excl]
        if ex:
            writes = list(writes) + ex
        for r in reads:
            if r.w is not None:
                self.wait(eng, r.w)
        for w in writes:
            if w.w is not None:
                self.wait(eng, w.w)
            for si, val in w.rs.items():
                self.wait(eng, (si, val))

    def _commit(self, me, reads, writes):
        si, val = me
        ex = [r for r in reads if r.excl]
        if ex:
            writes = list(writes) + ex
        for r in reads:
            if r.rs.get(si, 0) < val:
                r.rs[si] = val
        for w in writes:
            w.w = me
            w.rs = {}

    def op(self, eng, fn, reads=(), writes=()):
        self._deps(eng, reads, writes)
        inst = fn()
        eng.cnt += 1
        inst.then_inc(self.sems[eng.si], 1)
        self.ninst += 1
        self._commit((eng.si, eng.cnt), reads, writes)
        return inst

    def group(self, eng, fns, reads=(), writes=()):
        self._deps(eng, reads, writes)
        inst = None
        for fn in fns:
            inst = fn()
            self.ninst += 1
        eng.cnt += 1
        inst.then_inc(self.sems[eng.si], 1)
        self._commit((eng.si, eng.cnt), reads, writes)

    def dma(self, q, out, in_, reads=(), writes=()):
        slot = q.dma_next
        q.dma_next = (slot + 1) % len(q.dma_sems)
        si = q.dma_sems[slot]
        prev = q.dma_vals[slot]
        if prev > 0:
            self.wait(q, (si, prev))
        self._deps(q, reads, writes)
        inst = q.h.dma_start(out=out, in_=in_)
        inst.then_inc(self.sems[si], 16)
        q.dma_vals[slot] = prev + 16
        self.ninst += 1
        self._commit((si, prev + 16), reads, writes)

    def barrier(self, engines=None):
        engines = engines or (self.pe, self.act, self.dve, self.sp)
        evs = [(e.si, e.cnt) for e in (self.pe, self.act, self.dve) if e.cnt > 0]
        for q in (self.sp, self.pool):
            for si, v in zip(q.dma_sems, q.dma_vals):
                if v > 0 and q is self.sp:
                    evs.append((si, v))
        for e in engines:
            for ev in evs:
                if ev[0] == e.si:
                    continue
                self.wait(e, ev)

    def psum(self):
        b = self.banks[self.bank_i]
        self.bank_i = (self.bank_i + 1) % 6
        return b

    def psum_hold(self, i):
        return self.banks[6 + (i % 2)]

    def wslot(self):
        s = self.wslots[self.w_i]
        self.w_i = (self.w_i + 1) % self.NW
        return s

    def tile(self, name, shape, dtype, nchunk=None, es=None):
        if es is None:
            t = self.nc.alloc_sbuf_tensor(self.name(name), list(shape), dtype)
        else:
            t = es.enter_context(self.nc.sbuf_tensor(self.name(name), list(shape), dtype))
        return Tile(t, nchunk if nchunk is not None else (shape[1] if len(shape) == 3 else 1))

    def wload(self, Wd, k0, kn, n0, nn):
        s = self.wslot()
        if kn >= 128:
            assert kn % 128 == 0
            kc = kn // 128
            src = Wd[k0:k0 + kn, n0:n0 + nn].rearrange("(c p) n -> p c n", p=128)
            self.dma(self.pool, s.t[:, 0:kc, 0:nn], src, writes=[s.r[0]])
        else:
            kc = 1
            self.dma(self.pool, s.t[0:kn, 0, 0:nn], Wd[k0:k0 + kn, n0:n0 + nn], writes=[s.r[0]])
        return s, kc

    def wload_ap(self, src, kn, nn, q=None):
        s = self.wslot()
        q = q or self.pool
        if kn >= 128:
            assert kn % 128 == 0
            kc = kn // 128
            self.dma(q, s.t[:, 0:kc, 0:nn], src.rearrange("(c p) n -> p c n", p=128), writes=[s.r[0]])
        else:
            kc = 1
            self.dma(q, s.t[0:kn, 0, 0:nn], src, writes=[s.r[0]])
        return s, kc

    def mm(self, ps_ap, lhsT, rhs, start, stop):
        nc = self.nc
        return lambda: nc.tensor.matmul(ps_ap, lhsT, rhs, start=start, stop=stop)

    def linear_fm(self, Wd, K_, n0, n1, acts, evac):
        nkb = (K_ + 2047) // 2048
        if nkb > 1:
            assert len(acts) == 1
        for nb in range(n0, n1, 512):
            nn = min(512, n1 - nb)
            ncj = (nn + 127) // 128
            if nkb == 1:
                s, kc = self.wload(Wd, 0, K_, nb, nn)
                kp = min(K_, 128)
                for tt, (aps, res, T) in enumerate(acts):
                    for j in range(ncj):
                        mw = min(128, nn - j * 128)
                        b = self.psum()
                        fns = [self.mm(b.t[0:mw, 0:T], s.t[0:kp, c, j * 128:j * 128 + mw], aps[c],
                                       c == 0, c == kc - 1) for c in range(kc)]
                        self.group(self.pe, fns, reads=[s.r[0]] + list(res), writes=[b.r])
                        evac(b, nb // 128 + j, tt, mw)
            else:
                aps, res, T = acts[0]
                bs = [self.psum() for _ in range(ncj)]
                for kb in range(nkb):
                    k0 = kb * 2048
                    kn = min(2048, K_ - k0)
                    s, kc = self.wload(Wd, k0, kn, nb, nn)
                    for j in range(ncj):
                        mw = min(128, nn - j * 128)
                        b = bs[j]
                        fns = [self.mm(b.t[0:mw, 0:T], s.t[:, c, j * 128:j * 128 + mw], aps[k0 // 128 + c],
                                       kb == 0 and c == 0, kb == nkb - 1 and c == kc - 1) for c in range(kc)]
                        self.group(self.pe, fns, reads=[s.r[0]] + list(res[k0 // 128:k0 // 128 + kc]),
                                   writes=[b.r])
                for j in range(ncj):
                    evac(bs[j], nb // 128 + j, 0, min(128, nn - j * 128))


class StopBuild(Exception):
    pass


class Builder:
    def stop(self, n):
        if self.cfg.get("stop") == n:
            raise StopBuild()

    def __init__(self, cfg):
        self.cfg = cfg
        nc = self.nc = bass.Bass("TRN2", target_bir_lowering=False)
        self.k = K(nc)
        self.din = {}
        self.dout = {}

    def inp(self, name, shape, dtype=F32):
        t = self.nc.dram_tensor(name, list(shape), dtype, kind="ExternalInput").ap()
        self.din[name] = t
        return t

    def outp(self, name, shape, dtype=F32):
        t = self.nc.dram_tensor(name, list(shape), dtype, kind="ExternalOutput").ap()
        self.dout[name] = t
        return t

    def end_phase(self, es):
        self.k.barrier()
        es.close()

    def consts(self):
        k, nc = self.k, self.nc
        ident_d = self.inp("ident", [128, 128])
        self.ident = k.tile("ident", [128, 1, 128], F32)
        k.dma(k.sp, self.ident[0], ident_d[:, :], writes=[self.ident.r[0]])
        self.ones_bf = k.tile("ones", [128, 1, 128], BF16)
        k.op(k.dve, lambda: nc.vector.memset(self.ones_bf[0], 1.0), writes=[self.ones_bf.r[0]])
        self.epsT = k.tile("eps", [128, 1, 1], F32)
        k.op(k.dve, lambda: nc.vector.memset(self.epsT[0], EPS), writes=[self.epsT.r[0]])
        self.sqbuf = k.tile("sqbuf", [128, 2, 512], BF16)
        self.ntmp = k.tile("ntmp", [128, 2, 512], F32)
        self.rstd = k.tile("rstd", [128, 1, 512], F32)
        self.rowtmp = k.tile("rowtmp", [1, 1, 512], F32)
        self.rows_t = k.tile("rows_t", [128, 1, 128], F32)
        self.ones_row = k.tile("ones_row", [1, 1, 128], F32)
        k.op(k.dve, lambda: nc.vector.memset(self.ones_row.t[0:1, 0, :], 1.0), writes=[self.ones_row.r[0]])
        self.maskL = k.tile("maskL", [128, 1, 384], F32)
        k.dma(k.sp, self.maskL.t[:, 0, :], self.inp("maskL", [128, 384])[:, :], writes=[self.maskL.r[0]])

    def mods_layer_gen(self, l, csT, btm, mtm):
        k, nc = self.k, self.nc
        mod_w = self.din["mod_w"]
        mod_b = self.din["mod_b"]
        m = self.mods[l]
        for nb in range(0, 18432, 512):
            mi = (nb // 512) % 2
            for r in range(2):
                k.dma(k.sp, btm.t[r:r + 1, mi, :], mod_b[l:l + 1, nb:nb + 512], writes=[btm.r[mi]])
            s, kc = k.wload(mod_w[l], 0, 2048, nb, 512)
            bk = k.psum()
            fns = [k.mm(bk.t[0:2, 0:512], csT.t[:, 0, 2 * c:2 * c + 2], s.t[:, c, 0:512], c == 0, c == 15)
                   for c in range(16)]
            k.group(k.pe, fns, reads=[s.r[0], csT.r[0]], writes=[bk.r])
            k.op(k.dve, lambda bk=bk, mi=mi: nc.vector.tensor_tensor(
                out=mtm.t[0:2, mi, :], in0=bk.t[0:2, 0:512], in1=btm.t[0:2, mi, :], op=ALU.add),
                reads=[bk.r, btm.r[mi]], writes=[mtm.r[mi]])
            bt = k.psum()
            fns = [(lambda j=j, mi=mi: nc.tensor.transpose(bt.t[:, 2 * j:2 * j + 2], mtm.t[0:2, mi, j * 128:(j + 1) * 128],
                                                           self.ident.t[0:2, 0, 0:2])) for j in range(4)]
            k.group(k.pe, fns, reads=[mtm.r[mi], self.ident.r[0]], writes=[bt.r])
            c0 = 2 * (nb // 128)
            k.op(k.dve, lambda bt=bt, c0=c0: nc.vector.tensor_copy(out=m.t[:, 0, c0:c0 + 8], in_=bt.t[:, 0:8]),
                 reads=[bt.r], writes=[m.r[0]])
            yield
        self.cos[l] = [self.mod_coeffs(l, col) for col in range(2)]
        yield

    def mods_all(self):
        k, nc = self.k, self.nc
        c2 = self.din["c2"]
        nlayers = self.cfg.get("nlayers", 2)
        self.mods = [k.tile(f"mods{l}", [128, 1, 288], F32) for l in range(nlayers)]
        self.gfm = k.tile("gfm", [128, 1, 12 * 16], F32)
        self.cotiles = [[k.tile(f"co{l}_{col}", [128, 9, 16], F32, nchunk=1) for col in range(2)] for l in range(nlayers)]
        csT = k.tile("csT", [128, 1, 32], BF16)
        btm = k.tile("btm", [2, 2, 512], F32)
        mtm = k.tile("mtm", [2, 2, 512], F32)
        es = ExitStack()
        ctm = k.tile("ctm", [2, 1, 2048], F32, es=es)
        gtm = k.tile("gtm", [12, 1, 2048], F32, es=es)
        k.dma(k.sp, ctm.t[0:2, 0, :], c2[:, :], writes=[ctm.r[0]])
        k.op(k.act, lambda: nc.scalar.activation(out=ctm.t[0:2, 0, :], in_=ctm.t[0:2, 0, :], func=AF.Silu),
             reads=[ctm.r[0]], writes=[ctm.r[0]])
        b = k.psum()
        fns = [(lambda c=c: nc.tensor.transpose(b.t[:, 2 * c:2 * c + 2], ctm.t[0:2, 0, c * 128:(c + 1) * 128],
                                                self.ident.t[0:2, 0, 0:2])) for c in range(16)]
        k.group(k.pe, fns, reads=[ctm.r[0], self.ident.r[0]], writes=[b.r])
        k.op(k.dve, lambda: nc.vector.tensor_copy(out=csT.t[:, 0, :], in_=b.t[:, 0:32]), reads=[b.r], writes=[csT.r[0]])
        ng = self.din["norm_g"]
        k.dma(k.sp, gtm.t[0:12, 0, :], ng[:, :], writes=[gtm.r[0]])
        b = k.psum()
        fns = [(lambda c=c: nc.tensor.transpose(b.t[:, 12 * c:12 * c + 12], gtm.t[0:12, 0, c * 128:(c + 1) * 128],
                                                self.ident.t[0:12, 0, 0:12])) for c in range(16)]
        k.group(k.pe, fns, reads=[gtm.r[0], self.ident.r[0]], writes=[b.r])
        k.op(k.dve, lambda: nc.vector.tensor_copy(
            out=self.gfm.t[:, 0, :].rearrange("p (r c) -> p r c", c=16),
            in_=b.t[:, 0:192].rearrange("p (c r) -> p r c", r=12)), reads=[b.r], writes=[self.gfm.r[0]])
        self.cos = [None] * nlayers
        full = "mix" in self.cfg.get("parts", ("ffn0", "mix", "ffn1")) and 0 in self.cfg.get("layers", range(nlayers))
        if full:
            for sfx, L_ in (("S", LS), ("P", LP)):
                for cb in range(2):
                    self.bg.append(self.hyena_filters_gen(0, sfx, L_, cb))
        for _ in self.mods_layer_gen(0, csT, btm, mtm):
            self.bg_step(3)
        late = full and nlayers > 1 and self.cfg.get("late_mods", True)
        if not late:
            for l in range(1, nlayers):
                for _ in self.mods_layer_gen(l, csT, btm, mtm):
                    self.bg_step(2)
        self.bg_drain()
        self.end_phase(es)
        if late:
            self.late_mods = [self.mods_layer_gen(l, csT, btm, mtm) for l in range(1, nlayers)]
        else:
            self.late_mods = []

    def late_step(self):
        while self.late_mods:
            try:
                next(self.late_mods[0])
                return
            except StopIteration:
                self.late_mods.pop(0)

    def late_drain(self):
        while self.late_mods:
            self.late_step()

    def mod_coeffs(self, l, col):
        k, nc = self.k, self.nc
        m = self.mods[l]
        mv = m.t[:, 0, :].rearrange("p (jc two) -> p jc two", two=2)
        co = self.cotiles[l][col]
        g = self.gfm.t[:, 0, :].rearrange("p (r c) -> p r c", c=16)
        for j in range(3):
            shift = mv[:, (3 * j) * 16:(3 * j) * 16 + 16, col]
            scale = mv[:, (3 * j + 1) * 16:(3 * j + 1) * 16 + 16, col]
            gate = mv[:, (3 * j + 2) * 16:(3 * j + 2) * 16 + 16, col]
            gpre = g[:, l * 6 + 2 * j, :]
            gpost = g[:, l * 6 + 2 * j + 1, :]
            wmac = 1.0 if j == 1 else 0.5
            k.op(k.dve, lambda j=j, scale=scale, gpre=gpre: nc.vector.scalar_tensor_tensor(
                out=co.t[:, 3 * j, :], in0=scale, scalar=1.0, in1=gpre, op0=ALU.add, op1=ALU.mult),
                reads=[m.r[0], self.gfm.r[0]], writes=[co.r[0]])
            k.op(k.dve, lambda j=j, shift=shift: nc.vector.tensor_copy(out=co.t[:, 3 * j + 1, :], in_=shift),
                 reads=[m.r[0]], writes=[co.r[0]])
            k.op(k.dve, lambda j=j, gate=gate, gpost=gpost, wmac=wmac: nc.vector.scalar_tensor_tensor(
                out=co.t[:, 3 * j + 2, :], in0=gate, scalar=wmac, in1=gpost, op0=ALU.mult, op1=ALU.mult),
                reads=[m.r[0], self.gfm.r[0]], writes=[co.r[0]])
        return co

    def xres_tile_ap(self, ti):
        return self.xres[:, :, ti * 512:(ti + 1) * 512].rearrange("c p t -> p c t")

    def load_x_all(self, srcs):
        k, nc = self.k, self.nc
        es = ExitStack()
        st = k.tile("xstage", [128, 2, D], F32, es=es)
        xt = k.tile("xt0", [128, 16, 512], F32, es=es)
        tb_g = 0
        for xd, T in srcs:
            for tb in range(T // 128):
                si = tb_g % 2
                k.dma(k.sp, st.t[:, si, :], xd[tb * 128:(tb + 1) * 128, :], writes=[st.r[si]])
                ti, to = divmod(tb_g * 128, 512)
                for c4 in range(4):
                    b = k.psum()
                    fns = [(lambda c=c, i=i: nc.tensor.transpose(b.t[:, i * 128:(i + 1) * 128], st.t[:, si, c * 128:(c + 1) * 128],
                                                                 self.ident.t[:, 0, :])) for i, c in enumerate(range(c4 * 4, c4 * 4 + 4))]
                    k.group(k.pe, fns, reads=[st.r[si], self.ident.r[0]], writes=[b.r])
                    dst = xt.t[:, c4 * 4:c4 * 4 + 4, to:to + 128]
                    src = b.t[:, 0:512].rearrange("p (i t) -> p i t", t=128)
                    if c4 % 2 == 0:
                        k.op(k.act, lambda dst=dst, src=src: nc.scalar.copy(out=dst, in_=src), reads=[b.r], writes=[xt.r[0]])
                    else:
                        k.op(k.dve, lambda dst=dst, src=src: nc.vector.tensor_copy(out=dst, in_=src), reads=[b.r], writes=[xt.r[0]])
                if to == 384:
                    k.dma(k.sp, self.xres_tile_ap(ti), xt.t[:, :, :], reads=[xt.r[0]], writes=[self.xres_r[ti]])
                tb_g += 1
        self.end_phase(es)

    def store_x_all(self, dsts):
        k, nc = self.k, self.nc
        es = ExitStack()
        st = k.tile("ystage", [128, 2, D], F32, es=es)
        xt = k.tile("xt1", [128, 16, 512], F32, es=es)
        tb_g = 0
        for yd, T in dsts:
            for tb in range(T // 128):
                si = tb_g % 2
                ti, to = divmod(tb_g * 128, 512)
                if to == 0:
                    k.dma(k.sp, xt.t[:, :, :], self.xres_tile_ap(ti), reads=[self.xres_r[ti]], writes=[xt.r[0]])
                for c4 in range(4):
                    b = k.psum()
                    fns = [(lambda c=c, i=i: nc.tensor.transpose(b.t[:, i * 128:(i + 1) * 128], xt.t[:, c, to:to + 128],
                                                                 self.ident.t[:, 0, :])) for i, c in enumerate(range(c4 * 4, c4 * 4 + 4))]
                    k.group(k.pe, fns, reads=[xt.r[0], self.ident.r[0]], writes=[b.r])
                    dst = st.t[:, si, c4 * 512:(c4 + 1) * 512]
                    if c4 % 2 == 0:
                        k.op(k.act, lambda dst=dst, b=b: nc.scalar.copy(out=dst, in_=b.t[:, 0:512]), reads=[b.r], writes=[st.r[si]])
                    else:
                        k.op(k.dve, lambda dst=dst, b=b: nc.vector.tensor_copy(out=dst, in_=b.t[:, 0:512]), reads=[b.r], writes=[st.r[si]])
                k.dma(k.sp, yd[tb * 128:(tb + 1) * 128, :], st.t[:, si, :], reads=[st.r[si]])
                tb_g += 1
        self.end_phase(es)

    def rstd_of(self, srcs, res, T, out_t):
        k, nc = self.k, self.nc
        bk = k.psum()
        sq = self.sqbuf
        for c in range(16):
            i = c % 2
            k.op(k.act, lambda c=c, i=i: nc.scalar.activation(out=sq.t[:, i, 0:T], in_=srcs[c], func=AF.Square),
                 reads=[res[c]], writes=[sq.r[i]])
            k.op(k.pe, k.mm(bk.t[:, 0:T], self.ones_bf.t[:, 0, :], sq.t[:, i, 0:T], c == 0, c == 15),
                 reads=[sq.r[i], self.ones_bf.r[0]], writes=[bk.r])
        k.op(k.act, lambda: nc.scalar.activation(out=out_t.t[:, 0, 0:T], in_=bk.t[:, 0:T], func=AF.Sqrt,
                                                 scale=1.0 / D, bias=self.epsT.t[:, 0, :]),
             reads=[bk.r, self.epsT.r[0]], writes=[out_t.r[0]])
        k.op(k.dve, lambda: nc.vector.reciprocal(out=out_t.t[:, 0, 0:T], in_=out_t.t[:, 0, 0:T]),
             reads=[out_t.r[0]], writes=[out_t.r[0]])

    def modulate(self, xt, T, co, j, h, ho=0, hcol=0):
        k, nc = self.k, self.nc
        srcs = [xt.t[:, c, 0:T] for c in range(16)]
        res = [xt.r[0]] * 16
        self.rstd_of(srcs, res, T, self.rstd)
        tmp = self.ntmp
        for c in range(16):
            i = c % 2
            k.op(k.dve, lambda c=c, i=i: nc.vector.tensor_tensor(out=tmp.t[:, i, 0:T], in0=srcs[c], in1=self.rstd.t[:, 0, 0:T], op=ALU.mult),
                 reads=[res[c], self.rstd.r[0]], writes=[tmp.r[i]])
            k.op(k.act, lambda c=c, i=i: nc.scalar.activation(out=h.t[:, ho + c, hcol:hcol + T], in_=tmp.t[:, i, 0:T], func=AF.Identity,
                                                              scale=co.t[:, 3 * j, c:c + 1], bias=co.t[:, 3 * j + 1, c:c + 1]),
                 reads=[tmp.r[i], co.r[0]], writes=[h.r[ho + c]])

    def residual(self, xt, T, co, j, out, oo=0, ocol=0):
        k, nc = self.k, self.nc
        srcs = [out.t[:, oo + c, ocol:ocol + T] for c in range(16)]
        res = [out.r[oo + c] for c in range(16)]
        self.rstd_of(srcs, res, T, self.rstd)
        tmp = self.ntmp
        for c in range(16):
            i = c % 2
            k.op(k.dve, lambda c=c, i=i: nc.vector.tensor_tensor(out=tmp.t[:, i, 0:T], in0=srcs[c], in1=self.rstd.t[:, 0, 0:T], op=ALU.mult),
                 reads=[res[c], self.rstd.r[0]], writes=[tmp.r[i]])
            k.op(k.dve, lambda c=c, i=i: nc.vector.scalar_tensor_tensor(
                out=xt.t[:, c, 0:T], in0=tmp.t[:, i, 0:T], scalar=co.t[:, 3 * j + 2, c:c + 1], in1=xt.t[:, c, 0:T],
                op0=ALU.mult, op1=ALU.add), reads=[tmp.r[i], co.r[0], xt.r[0]], writes=[xt.r[0]])

    def ffn_phase(self, l, fi, j, tiles):
        k, nc = self.k, self.nc
        es = ExitStack()
        T = 512
        wg = self.din["ffn_wg"][l, fi]
        wu = self.din["ffn_wu"][l, fi]
        wd = self.din["ffn_wd"][l, fi]
        xt = k.tile("xt", [128, 16, 512], F32, nchunk=1, es=es)
        hh = [k.tile("h", [128, 16, 512], BF16, es=es) for _ in range(2)]
        hid = k.tile("hid", [128, FC, 512], BF16, es=es)
        out = k.tile("fout", [128, 16, 512], BF16, es=es)
        sg = k.tile("sgt", [128, 4, 512], BF16, es=es)
        tiles = list(tiles)

        def co_of(ti):
            return self.cos[l][0 if ti < 2 else 1]

        def prenorm(ti, h):
            k.dma(k.sp, xt.t[:, :, :], self.xres_tile_ap(ti), reads=[self.xres_r[ti]], writes=[xt.r[0]])
            self.modulate(xt, T, co_of(ti), j, h)

        def resid(ti):
            k.dma(k.sp, xt.t[:, :, :], self.xres_tile_ap(ti), reads=[self.xres_r[ti]], writes=[xt.r[0]])
            self.residual(xt, T, co_of(ti), j, out)
            k.dma(k.sp, self.xres_tile_ap(ti), xt.t[:, :, :], reads=[xt.r[0]], writes=[self.xres_r[ti]])

        def phase_a(h, nb0, nb1):
            haps = [h.t[:, c, 0:T] for c in range(16)]
            for nb in range(nb0, nb1, 512):
                sg_, kc = k.wload(wg, 0, 2048, nb, 512)
                su_, _ = k.wload(wu, 0, 2048, nb, 512)
                for jj in range(4):
                    bg = k.psum()
                    k.group(k.pe, [k.mm(bg.t[:, 0:T], sg_.t[:, c, jj * 128:(jj + 1) * 128], haps[c], c == 0, c == 15) for c in range(16)],
                            reads=[sg_.r[0]] + h.r, writes=[bg.r])
                    k.op(k.act, lambda bg=bg, jj=jj: nc.scalar.activation(out=sg.t[:, jj, :], in_=bg.t[:, 0:T], func=AF.Silu),
                         reads=[bg.r], writes=[sg.r[jj]])
                for jj in range(4):
                    f = nb // 128 + jj
                    bu = k.psum()
                    k.group(k.pe, [k.mm(bu.t[:, 0:T], su_.t[:, c, jj * 128:(jj + 1) * 128], haps[c], c == 0, c == 15) for c in range(16)],
                            reads=[su_.r[0]] + h.r, writes=[bu.r])
                    k.op(k.dve, lambda bu=bu, jj=jj, f=f: nc.vector.tensor_tensor(out=hid.t[:, f, :], in0=sg.t[:, jj, :], in1=bu.t[:, 0:T], op=ALU.mult),
                         reads=[bu.r, sg.r[jj]], writes=[hid.r[f]])

        def evac(b, nch, tt, rows):
            k.op(k.act, lambda: nc.scalar.copy(out=out.t[:, nch, :], in_=b.t[:, 0:T]), reads=[b.r], writes=[out.r[nch]])

        def phase_b(n0, n1):
            k.linear_fm(wd, DFF, n0, n1, [([hid.t[:, f, :] for f in range(FC)], hid.r, T)], evac)

        prenorm(tiles[0], hh[0])
        for i, ti in enumerate(tiles):
            h = hh[i % 2]
            phase_a(h, 0, 1024)
            if i > 0:
                resid(tiles[i - 1])
            phase_a(h, 1024, DFF)
            phase_b(0, 1024)
            if i + 1 < len(tiles):
                prenorm(tiles[i + 1], hh[(i + 1) % 2])
            phase_b(1024, D)
        resid(tiles[-1])
        self.end_phase(es)

    def ev(self, i):
        return self.k.act if i % 2 == 0 else self.k.dve

    def copy_op(self, eng, out, in_, reads, writes):
        k, nc = self.k, self.nc
        if eng is k.act:
            k.op(eng, lambda: nc.scalar.copy(out=out, in_=in_), reads=reads, writes=writes)
        else:
            k.op(eng, lambda: nc.vector.tensor_copy(out=out, in_=in_), reads=reads, writes=writes)

    def bcast_row(self, dst_ap, dst_res, src_dram, n):
        k, nc = self.k, self.nc
        row = self.rowtmp
        k.dma(k.sp, row.t[0:1, 0, 0:n], src_dram, writes=[row.r[0]])
        b = k.psum()
        k.op(k.pe, k.mm(b.t[:, 0:n], self.ones_row.t[0:1, 0, :], row.t[0:1, 0, 0:n], True, True),
             reads=[row.r[0], self.ones_row.r[0]], writes=[b.r])
        self.copy_op(k.dve, dst_ap, b.t[:, 0:n], [b.r], [dst_res])

    def load_rows_fm(self, dst_ap, dst_res, src2d, nrows):
        k, nc = self.k, self.nc
        tmp = self.rows_t
        k.dma(k.sp, tmp.t[0:nrows, 0, :], src2d, writes=[tmp.r[0]])
        b = k.psum()
        k.op(k.pe, lambda: nc.tensor.transpose(b.t[:, 0:nrows], tmp.t[0:nrows, 0, :], self.ident.t[0:nrows, 0, 0:nrows]),
             reads=[tmp.r[0], self.ident.r[0]], writes=[b.r])
        self.copy_op(k.dve, dst_ap, b.t[:, 0:nrows], [b.r], [dst_res])

    def prenorm_unit(self, unit, co, j, h, es):
        k = self.k
        key = (id(co), unit["t0"])
        t0, Tu = unit["t0"], unit["Tu"]
        hd_ap = self.h_d[:, :, t0:t0 + Tu].rearrange("c p t -> p c t")
        if self.h_key == key:
            k.dma(k.sp, h.t[:, :, :], hd_ap, reads=[self.h_r], writes=h.r)
            return
        xt = k.tile("xtp", [128, 16, 512], F32, nchunk=1, es=es)
        for i, ti in enumerate(unit["tiles"]):
            k.dma(k.sp, xt.t[:, :, :], self.xres_tile_ap(ti), reads=[self.xres_r[ti]], writes=[xt.r[0]])
            self.modulate(xt, 512, co, j, h, ho=0, hcol=i * 512)
        k.dma(k.sp, hd_ap, h.t[:, :, :], reads=h.r, writes=[self.h_r])
        self.h_key = key

    def h_acts(self, h, unit):
        return [([h.t[:, c, i * 512:(i + 1) * 512] for c in range(16)], h.r, 512) for i in range(len(unit["tiles"]))]

    def mo_ap(self, unit, c0, nc_):
        t0 = unit["t0"]
        return self.mo_d[c0:c0 + nc_, :, t0:t0 + unit["Tu"]].rearrange("c p t -> p c t")

    def residual_unit(self, unit, co, j, o, es):
        k = self.k
        xt = k.tile("xtr", [128, 16, 512], F32, nchunk=1, es=es)
        for i, ti in enumerate(unit["tiles"]):
            k.dma(k.sp, xt.t[:, :, :], self.xres_tile_ap(ti), reads=[self.xres_r[ti]], writes=[xt.r[0]])
            self.residual(xt, 512, co, j, o, oo=0, ocol=i * 512)
            k.dma(k.sp, self.xres_tile_ap(ti), xt.t[:, :, :], reads=[xt.r[0]], writes=[self.xres_r[ti]])

    def out_proj_unit(self, unit, W_out, co):
        k, nc = self.k, self.nc
        es = ExitStack()
        Tu = unit["Tu"]
        mo = k.tile("mo", [128, 16, Tu], BF16, es=es)
        o = k.tile("o", [128, 16, Tu], BF16, es=es)
        k.dma(k.sp, mo.t[:, :, :], self.mo_ap(unit, 0, 16), reads=self.mo_rs, writes=mo.r)
        acts = [([mo.t[:, c, i * 512:(i + 1) * 512] for c in range(16)], mo.r, 512) for i in range(Tu // 512)]

        def evac(b, nch, tt, rows):
            self.copy_op(self.ev(nch + tt), o.t[:, nch, tt * 512:(tt + 1) * 512], b.t[:, 0:512], [b.r], [o.r[nch]])
        k.linear_fm(W_out, D, 0, D, acts, evac)
        self.residual_unit(unit, co, 1, o, es)
        self.end_phase(es)

    def rope_inplace(self, x, chunks, Tu, perm, cos, sin, tmp):
        k, nc = self.k, self.nc
        for c in chunks:
            for tt in range(Tu // 512):
                sl = slice(tt * 512, (tt + 1) * 512)
                b = k.psum()
                k.op(k.pe, k.mm(b.t[:, 0:512], perm.t[:, 0, :], x.t[:, c, sl], True, True), reads=[perm.r[0], x.r[c]], writes=[b.r])
                k.op(k.dve, lambda c=c, sl=sl: nc.vector.tensor_tensor(out=tmp.t[:, 0, :], in0=x.t[:, c, sl], in1=cos.t[:, 0, sl], op=ALU.mult),
                     reads=[x.r[c], cos.r[0]], writes=[tmp.r[0]])
                k.op(k.dve, lambda b=b, sl=sl: nc.vector.tensor_tensor(out=tmp.t[:, 1, :], in0=b.t[:, 0:512], in1=sin.t[:, 0, sl], op=ALU.mult),
                     reads=[b.r, sin.r[0]], writes=[tmp.r[1]])
                k.op(k.dve, lambda c=c, sl=sl: nc.vector.tensor_tensor(out=x.t[:, c, sl], in0=tmp.t[:, 0, :], in1=tmp.t[:, 1, :], op=ALU.add),
                     reads=[tmp.r[0], tmp.r[1]], writes=[x.r[c]])

    def attn_stage_a(self, blk, wk):
        k, nc = self.k, self.nc
        qparts, segs, sink_ap, sink_res, scale = blk["qparts"], blk["segs"], blk["sink"], blk["sink_res"], blk["scale"]
        nseg = len(segs)
        p, st = wk["p"], wk["st"]
        srcs = []
        off = 0
        qres = [r for _, r in qparts]
        for i, (n, kaps, kres, mask) in enumerate(segs):
            b = k.psum()
            np_ = len(qparts)
            k.group(k.pe, [k.mm(b.t[:, 0:n], qparts[a][0], kaps[a], a == 0, a == np_ - 1) for a in range(np_)],
                    reads=qres + list(kres), writes=[b.r])
            if mask is not None:
                dst = p.t[:, 0, off:off + n]
                k.op(k.dve, lambda b=b, n=n, dst=dst, mask=mask: nc.vector.tensor_tensor(out=dst, in0=b.t[:, 0:n], in1=mask, op=ALU.add),
                     reads=[b.r, self.maskL.r[0]], writes=[p.r[0]])
                srcs.append((dst, p.r[0], off, n))
            else:
                srcs.append((b.t[:, 0:n], b.r, off, n))
            off += n
        blk["nkeys"] = off
        for i, (src, sres, o_, n) in enumerate(srcs):
            k.op(k.dve, lambda src=src, i=i: nc.vector.reduce_max(out=st.t[:, 0, i:i + 1], in_=src, axis=mybir.AxisListType.X),
                 reads=[sres], writes=[st.r[0]])
        if nseg > 1:
            k.op(k.dve, lambda: nc.vector.reduce_max(out=st.t[:, 0, 4:5], in_=st.t[:, 0, 0:nseg], axis=mybir.AxisListType.X),
                 reads=[st.r[0]], writes=[st.r[0]])
            rm = st.t[:, 0, 4:5]
        else:
            rm = st.t[:, 0, 0:1]
        negm = st.t[:, 0, 5:6]
        if sink_ap is not None:
            k.op(k.dve, lambda: nc.vector.tensor_scalar(out=negm, in0=rm, scalar1=scale, scalar2=sink_ap, op0=ALU.mult, op1=ALU.max),
                 reads=[st.r[0], sink_res], writes=[st.r[0]])
            k.op(k.dve, lambda: nc.vector.tensor_scalar(out=negm, in0=negm, scalar1=-1.0, scalar2=None, op0=ALU.mult),
                 reads=[st.r[0]], writes=[st.r[0]])
        else:
            k.op(k.dve, lambda: nc.vector.tensor_scalar(out=negm, in0=rm, scalar1=-scale, scalar2=None, op0=ALU.mult),
                 reads=[st.r[0]], writes=[st.r[0]])
        sm = wk["sm"]
        for i, (src, sres, o_, n) in enumerate(srcs):
            k.op(k.act, lambda src=src, o_=o_, n=n, i=i: nc.scalar.activation(
                out=p.t[:, 0, o_:o_ + n], in_=src, func=AF.Exp, scale=scale, bias=negm, accum_out=sm.t[:, 0, i:i + 1]),
                reads=[sres, st.r[0]], writes=[p.r[0], sm.r[0]])
        ns = nseg
        if sink_ap is not None:
            k.op(k.act, lambda: nc.scalar.activation(out=sm.t[:, 0, nseg:nseg + 1], in_=negm, func=AF.Exp, scale=1.0, bias=sink_ap),
                 reads=[st.r[0], sink_res], writes=[sm.r[0]])
            ns += 1
        blk["ns"] = ns

    def attn_stage_b(self, blk, wk):
        k, nc = self.k, self.nc
        p, pT, sm = wk["p"], wk["pT"], wk["sm"]
        nkeys, ns, vblocks, outbank, outcol = blk["nkeys"], blk["ns"], blk["vbl"], blk["ob"], blk["outcol"]
        den = sm.t[:, 0, 6:7]
        k.op(k.dve, lambda: nc.vector.reduce_sum(out=den, in_=sm.t[:, 0, 0:ns], axis=mybir.AxisListType.X),
             reads=[sm.r[0]], writes=[sm.r[0]])
        k.op(k.dve, lambda: nc.vector.reciprocal(out=den, in_=den), reads=[sm.r[0]], writes=[sm.r[0]])
        k.op(k.dve, lambda: nc.vector.tensor_scalar(out=p.t[:, 0, 0:nkeys], in0=p.t[:, 0, 0:nkeys], scalar1=den, scalar2=None, op0=ALU.mult),
             reads=[p.r[0], sm.r[0]], writes=[p.r[0]])
        nkb = nkeys // 128
        for g0 in range(0, nkb, 4):
            gn = min(4, nkb - g0)
            b = k.psum()
            k.group(k.pe, [(lambda i=i: nc.tensor.transpose(b.t[:, i * 128:(i + 1) * 128], p.t[:, 0, (g0 + i) * 128:(g0 + i + 1) * 128], self.ident.t[:, 0, :]))
                           for i in range(gn)], reads=[p.r[0], self.ident.r[0]], writes=[b.r])
            self.copy_op(self.ev(g0 // 4), pT.t[:, g0:g0 + gn, :], b.t[:, 0:gn * 128].rearrange("p (i t) -> p i t", t=128), [b.r], [pT.r[0]])
        k.group(k.pe, [k.mm(outbank.t[:, outcol:outcol + 128], vblocks[kb][0], pT.t[:, kb, :], kb == 0, kb == nkb - 1) for kb in range(nkb)],
                reads=[pT.r[0]] + [r for _, r in vblocks], writes=[outbank.r])
        if blk.get("post"):
            blk["post"]()

    def attn_run(self, blocks, wks):
        n = len(blocks)
        nw = len(wks)
        if n == 0:
            return
        self.attn_stage_a(blocks[0], wks[0])
        for i in range(n):
            if i + 1 < n:
                self.attn_stage_a(blocks[i + 1], wks[(i + 1) % nw])
            self.attn_stage_b(blocks[i], wks[i % nw])
            if self.attn_hook is not None and i % 2 == 1:
                self.attn_hook()

    def attn_wks(self, es, pw, npt, n=3):
        k = self.k
        return [{"p": k.tile("p", [128, 1, pw], F32, es=es), "pT": k.tile("pT", [128, npt, 128], BF16, nchunk=1, es=es),
                 "st": k.tile("st", [128, 1, 8], F32, es=es), "sm": k.tile("sm", [128, 1, 8], F32, es=es)} for _ in range(n)]

    def even_attention(self, l, unit):
        k, nc = self.k, self.nc
        e = l // 2
        Tu, col, isS = unit["Tu"], unit["col"], unit["isS"]
        co = self.cos[l][col]
        W_in = self.din["ev_w_in"][e]
        ntb = Tu // 128
        es = ExitStack()
        qT = k.tile("qT", [128, 8, Tu], BF16, es=es)
        kT = k.tile("kT", [128, 2, Tu], BF16, es=es)
        kv_tm = k.tile("kv_tm", [128, 1 if isS else ntb, 512], F32, es=es)
        v_bf = k.tile("v_bf", [128, ntb, 256], BF16, es=es)
        sinkb = k.tile("sinkb", [128, 1, 8], F32, es=es)
        ostage = k.tile("ostage", [128, 2, Tu], BF16, es=es)
        wk = self.attn_wks(es, 896 if isS else 256, 7 if isS else 2)
        self.bcast_row(sinkb.t[:, 0, :], sinkb.r[0], self.din["attn_sink"][e:e + 1, :], 8)
        if isS:
            cosT = k.tile("cosT", [128, 1, LS], F32, es=es)
            sinT = k.tile("sinT", [128, 1, LS], F32, es=es)
            perm = k.tile("perm", [128, 1, 128], BF16, es=es)
            k.dma(k.sp, cosT.t[:, 0, :], self.din["ropeE_cos"][:, :], writes=[cosT.r[0]])
            k.dma(k.sp, sinT.t[:, 0, :], self.din["ropeE_sin"][:, :], writes=[sinT.r[0]])
            k.dma(k.sp, perm.t[:, 0, :], self.din["permE"][:, :], writes=[perm.r[0]])
            rtmp = k.tile("rtmp", [128, 2, 512], F32, es=es)
            kcT = k.tile("kcT", [128, 2, 512], BF16, es=es)
            vc = k.tile("vc", [128, 8, 128], BF16, nchunk=2, es=es)
            ctmp = k.tile("ctmp", [128, 4, 128], F32, nchunk=1, es=es)
        self.stop(1)
        es_h = ExitStack()
        h = k.tile("hA", [128, 16, Tu], BF16, es=es_h)
        self.prenorm_unit(unit, co, 1, h, es_h)
        acts = self.h_acts(h, unit)

        def evac_q(b, nch, tt, rows):
            self.copy_op(self.ev(nch + tt), qT.t[:, nch - 24, tt * 512:(tt + 1) * 512], b.t[:, 0:512], [b.r], [qT.r[nch - 24]])
        self.stop(21)
        k.linear_fm(W_in, D, 3072, 4096, acts, evac_q)
        self.stop(22)
        s, kc = k.wload(W_in, 0, D, 4096, 512)
        for tt, (aps, res, T) in enumerate(acts):
            for j in range(2):
                b = k.psum()
                k.group(k.pe, [k.mm(b.t[:, 0:T], s.t[:, c, j * 128:(j + 1) * 128], aps[c], c == 0, c == 15) for c in range(16)],
                        reads=[s.r[0]] + list(res), writes=[b.r])
                self.copy_op(self.ev(j), kT.t[:, j, tt * 512:(tt + 1) * 512], b.t[:, 0:512], [b.r], [kT.r[j]])
        self.stop(23)
        for tb in range(ntb):
            b = k.psum()
            k.group(k.pe, [k.mm(b.t[:, 0:512], h.t[:, c, tb * 128:(tb + 1) * 128], s.t[:, c, 0:512], c == 0, c == 15) for c in range(16)],
                    reads=[s.r[0]] + h.r, writes=[b.r])
            if not isS:
                self.copy_op(k.act, kv_tm.t[:, tb, :], b.t[:, 0:512], [b.r], [kv_tm.r[tb]])
            self.copy_op(k.dve, v_bf.t[:, tb, :], b.t[:, 256:512], [b.r], [v_bf.r[tb]])
        self.stop(24)
        self.end_phase(es_h)
        self.stop(2)
        if not isS:
            for bi in range(NP_):
                for hk in range(2):
                    for which, dn in ((0, "nk"), (1, "nv")):
                        dst = self.dout[dn][bi, hk].rearrange("(tb p) d -> p tb d", p=128)
                        src = kv_tm.t[:, bi * 2:bi * 2 + 2, which * 256 + hk * 128:which * 256 + (hk + 1) * 128]
                        k.dma(k.sp, dst, src, reads=[kv_tm.r[bi * 2], kv_tm.r[bi * 2 + 1]])
        else:
            self.rope_inplace(qT, range(8), Tu, perm, cosT, sinT, rtmp)
            self.rope_inplace(kT, range(2), Tu, perm, cosT, sinT, rtmp)
            for hk in range(2):
                k.dma(k.sp, ctmp.t[:, :, :], self.din["cache_k"][hk].rearrange("(kb p) d -> p kb d", p=128), writes=[ctmp.r[0]])
                b = k.psum()
                k.group(k.pe, [(lambda i=i: nc.tensor.transpose(b.t[:, i * 128:(i + 1) * 128], ctmp.t[:, i, :], self.ident.t[:, 0, :])) for i in range(4)],
                        reads=[ctmp.r[0], self.ident.r[0]], writes=[b.r])
                self.copy_op(k.act, kcT.t[:, hk, :], b.t[:, 0:512], [b.r], [kcT.r[hk]])
                k.barrier(engines=(k.pool,))
                k.dma(k.pool, vc.t[:, hk * 4:(hk + 1) * 4, :], self.din["cache_v"][hk].rearrange("(kb p) d -> p kb d", p=128), writes=[vc.r[hk]])
        self.stop(3)
        SC = 128 ** -0.5
        blocks = []
        for hh in range(8):
            hk = hh // 4
            osl = hh % 2
            for qg in range(Tu // 512):
                ob = k.psum_hold(hh * 2 + qg)
                for qi in range(4):
                    qb = qg * 4 + qi
                    qparts = [(qT.t[:, hh, qb * 128:(qb + 1) * 128], qT.r[hh])]
                    if isS:
                        lo, hi = max(0, qb - 1), min(7, qb + 1)
                        n = (hi - lo + 1) * 128
                        m0 = 128 if qb == 0 else 0
                        segs = [(n, [kT.t[:, hk, lo * 128:lo * 128 + n]], [kT.r[hk]], self.maskL.t[:, 0, m0:m0 + n]),
                                (512, [kcT.t[:, hk, :]], [kcT.r[hk]], None)]
                        vbl = [(v_bf.t[:, kb, hk * 128:(hk + 1) * 128], v_bf.r[kb]) for kb in range(lo, hi + 1)]
                        vbl += [(vc.t[:, hk * 4 + kb, :], vc.r[hk]) for kb in range(4)]
                    else:
                        bi = qb // 2
                        segs = [(256, [kT.t[:, hk, bi * 256:(bi + 1) * 256]], [kT.r[hk]], None)]
                        vbl = [(v_bf.t[:, bi * 2 + kb, hk * 128:(hk + 1) * 128], v_bf.r[bi * 2 + kb]) for kb in range(2)]
                    blk = dict(qparts=qparts, segs=segs, vbl=vbl, sink=sinkb.t[:, 0, hh:hh + 1], sink_res=sinkb.r[0], scale=SC, ob=ob, outcol=qi * 128)
                    if qi == 3:
                        def post(hh=hh, qg=qg, ob=ob, osl=osl, last=(qg == Tu // 512 - 1)):
                            self.copy_op(self.ev(qg), ostage.t[:, osl, qg * 512:(qg + 1) * 512], ob.t[:, 0:512], [ob.r], [ostage.r[osl]])
                            if last:
                                k.dma(k.sp, self.mo_ap(unit, 8 + hh, 1), ostage.t[:, osl:osl + 1, :], reads=[ostage.r[osl]], writes=[self.mo_rs[8 + hh]])
                        blk["post"] = post
                    blocks.append(blk)
        self.attn_hook = self.late_step
        self.attn_run(blocks, wk)
        self.attn_hook = None
        self.end_phase(es)
    def hyena_filters_gen(self, e, sfx, L, cb):
        k, nc = self.k, self.nc
        nt = L // 128
        CF, SF = self.din["CF" + sfx], self.din["SF" + sfx]
        es_f = ExitStack()
        KA = k.tile("KAf", [128, 2 * nt, 512], BF16, es=es_f)
        KB = k.tile("KBf", [128, 2 * nt, 512], BF16, es=es_f)
        wf0 = k.tile("wf0f", [128, 1, 1], F32, es=es_f)
        tmpA = k.tile("tmpAf", [128, 2, 512], F32, es=es_f)
        k.dma(k.sp, wf0.t[:, 0, :], self.din["wf0" + sfx][:, :], writes=[wf0.r[0]])
        pw = [k.tile("pwf", [128, 8, 512], BF16, nchunk=1, es=es_f) for _ in range(2)]
        w3st = k.tile("w3st", [64, 2, 512], F32, es=es_f)

        def pload(i, src, kn, nn):
            k.dma(k.sp, pw[i].t[:, 0:kn // 128, 0:nn], src.rearrange("(c p) n -> p c n", p=128), writes=[pw[i].r[0]])
            return pw[i], kn // 128

        def pload_w3(i, src):
            k.dma(k.sp, w3st.t[0:64, i, :], src, writes=[w3st.r[i]])
            self.copy_op(k.act, pw[i].t[0:64, 0, 0:512], w3st.t[0:64, i, :], [w3st.r[i]], [pw[i].r[0]])
            return pw[i], 1
        zT = k.tile("zT", [33, 1, L], F32, es=es_f)
        w1 = k.tile("w1", [33, 1, 64], F32, es=es_f)
        w2 = k.tile("w2", [64, 1, 64], F32, es=es_f)
        fv = k.tile("fv", [64, 1, 8], F32, es=es_f)
        h1T = k.tile("h1T", [64, 1, L], F32, es=es_f)
        h2T = k.tile("h2T", [64, 1, L], F32, es=es_f)
        h2b = k.tile("h2b", [64, 1, L], BF16, es=es_f)
        st_ = k.tile("sintmp", [64, 2, 512], F32, es=es_f)
        decay = k.tile("decay", [128, nt, 512], F32, nchunk=1, es=es_f)
        hs = k.tile("hs", [128, nt, 512], BF16, es=es_f)
        hd = k.tile("hd", [128, nt, 512], BF16, es=es_f)
        k.dma(k.sp, zT.t[0:33, 0, :], self.din["zT" + sfx][:, :], writes=[zT.r[0]])
        k.dma(k.sp, w1.t[0:33, 0, :], self.din["hy_f_w1"][e], writes=[w1.r[0]])
        k.dma(k.sp, w2.t[0:64, 0, :], self.din["hy_f_w2"][e], writes=[w2.r[0]])
        k.dma(k.sp, decay.t[:, :, :], self.din["decay" + sfx][:, cb * 512:(cb + 1) * 512].rearrange("(tb p) c -> p tb c", p=128), writes=[decay.r[0]])
        rows = self.rows_t
        for i, nm in enumerate(("hy_f_b1", "hy_f_b2", "hy_f_freq")):
            k.dma(k.sp, rows.t[i:i + 1, 0, 0:64], self.din[nm][e:e + 1, :], writes=[rows.r[0]])
        b = k.psum()
        k.op(k.pe, lambda: nc.tensor.transpose(b.t[0:64, 0:3], rows.t[0:3, 0, 0:64], self.ident.t[0:3, 0, 0:3]),
             reads=[rows.r[0], self.ident.r[0]], writes=[b.r])
        self.copy_op(k.dve, fv.t[0:64, 0, 0:3], b.t[0:64, 0:3], [b.r], [fv.r[0]])
        k.op(k.dve, lambda: nc.vector.tensor_scalar(out=fv.t[0:64, 0, 3:4], in0=fv.t[0:64, 0, 2:3], scalar1=1.0 / 3.0, scalar2=None, op0=ALU.mult),
             reads=[fv.r[0]], writes=[fv.r[0]])
        for i in range(2):
            k.op(k.dve, lambda i=i: nc.vector.tensor_tensor(out=fv.t[0:64, 0, 4 + i:5 + i], in0=fv.t[0:64, 0, 3:4], in1=fv.t[0:64, 0, i:i + 1], op=ALU.mult),
                 reads=[fv.r[0]], writes=[fv.r[0]])

        def sin_layer(wt, kp, src, dst, bi):
            for c0 in range(0, L, 512):
                n = min(512, L - c0)
                bk = k.psum()
                k.op(k.pe, k.mm(bk.t[0:64, 0:n], wt.t[0:kp, 0, :], src.t[0:kp, 0, c0:c0 + n], True, True), reads=[wt.r[0], src.r[0]], writes=[bk.r])
                s_ = st_.t[0:64, 0, 0:n]
                t_ = st_.t[0:64, 1, 0:n]
                k.op(k.act, lambda bk=bk, n=n, s_=s_: nc.scalar.activation(out=s_, in_=bk.t[0:64, 0:n], func=AF.Sin, scale=fv.t[0:64, 0, 3:4], bias=fv.t[0:64, 0, 4 + bi:5 + bi]),
                     reads=[bk.r, fv.r[0]], writes=[st_.r[0]])
                k.op(k.dve, lambda s_=s_, t_=t_: nc.vector.tensor_tensor(out=t_, in0=s_, in1=s_, op=ALU.mult), reads=[st_.r[0]], writes=[st_.r[1]])
                k.op(k.dve, lambda t_=t_: nc.vector.tensor_scalar(out=t_, in0=t_, scalar1=-4.0, scalar2=3.0, op0=ALU.mult, op1=ALU.add), reads=[st_.r[1]], writes=[st_.r[1]])
                k.op(k.dve, lambda s_=s_, t_=t_, c0=c0, n=n: nc.vector.tensor_tensor(out=dst.t[0:64, 0, c0:c0 + n], in0=s_, in1=t_, op=ALU.mult),
                     reads=[st_.r[0], st_.r[1]], writes=[dst.r[0]])
        sin_layer(w1, 33, zT, h1T, 0)
        yield
        sin_layer(w2, 64, h1T, h2T, 1)
        yield
        self.copy_op(k.act, h2b.t[0:64, 0, :], h2T.t[0:64, 0, :], [h2T.r[0]], [h2b.r[0]])
        w3 = self.din["hy_f_w3"][e]
        for n in range(2):
            sf_, _ = pload_w3(0, w3[:, (n * 2 + 0) * 1024 + cb * 512:(n * 2 + 0) * 1024 + cb * 512 + 512])
            sb_, _ = pload_w3(1, w3[:, (n * 2 + 1) * 1024 + cb * 512:(n * 2 + 1) * 1024 + cb * 512 + 512])
            for tb in range(nt):
                bf_ = k.psum()
                bb_ = k.psum()
                k.op(k.pe, k.mm(bf_.t[:, 0:512], h2b.t[0:64, 0, tb * 128:(tb + 1) * 128], sf_.t[0:64, 0, 0:512], True, True), reads=[h2b.r[0], sf_.r[0]], writes=[bf_.r])
                k.op(k.pe, k.mm(bb_.t[:, 0:512], h2b.t[0:64, 0, tb * 128:(tb + 1) * 128], sb_.t[0:64, 0, 0:512], True, True), reads=[h2b.r[0], sb_.r[0]], writes=[bb_.r])
                k.op(k.dve, lambda bf_=bf_, tb=tb: nc.vector.tensor_tensor(out=tmpA.t[:, 0, :], in0=bf_.t[:, 0:512], in1=decay.t[:, tb, :], op=ALU.mult),
                     reads=[bf_.r, decay.r[0]], writes=[tmpA.r[0]])
                k.op(k.dve, lambda bb_=bb_, tb=tb: nc.vector.tensor_tensor(out=tmpA.t[:, 1, :], in0=bb_.t[:, 0:512], in1=decay.t[:, tb, :], op=ALU.mult),
                     reads=[bb_.r, decay.r[0]], writes=[tmpA.r[1]])
                if tb == 0:
                    k.op(k.dve, lambda: nc.vector.memset(tmpA.t[0:1, 1, :], 0.0), writes=[tmpA.r[1]])
                k.op(k.dve, lambda tb=tb: nc.vector.tensor_tensor(out=hs.t[:, tb, :], in0=tmpA.t[:, 0, :], in1=tmpA.t[:, 1, :], op=ALU.add),
                     reads=[tmpA.r[0], tmpA.r[1]], writes=[hs.r[tb]])
                k.op(k.dve, lambda tb=tb: nc.vector.tensor_tensor(out=hd.t[:, tb, :], in0=tmpA.t[:, 0, :], in1=tmpA.t[:, 1, :], op=ALU.subtract),
                     reads=[tmpA.r[0], tmpA.r[1]], writes=[hd.r[tb]])
                yield
            for f0 in range(0, L, 512):
                fn = min(512, L - f0)
                sc_, _ = pload(0, CF[:, f0:f0 + fn], L, fn)
                ss_, _ = pload(1, SF[:, f0:f0 + fn], L, fn)
                for j in range(fn // 128):
                    fc = f0 // 128 + j
                    ba = k.psum()
                    bb = k.psum()
                    k.group(k.pe, [k.mm(ba.t[:, 0:512], sc_.t[:, tb, j * 128:(j + 1) * 128], hs.t[:, tb, :], tb == 0, tb == nt - 1) for tb in range(nt)],
                            reads=[sc_.r[0]] + hs.r, writes=[ba.r])
                    k.group(k.pe, [k.mm(bb.t[:, 0:512], ss_.t[:, tb, j * 128:(j + 1) * 128], hd.t[:, tb, :], tb == 0, tb == nt - 1) for tb in range(nt)],
                            reads=[ss_.r[0]] + hd.r, writes=[bb.r])
                    wsc = wf0.t[:, 0, :] if fc == 0 else 1.0 / L
                    k.op(k.dve, lambda ba=ba, fc=fc, wsc=wsc, n=n: nc.vector.tensor_scalar(out=KA.t[:, n * nt + fc, :], in0=ba.t[:, 0:512], scalar1=wsc, scalar2=None, op0=ALU.mult),
                         reads=[ba.r, wf0.r[0]], writes=[KA.r[n * nt + fc]])
                    k.op(k.dve, lambda bb=bb, fc=fc, wsc=wsc, n=n: nc.vector.tensor_scalar(out=KB.t[:, n * nt + fc, :], in0=bb.t[:, 0:512], scalar1=wsc, scalar2=None, op0=ALU.mult),
                         reads=[bb.r, wf0.r[0]], writes=[KB.r[n * nt + fc]])
                    if fc == 0:
                        bn = k.psum()
                        k.group(k.pe, [k.mm(bn.t[0:1, 0:512], ss_.t[:, tb, 0:1], hs.t[:, tb, :], tb == 0, tb == nt - 1) for tb in range(nt)],
                                reads=[ss_.r[0]] + hs.r, writes=[bn.r])
                        k.op(k.dve, lambda bn=bn, n=n: nc.vector.tensor_scalar(out=KB.t[0:1, n * nt, :], in0=bn.t[0:1, 0:512], scalar1=0.5 / L, scalar2=None, op0=ALU.mult),
                             reads=[bn.r], writes=[KB.r[n * nt]])
                    yield
        dA, dB, dr = self.kab_d[(sfx, cb)]
        k.dma(k.sp, dA.rearrange("c p n -> p c n"), KA.t[:, :, :], reads=KA.r, writes=[dr])
        k.dma(k.sp, dB.rearrange("c p n -> p c n"), KB.t[:, :, :], reads=KB.r, writes=[dr])
        self.end_phase(es_f)
        yield

    def bg_step(self, n=1):
        for _ in range(n):
            while self.bg:
                try:
                    next(self.bg[0])
                    break
                except StopIteration:
                    self.bg.pop(0)

    def bg_drain(self):
        while self.bg:
            self.bg_step()

    def hyena_pass(self, l, unit, cb):
        k, nc = self.k, self.nc
        e = l // 2
        Tu, col = unit["Tu"], unit["col"]
        co = self.cos[l][col]
        W_in = self.din["ev_w_in"][e]
        L = unit["seqs"][0][1]
        nt = L // 128
        sfx = "S" if L == LS else "P"
        CF, SF, SI = self.din["CF" + sfx], self.din["SF" + sfx], self.din["SI" + sfx]
        es = ExitStack()
        u = k.tile("u", [128, 12, Tu], BF16, es=es)
        KA = k.tile("KA", [128, 2 * nt, 512], BF16, es=es)
        KB = k.tile("KB", [128, 2 * nt, 512], BF16, es=es)
        cwb = k.tile("cwb", [128, 1, 96], F32, es=es)
        bias_bc = k.tile("bias_bc", [128, 2, 512], F32, es=es)
        tmpA = k.tile("tmpA", [128, 2, 512], F32, es=es)
        tmpB = k.tile("tmpB", [128, 2, 512], F32, es=es)
        self.load_rows_fm(cwb.t[:, 0, 0:72], cwb.r[0], self.din["hy_conv_w"][e].rearrange("r (c p) -> (r c) p", p=128), 72)
        self.load_rows_fm(cwb.t[:, 0, 72:96], cwb.r[0], self.din["hy_conv_b"][e:e + 1, :].rearrange("o (c p) -> (o c) p", p=128), 24)
        for n in range(2):
            self.bcast_row(bias_bc.t[:, n, :], bias_bc.r[n], self.din["hy_bias"][e, n:n + 1, cb * 512:(cb + 1) * 512], 512)
        es_h = ExitStack()
        h = k.tile("hH", [128, 16, Tu], BF16, es=es_h)
        self.prenorm_unit(unit, co, 1, h, es_h)
        acts = self.h_acts(h, unit)

        def evac_u(b, nch, tt, rows):
            a, j = nch // 8, nch % 8 - cb * 4
            self.copy_op(self.ev(nch + tt), u.t[:, a * 4 + j, tt * 512:(tt + 1) * 512], b.t[:, 0:512], [b.r], [u.r[a * 4 + j]])
        for a in range(3):
            k.linear_fm(W_in, D, a * 1024 + cb * 512, a * 1024 + cb * 512 + 512, acts, evac_u)
        self.end_phase(es_h)
        dA, dB, dr = self.kab_d[(sfx, cb)]
        k.dma(k.sp, KA.t[:, :, :], dA.rearrange("c p n -> p c n"), reads=[dr], writes=KA.r)
        k.dma(k.sp, KB.t[:, :, :], dB.rearrange("c p n -> p c n"), reads=[dr], writes=KB.r)
        v_tm = k.tile("v_tm", [128, nt, 512], BF16, es=es)
        g_tm = [k.tile(f"g{i}_tm", [128, nt, 512], BF16, es=es) for i in range(2)]
        for (off, _L) in unit["seqs"]:
            es_a = ExitStack()
            ucb = k.tile("ucb", [128, 4, L], F32, es=es_a)
            for a in range(3):
                dst_tm = v_tm if a == 0 else g_tm[a - 1]
                for i in range(4):
                    gc = a * 8 + cb * 4 + i
                    ch = a * 4 + i
                    k.op(k.dve, lambda i=i, gc=gc, ch=ch: nc.vector.tensor_scalar(
                        out=ucb.t[:, i, :], in0=u.t[:, ch, off:off + L], scalar1=cwb.t[:, 0, 24 + gc:25 + gc], scalar2=cwb.t[:, 0, 72 + gc:73 + gc],
                        op0=ALU.mult, op1=ALU.add), reads=[u.r[ch], cwb.r[0]], writes=[ucb.r[i]])
                    k.op(k.dve, lambda i=i, gc=gc, ch=ch: nc.vector.scalar_tensor_tensor(
                        out=ucb.t[:, i, 1:L], in0=u.t[:, ch, off:off + L - 1], scalar=cwb.t[:, 0, gc:gc + 1], in1=ucb.t[:, i, 1:L],
                        op0=ALU.mult, op1=ALU.add), reads=[u.r[ch], cwb.r[0], ucb.r[i]], writes=[ucb.r[i]])
                    k.op(k.dve, lambda i=i, gc=gc, ch=ch: nc.vector.scalar_tensor_tensor(
                        out=ucb.t[:, i, 0:L - 1], in0=u.t[:, ch, off + 1:off + L], scalar=cwb.t[:, 0, 48 + gc:49 + gc], in1=ucb.t[:, i, 0:L - 1],
                        op0=ALU.mult, op1=ALU.add), reads=[u.r[ch], cwb.r[0], ucb.r[i]], writes=[ucb.r[i]])
                for tb in range(nt):
                    bk = k.psum()
                    k.group(k.pe, [(lambda i=i: nc.tensor.transpose(bk.t[:, i * 128:(i + 1) * 128], ucb.t[:, i, tb * 128:(tb + 1) * 128], self.ident.t[:, 0, :])) for i in range(4)],
                            reads=ucb.r + [self.ident.r[0]], writes=[bk.r])
                    self.copy_op(self.ev(tb), dst_tm.t[:, tb, :], bk.t[:, 0:512], [bk.r], [dst_tm.r[tb]])
            self.end_phase(es_a)
            es_b = ExitStack()
            P_ = k.tile("P_", [128, nt, 512], BF16, es=es_b)
            Q_ = k.tile("Q_", [128, nt, 512], BF16, es=es_b)
            z2f = k.tile("z2f", [128, nt, 512], F32, es=es_b)
            hstage = k.tile("hstage", [128, 4, L], BF16, nchunk=1, es=es_b)
            for n in range(2):
                z = v_tm if n == 0 else g_tm[0]
                gate = g_tm[n]
                for f0 in range(0, L, 512):
                    fn = min(512, L - f0)
                    sc_, _ = k.wload_ap(CF[:, f0:f0 + fn], L, fn)
                    ss_, _ = k.wload_ap(SF[:, f0:f0 + fn], L, fn)
                    for j in range(fn // 128):
                        fc = f0 // 128 + j
                        ba = k.psum()
                        bb = k.psum()
                        k.group(k.pe, [k.mm(ba.t[:, 0:512], sc_.t[:, tb, j * 128:(j + 1) * 128], z.t[:, tb, :], tb == 0, tb == nt - 1) for tb in range(nt)],
                                reads=[sc_.r[0]] + z.r, writes=[ba.r])
                        k.group(k.pe, [k.mm(bb.t[:, 0:512], ss_.t[:, tb, j * 128:(j + 1) * 128], z.t[:, tb, :], tb == 0, tb == nt - 1) for tb in range(nt)],
                                reads=[ss_.r[0]] + z.r, writes=[bb.r])
                        ka, kb_ = KA.t[:, n * nt + fc, :], KB.t[:, n * nt + fc, :]
                        kres = [KA.r[n * nt + fc], KB.r[n * nt + fc]]
                        k.op(k.dve, lambda ba=ba, ka=ka: nc.vector.tensor_tensor(out=tmpA.t[:, 0, :], in0=ba.t[:, 0:512], in1=ka, op=ALU.mult), reads=[ba.r] + kres, writes=[tmpA.r[0]])
                        k.op(k.dve, lambda bb=bb, kb_=kb_: nc.vector.tensor_tensor(out=tmpA.t[:, 1, :], in0=bb.t[:, 0:512], in1=kb_, op=ALU.mult), reads=[bb.r] + kres, writes=[tmpA.r[1]])
                        k.op(k.dve, lambda fc=fc: nc.vector.tensor_tensor(out=P_.t[:, fc, :], in0=tmpA.t[:, 0, :], in1=tmpA.t[:, 1, :], op=ALU.subtract),
                             reads=[tmpA.r[0], tmpA.r[1]], writes=[P_.r[fc]])
                        k.op(k.dve, lambda ba=ba, kb_=kb_: nc.vector.tensor_tensor(out=tmpB.t[:, 0, :], in0=ba.t[:, 0:512], in1=kb_, op=ALU.mult), reads=[ba.r] + kres, writes=[tmpB.r[0]])
                        k.op(k.dve, lambda bb=bb, ka=ka: nc.vector.tensor_tensor(out=tmpB.t[:, 1, :], in0=bb.t[:, 0:512], in1=ka, op=ALU.mult), reads=[bb.r] + kres, writes=[tmpB.r[1]])
                        k.op(k.dve, lambda fc=fc: nc.vector.tensor_tensor(out=Q_.t[:, fc, :], in0=tmpB.t[:, 0, :], in1=tmpB.t[:, 1, :], op=ALU.add),
                             reads=[tmpB.r[0], tmpB.r[1]], writes=[Q_.r[fc]])
                        if fc == 0:
                            k.op(k.dve, lambda ba=ba, n=n: nc.vector.tensor_tensor(out=P_.t[0:1, 0, :], in0=ba.t[0:1, 0:512], in1=KA.t[0:1, n * nt, :], op=ALU.mult),
                                 reads=[ba.r] + kres, writes=[P_.r[0]])
                            k.op(k.dve, lambda bb=bb, n=n: nc.vector.tensor_tensor(out=Q_.t[0:1, 0, :], in0=bb.t[0:1, 0:512], in1=KB.t[0:1, n * nt, :], op=ALU.mult),
                                 reads=[bb.r] + kres, writes=[Q_.r[0]])
                for t0 in range(0, L, 512):
                    tn = min(512, L - t0)
                    sc_, _ = k.wload_ap(CF[:, t0:t0 + tn], L, tn)
                    si_, _ = k.wload_ap(SI[:, t0:t0 + tn], L, tn)
                    for j in range(tn // 128):
                        tb = t0 // 128 + j
                        by = k.psum()
                        fns = []
                        for fc in range(nt):
                            fns.append(k.mm(by.t[:, 0:512], sc_.t[:, fc, j * 128:(j + 1) * 128], P_.t[:, fc, :], fc == 0, False))
                            fns.append(k.mm(by.t[:, 0:512], si_.t[:, fc, j * 128:(j + 1) * 128], Q_.t[:, fc, :], False, fc == nt - 1))
                        k.group(k.pe, fns, reads=[sc_.r[0], si_.r[0]] + P_.r + Q_.r, writes=[by.r])
                        k.op(k.dve, lambda tb=tb, n=n, z=z: nc.vector.tensor_tensor(out=tmpA.t[:, 0, :], in0=z.t[:, tb, :], in1=bias_bc.t[:, n, :], op=ALU.mult),
                             reads=[z.r[tb], bias_bc.r[n]], writes=[tmpA.r[0]])
                        k.op(k.dve, lambda by=by: nc.vector.tensor_tensor(out=tmpA.t[:, 1, :], in0=by.t[:, 0:512], in1=tmpA.t[:, 0, :], op=ALU.add),
                             reads=[by.r, tmpA.r[0]], writes=[tmpA.r[1]])
                        dst = gate.t[:, tb, :] if n == 0 else z2f.t[:, tb, :]
                        dres = gate.r[tb] if n == 0 else z2f.r[tb]
                        k.op(k.dve, lambda dst=dst, gate=gate, tb=tb: nc.vector.tensor_tensor(out=dst, in0=tmpA.t[:, 1, :], in1=gate.t[:, tb, :], op=ALU.mult),
                             reads=[tmpA.r[1], gate.r[tb]], writes=[dres])
            for i in range(4):
                for t0 in range(0, nt, 4):
                    tn = min(4, nt - t0)
                    bk = k.psum()
                    k.group(k.pe, [(lambda q=q: nc.tensor.transpose(bk.t[:, q * 128:(q + 1) * 128], z2f.t[:, t0 + q, i * 128:(i + 1) * 128], self.ident.t[:, 0, :])) for q in range(tn)],
                            reads=z2f.r + [self.ident.r[0]], writes=[bk.r])
                    self.copy_op(self.ev(i), hstage.t[:, i, t0 * 128:(t0 + tn) * 128], bk.t[:, 0:tn * 128], [bk.r], [hstage.r[0]])
            t0g = unit["t0"] + off
            k.dma(k.sp, self.mo_d[cb * 4:cb * 4 + 4, :, t0g:t0g + L].rearrange("c p t -> p c t"), hstage.t[:, :, :],
                  reads=[hstage.r[0]], writes=[self.mo_rs[cb * 4 + i] for i in range(4)])
            self.end_phase(es_b)
        self.end_phase(es)

    def even_mixer(self, l, unit):
        mp = self.cfg.get("mixparts", ("att", "hy", "out"))
        if "att" in mp:
            self.even_attention(l, unit)
        if "hy" in mp:
            for cb in range(2):
                self.hyena_pass(l, unit, cb)
        if "out" in mp:
            self.out_proj_unit(unit, self.din["ev_w_out"][l // 2], self.cos[l][unit["col"]])
        if not unit["isS"]:
            self.late_drain()

    def rstd_gen(self, srcs, res, T, out_t, Dn):
        k, nc = self.k, self.nc
        bk = k.psum()
        sq = self.sqbuf
        n = len(srcs)
        for c in range(n):
            i = c % 2
            k.op(k.act, lambda c=c, i=i: nc.scalar.activation(out=sq.t[:, i, 0:T], in_=srcs[c], func=AF.Square),
                 reads=[res[c]], writes=[sq.r[i]])
            k.op(k.pe, k.mm(bk.t[:, 0:T], self.ones_bf.t[:, 0, :], sq.t[:, i, 0:T], c == 0, c == n - 1),
                 reads=[sq.r[i], self.ones_bf.r[0]], writes=[bk.r])
        k.op(k.act, lambda: nc.scalar.activation(out=out_t.t[:, 0, 0:T], in_=bk.t[:, 0:T], func=AF.Sqrt,
                                                 scale=1.0 / Dn, bias=self.epsT.t[:, 0, :]),
             reads=[bk.r, self.epsT.r[0]], writes=[out_t.r[0]])
        k.op(k.dve, lambda: nc.vector.reciprocal(out=out_t.t[:, 0, 0:T], in_=out_t.t[:, 0, 0:T]),
             reads=[out_t.r[0]], writes=[out_t.r[0]])

    def mla_pass(self, l, unit):
        k, nc = self.k, self.nc
        o_ = l // 2
        Tu, col, isS = unit["Tu"], unit["col"], unit["isS"]
        co = self.cos[l][col]
        W_in = self.din["od_w_in"][o_]
        w_qb = self.din["mla_w_qb"][o_]
        w_kvb = self.din["mla_w_kvb"][o_]
        Tk = Tu + (512 if isS else 0)
        ntt = Tu // 512
        es = ExitStack()
        qn = k.tile("qn", [128, 4, Tu], BF16, es=es)
        ckvT = k.tile("ckvT", [128, 2, Tk], BF16, es=es)
        krT = k.tile("krT", [128, 1, Tk], BF16, es=es)
        gains = k.tile("gains", [128, 1, 8], F32, es=es)
        self.load_rows_fm(gains.t[:, 0, 0:4], gains.r[0], self.din["mla_q_norm"][o_:o_ + 1, :].rearrange("o (c p) -> (o c) p", p=128), 4)
        self.load_rows_fm(gains.t[:, 0, 4:6], gains.r[0], self.din["mla_kv_norm"][o_:o_ + 1, :].rearrange("o (c p) -> (o c) p", p=128), 2)
        if isS:
            cosT = k.tile("cosO", [128, 1, LS], F32, es=es)
            sinT = k.tile("sinO", [128, 1, LS], F32, es=es)
            perm = k.tile("permO", [128, 1, 128], BF16, es=es)
            k.dma(k.sp, cosT.t[:, 0, :], self.din["ropeO_cos"][:, :], writes=[cosT.r[0]])
            k.dma(k.sp, sinT.t[:, 0, :], self.din["ropeO_sin"][:, :], writes=[sinT.r[0]])
            k.dma(k.sp, perm.t[:, 0, :], self.din["permO"][:, :], writes=[perm.r[0]])
            rtmp = k.tile("rtmpO", [128, 2, 512], F32, es=es)
        es_h = ExitStack()
        h = k.tile("hM", [128, 16, Tu], BF16, es=es_h)
        lat = k.tile("lat", [128, 7, Tu], F32, es=es_h)
        self.prenorm_unit(unit, co, 1, h, es_h)
        acts = self.h_acts(h, unit)

        def evac_lat(b, nch, tt, rows):
            self.copy_op(self.ev(nch + tt), lat.t[:, nch - 40, tt * 512:(tt + 1) * 512], b.t[:, 0:512], [b.r], [lat.r[nch - 40]])
        k.linear_fm(W_in, D, 5120, 5888, acts, evac_lat)
        s = k.wslot()
        for half in range(2):
            k.dma(k.pool, s.t[:, 0:16, half * 64:(half + 1) * 64], W_in[:, 5888:5952].rearrange("(c p) n -> p c n", p=128), writes=[s.r[0]])
        for tt, (aps, res, T) in enumerate(acts):
            b = k.psum()
            k.group(k.pe, [k.mm(b.t[:, 0:T], s.t[:, c, 0:128], aps[c], c == 0, c == 15) for c in range(16)],
                    reads=[s.r[0]] + list(res), writes=[b.r])
            self.copy_op(self.ev(tt), lat.t[:, 6, tt * 512:(tt + 1) * 512], b.t[:, 0:512], [b.r], [lat.r[6]])
        ckv32 = k.tile("ckv32", [128, 2, Tu], F32, es=es_h) if not isS else None
        for tt in range(ntt):
            sl = slice(tt * 512, (tt + 1) * 512)
            self.rstd_gen([lat.t[:, c, sl] for c in range(4)], [lat.r[c] for c in range(4)], 512, self.rstd, 512.0)
            for c in range(4):
                k.op(k.dve, lambda c=c, sl=sl: nc.vector.scalar_tensor_tensor(
                    out=qn.t[:, c, sl], in0=lat.t[:, c, sl], scalar=gains.t[:, 0, c:c + 1], in1=self.rstd.t[:, 0, 0:512],
                    op0=ALU.mult, op1=ALU.mult), reads=[lat.r[c], gains.r[0], self.rstd.r[0]], writes=[qn.r[c]])
            self.rstd_gen([lat.t[:, 4 + c, sl] for c in range(2)], [lat.r[4 + c] for c in range(2)], 512, self.rstd, 256.0)
            for c in range(2):
                if isS:
                    k.op(k.dve, lambda c=c, sl=sl: nc.vector.scalar_tensor_tensor(
                        out=ckvT.t[:, c, sl], in0=lat.t[:, 4 + c, sl], scalar=gains.t[:, 0, 4 + c:5 + c], in1=self.rstd.t[:, 0, 0:512],
                        op0=ALU.mult, op1=ALU.mult), reads=[lat.r[4 + c], gains.r[0], self.rstd.r[0]], writes=[ckvT.r[c]])
                else:
                    k.op(k.dve, lambda c=c, sl=sl: nc.vector.scalar_tensor_tensor(
                        out=ckv32.t[:, c, sl], in0=lat.t[:, 4 + c, sl], scalar=gains.t[:, 0, 4 + c:5 + c], in1=self.rstd.t[:, 0, 0:512],
                        op0=ALU.mult, op1=ALU.mult), reads=[lat.r[4 + c], gains.r[0], self.rstd.r[0]], writes=[ckv32.r[c]])
                    self.copy_op(k.act, ckvT.t[:, c, sl], ckv32.t[:, c, sl], [ckv32.r[c]], [ckvT.r[c]])
        self.copy_op(k.act, krT.t[:, 0, 0:Tu], lat.t[:, 6, 0:Tu], [lat.r[6]], [krT.r[0]])
        if not isS:
            ost = k.tile("ost", [128, 2, 320], F32, es=es_h)
            for tb in range(Tu // 128):
                bi, tq = divmod(tb, 2)
                b = k.psum()
                fns = [(lambda c=c: nc.tensor.transpose(b.t[:, c * 128:(c + 1) * 128], ckv32.t[:, c, tb * 128:(tb + 1) * 128], self.ident.t[:, 0, :])) for c in range(2)]
                fns.append(lambda: nc.tensor.transpose(b.t[:, 256:320], lat.t[0:64, 6, tb * 128:(tb + 1) * 128], self.ident.t[0:64, 0, 0:64]))
                k.group(k.pe, fns, reads=[ckv32.r[0], ckv32.r[1], lat.r[6], self.ident.r[0]], writes=[b.r])
                si = tb % 2
                self.copy_op(self.ev(tb), ost.t[:, si, :], b.t[:, 0:320], [b.r], [ost.r[si]])
                k.dma(k.sp, self.dout["nckv"][bi, tq * 128:(tq + 1) * 128, :], ost.t[:, si, 0:256], reads=[ost.r[si]])
                k.dma(k.sp, self.dout["nkr"][bi, tq * 128:(tq + 1) * 128, :], ost.t[:, si, 256:320], reads=[ost.r[si]])
        else:
            ctmp = k.tile("ctmpO", [128, 4, 384], F32, nchunk=1, es=es_h)
            k.dma(k.sp, ctmp.t[:, :, 0:256], self.din["cache_ckv"].rearrange("(kb p) d -> p kb d", p=128), writes=[ctmp.r[0]])
            for half in range(2):
                k.dma(k.sp, ctmp.t[:, :, 256 + half * 64:320 + half * 64], self.din["cache_kr"].rearrange("(kb p) d -> p kb d", p=128), writes=[ctmp.r[0]])
            for c in range(3):
                b = k.psum()
                k.group(k.pe, [(lambda kb=kb: nc.tensor.transpose(b.t[:, kb * 128:(kb + 1) * 128], ctmp.t[:, kb, c * 128:(c + 1) * 128], self.ident.t[:, 0, :])) for kb in range(4)],
                        reads=[ctmp.r[0], self.ident.r[0]], writes=[b.r])
                if c < 2:
                    self.copy_op(self.ev(c), ckvT.t[:, c, Tu:Tu + 512], b.t[:, 0:512], [b.r], [ckvT.r[c]])
                else:
                    self.copy_op(k.act, krT.t[:, 0, Tu:Tu + 512], b.t[:, 0:512], [b.r], [krT.r[0]])
            self.rope_inplace_cols(krT, 0, Tu, perm, cosT, sinT, rtmp)
        self.end_phase(es_h)
        qnT = k.tile("qnT", [128, 8, Tu], BF16, es=es)
        qrT = k.tile("qrT", [128, 4, Tu], BF16, es=es)
        knT = k.tile("knT", [128, 8, Tk], BF16, es=es)
        V_tm = k.tile("V_tm", [128, Tk // 128, 1024], BF16, es=es)
        ostage = k.tile("ostageM", [128, 2, Tu], BF16, es=es)
        wk = self.attn_wks(es, 1536 if isS else 256, 12 if isS else 2)
        qacts = [([qn.t[:, c, i * 512:(i + 1) * 512] for c in range(4)], qn.r, 512) for i in range(ntt)]
        wq3 = w_qb.rearrange("k (h n) -> k h n", n=192)
        for hb in range(2):
            s = k.wslot()
            for j in range(4):
                k.dma(k.pool, s.t[:, 0:4, j * 128:(j + 1) * 128], wq3[:, hb * 4 + j, 0:128].rearrange("(c p) n -> p c n", p=128), writes=[s.r[0]])
            for tt, (aps, res, T) in enumerate(qacts):
                for j in range(4):
                    b = k.psum()
                    k.group(k.pe, [k.mm(b.t[:, 0:T], s.t[:, c, j * 128:(j + 1) * 128], aps[c], c == 0, c == 3) for c in range(4)],
                            reads=[s.r[0]] + list(res), writes=[b.r])
                    self.copy_op(self.ev(j), qnT.t[:, hb * 4 + j, tt * 512:(tt + 1) * 512], b.t[:, 0:512], [b.r], [qnT.r[hb * 4 + j]])
        s = k.wslot()
        for j in range(8):
            k.dma(k.pool, s.t[:, 0:4, j * 64:(j + 1) * 64], wq3[:, j, 128:192].rearrange("(c p) n -> p c n", p=128), writes=[s.r[0]])
        for tt, (aps, res, T) in enumerate(qacts):
            for j in range(4):
                b = k.psum()
                k.group(k.pe, [k.mm(b.t[:, 0:T], s.t[:, c, j * 128:(j + 1) * 128], aps[c], c == 0, c == 3) for c in range(4)],
                        reads=[s.r[0]] + list(res), writes=[b.r])
                self.copy_op(self.ev(j), qrT.t[:, j, tt * 512:(tt + 1) * 512], b.t[:, 0:512], [b.r], [qrT.r[j]])
        if isS:
            self.rope_inplace(qrT, range(4), Tu, perm, cosT, sinT, rtmp)
        wkv3 = w_kvb.rearrange("k (h n) -> k h n", n=256)
        ntk = (Tk + 511) // 512
        for hb in range(2):
            s = k.wslot()
            for j in range(4):
                k.dma(k.pool, s.t[:, 0:2, j * 128:(j + 1) * 128], wkv3[:, hb * 4 + j, 0:128].rearrange("(c p) n -> p c n", p=128), writes=[s.r[0]])
            for tt in range(ntk):
                sl = slice(tt * 512, (tt + 1) * 512)
                for j in range(4):
                    b = k.psum()
                    k.group(k.pe, [k.mm(b.t[:, 0:512], s.t[:, c, j * 128:(j + 1) * 128], ckvT.t[:, c, sl], c == 0, c == 1) for c in range(2)],
                            reads=[s.r[0]] + ckvT.r, writes=[b.r])
                    self.copy_op(self.ev(j), knT.t[:, hb * 4 + j, sl], b.t[:, 0:512], [b.r], [knT.r[hb * 4 + j]])
            s = k.wslot()
            for j in range(4):
                k.dma(k.pool, s.t[:, 0:2, j * 128:(j + 1) * 128], wkv3[:, hb * 4 + j, 128:256].rearrange("(c p) n -> p c n", p=128), writes=[s.r[0]])
            for tb in range(Tk // 128):
                b = k.psum()
                k.group(k.pe, [k.mm(b.t[:, 0:512], ckvT.t[:, c, tb * 128:(tb + 1) * 128], s.t[:, c, 0:512], c == 0, c == 1) for c in range(2)],
                        reads=[s.r[0]] + ckvT.r, writes=[b.r])
                self.copy_op(self.ev(tb), V_tm.t[:, tb, hb * 512:(hb + 1) * 512], b.t[:, 0:512], [b.r], [V_tm.r[tb]])
        SC = 192 ** -0.5
        blocks = []
        for hh in range(8):
            osl = hh % 2
            pb = (hh % 2) * 64
            for qg in range(ntt):
                ob = k.psum_hold(hh * 2 + qg)
                for qi in range(4):
                    qb = qg * 4 + qi
                    qparts = [(qnT.t[:, hh, qb * 128:(qb + 1) * 128], qnT.r[hh]),
                              (qrT.t[pb:pb + 64, hh // 2, qb * 128:(qb + 1) * 128], qrT.r[hh // 2])]
                    if isS:
                        kranges = [(i * 512, 512) for i in range(3)]
                    else:
                        kranges = [((qb // 2) * 256, 256)]
                    segs = [(n, [knT.t[:, hh, k0:k0 + n], krT.t[pb:pb + 64, 0, k0:k0 + n]], [knT.r[hh], krT.r[0]], None) for k0, n in kranges]
                    vbl = []
                    for k0, n in kranges:
                        vbl += [(V_tm.t[:, k0 // 128 + kb, hh * 128:(hh + 1) * 128], V_tm.r[k0 // 128 + kb]) for kb in range(n // 128)]
                    blk = dict(qparts=qparts, segs=segs, vbl=vbl, sink=None, sink_res=None, scale=SC, ob=ob, outcol=qi * 128)
                    if qi == 3:
                        def post(hh=hh, qg=qg, ob=ob, osl=osl, last=(qg == ntt - 1)):
                            self.copy_op(self.ev(qg), ostage.t[:, osl, qg * 512:(qg + 1) * 512], ob.t[:, 0:512], [ob.r], [ostage.r[osl]])
                            if last:
                                k.dma(k.sp, self.mo_ap(unit, 8 + hh, 1), ostage.t[:, osl:osl + 1, :], reads=[ostage.r[osl]], writes=[self.mo_rs[8 + hh]])
                        blk["post"] = post
                    blocks.append(blk)
        self.attn_run(blocks, wk)
        self.end_phase(es)

    def rope_inplace_cols(self, x, c, Tu, perm, cos, sin, tmp):
        self.rope_inplace(x, [c], Tu, perm, cos, sin, tmp)

    def end_phase_keep(self, es, keep=False):
        pass

    def hgrn_pass(self, l, unit, hg, nh):
        k, nc = self.k, self.nc
        o_ = l // 2
        Tu, col, isS = unit["Tu"], unit["col"], unit["isS"]
        co = self.cos[l][col]
        W_in = self.din["od_w_in"][o_]
        ntb = Tu // 128
        heads = [nh * hg + i for i in range(nh)]
        es = ExitStack()
        pj = k.tile("pj5", [128, 5 * nh, Tu], BF16, es=es)
        v_tm = k.tile("hv_tm", [128, ntb, 128 * nh], BF16, es=es)
        lbt = k.tile("lbt", [128, 1, 64], F32, es=es)
        gn = k.tile("gn", [128, 1, 1], F32, es=es)
        oacc = k.tile("oacc", [128, nh, Tu], F32, es=es)
        Sst = k.tile("Sst", [128, 2 * nh, 128], F32, es=es)
        Sbf = k.tile("Sbf", [128, 2 * nh, 128], BF16, es=es)
        ostage = k.tile("ostageH", [128, nh, Tu], BF16, es=es)
        self.load_rows_fm(lbt.t[:, 0, 0:32], lbt.r[0], self.din["hg_lb"].rearrange("l d (h p) -> (l d h) p", p=128), 32)
        k.op(k.dve, lambda: nc.vector.tensor_tensor(out=lbt.t[:, 0, 32:48], in0=lbt.t[:, 0, 16:32], in1=lbt.t[:, 0, 0:16], op=ALU.subtract),
             reads=[lbt.r[0]], writes=[lbt.r[0]])
        k.op(k.act, lambda: nc.scalar.activation(out=lbt.t[:, 0, 32:48], in_=lbt.t[:, 0, 32:48], func=AF.Sigmoid), reads=[lbt.r[0]], writes=[lbt.r[0]])
        k.op(k.dve, lambda: nc.vector.tensor_scalar(out=lbt.t[:, 0, 48:64], in0=lbt.t[:, 0, 32:48], scalar1=-1.0, scalar2=1.0, op0=ALU.mult, op1=ALU.add),
             reads=[lbt.r[0]], writes=[lbt.r[0]])
        self.load_rows_fm(gn.t[:, 0, 0:1], gn.r[0], self.din["hg_norm"][o_:o_ + 1, :], 1)
        es_h = ExitStack()
        h = k.tile("hG", [128, 16, Tu], BF16, es=es_h)
        self.prenorm_unit(unit, co, 1, h, es_h)
        acts = self.h_acts(h, unit)
        for a in range(5):
            s, kc = k.wload(W_in, 0, D, a * 1024 + hg * 128 * nh, 128 * nh)
            for tt, (aps, res, T) in enumerate(acts):
                for j in range(nh):
                    b = k.psum()
                    k.group(k.pe, [k.mm(b.t[:, 0:T], s.t[:, c, j * 128:(j + 1) * 128], aps[c], c == 0, c == 15) for c in range(16)],
                            reads=[s.r[0]] + list(res), writes=[b.r])
                    self.copy_op(self.ev(j), pj.t[:, a * nh + j, tt * 512:(tt + 1) * 512], b.t[:, 0:512], [b.r], [pj.r[a * nh + j]])
            if a == 3:
                for tb in range(ntb):
                    b = k.psum()
                    k.group(k.pe, [k.mm(b.t[:, 0:128 * nh], h.t[:, c, tb * 128:(tb + 1) * 128], s.t[:, c, 0:128 * nh], c == 0, c == 15) for c in range(16)],
                            reads=[s.r[0]] + h.r, writes=[b.r])
                    self.copy_op(self.ev(tb), v_tm.t[:, tb, :], b.t[:, 0:128 * nh], [b.r], [v_tm.r[tb]])
        self.end_phase(es_h)
        es_w = ExitStack()
        L = unit["seqs"][0][1]
        nseq = len(unit["seqs"])
        nch = Tu // 64
        w32 = k.tile("w32", [128, 5, Tu], F32, es=es_w)
        cmask = k.tile("cmask", [128, 1, Tu], F32, es=es_w)
        k.op(k.dve, lambda: nc.vector.memset(cmask.t[:, 0, :], 1.0), writes=[cmask.r[0]])
        k.op(k.dve, lambda: nc.vector.memset(cmask.t[:, 0, :].rearrange("p (c s) -> p c s", s=64)[:, :, 0:1], 0.0), writes=[cmask.r[0]])
        tri = k.tile("tri", [128, 2, 128], F32, es=es_w)
        k.dma(k.sp, tri.t[:, :, :], self.din["hg_tri"].rearrange("d s t -> s d t"), writes=tri.r)
        QI = k.tile("QI", [128, 2 * nh, Tu], BF16, es=es_w)
        KI = k.tile("KI", [128, 2 * nh, Tu], BF16, es=es_w)
        QX = k.tile("QX", [128, nh, Tu], BF16, es=es_w)
        KS = k.tile("KS", [128, 2 * nh * ntb, 128], BF16, nchunk=2 * nh, es=es_w)
        dec = k.tile("dec", [128, 2 * nh, nch], F32, es=es_w)
        ATs = k.tile("ATs", [128, 2 * nh, 128], BF16, es=es_w)
        for hi, hd_ in enumerate(heads):
            k.op(k.act, lambda hi=hi: nc.scalar.activation(out=w32.t[:, 4, :], in_=pj.t[:, 0 * nh + hi, :], func=AF.Silu), reads=[pj.r[hi]], writes=[w32.r[4]])
            for d in range(2):
                ch = hi * 2 + d
                lcol = d * 8 + hd_
                fsrc = pj.t[:, (1 + d) * nh + hi, :]
                fres = pj.r[(1 + d) * nh + hi]
                k.op(k.act, lambda fsrc=fsrc: nc.scalar.activation(out=w32.t[:, 0, :], in_=fsrc, func=AF.Sigmoid), reads=[fres], writes=[w32.r[0]])
                k.op(k.dve, lambda lcol=lcol: nc.vector.tensor_scalar(out=w32.t[:, 0, :], in0=w32.t[:, 0, :], scalar1=lbt.t[:, 0, 48 + lcol:49 + lcol],
                                                                      scalar2=lbt.t[:, 0, 32 + lcol:33 + lcol], op0=ALU.mult, op1=ALU.add),
                     reads=[w32.r[0], lbt.r[0]], writes=[w32.r[0]])
                k.op(k.act, lambda: nc.scalar.activation(out=w32.t[:, 1, :], in_=w32.t[:, 0, :], func=AF.Ln), reads=[w32.r[0]], writes=[w32.r[1]])
                k.op(k.dve, lambda: nc.vector.tensor_scalar(out=w32.t[:, 0, :], in0=w32.t[:, 0, :], scalar1=-1.0, scalar2=1.0, op0=ALU.mult, op1=ALU.add),
                     reads=[w32.r[0]], writes=[w32.r[0]])
                k.op(k.dve, lambda: nc.vector.tensor_tensor_scan(out=w32.t[:, 2, :], data0=cmask.t[:, 0, :], data1=w32.t[:, 1, :], initial=0.0,
                                                                 op0=ALU.mult, op1=ALU.add), reads=[cmask.r[0], w32.r[1]], writes=[w32.r[2]])
                b3 = w32.t[:, 2, :].rearrange("p (c s) -> p c s", s=64)
                k.op(k.act, lambda ch=ch, b3=b3: nc.scalar.activation(out=dec.t[:, ch, :], in_=b3[:, :, 63], func=AF.Exp), reads=[w32.r[2]], writes=[dec.r[ch]])
                if d == 0:
                    k.op(k.act, lambda: nc.scalar.activation(out=w32.t[:, 3, :], in_=w32.t[:, 2, :], func=AF.Exp), reads=[w32.r[2]], writes=[w32.r[3]])
                    k.op(k.dve, lambda ch=ch: nc.vector.tensor_tensor(out=QI.t[:, ch, :], in0=w32.t[:, 4, :], in1=w32.t[:, 3, :], op=ALU.mult),
                         reads=[w32.r[4], w32.r[3]], writes=[QI.r[ch]])
                    k.op(k.act, lambda: nc.scalar.activation(out=w32.t[:, 3, :], in_=w32.t[:, 2, :], func=AF.Exp, scale=-1.0), reads=[w32.r[2]], writes=[w32.r[3]])
                    k.op(k.dve, lambda ch=ch: nc.vector.tensor_tensor(out=KI.t[:, ch, :], in0=w32.t[:, 0, :], in1=w32.t[:, 3, :], op=ALU.mult),
                         reads=[w32.r[0], w32.r[3]], writes=[KI.r[ch]])
                    k.op(k.dve, lambda b3=b3: nc.vector.tensor_tensor(out=w32.t[:, 3, :].rearrange("p (c s) -> p c s", s=64),
                                                                      in0=b3[:, :, 63:64].to_broadcast([128, nch, 64]), in1=b3, op=ALU.subtract),
                         reads=[w32.r[2]], writes=[w32.r[3]])
                else:
                    k.op(k.dve, lambda: nc.vector.tensor_tensor(out=w32.t[:, 1, :], in0=w32.t[:, 2, :], in1=w32.t[:, 1, :], op=ALU.subtract),
                         reads=[w32.r[2], w32.r[1]], writes=[w32.r[1]])
                    k.op(k.act, lambda: nc.scalar.activation(out=w32.t[:, 3, :], in_=w32.t[:, 1, :], func=AF.Exp, scale=-1.0), reads=[w32.r[1]], writes=[w32.r[3]])
                    k.op(k.dve, lambda ch=ch: nc.vector.tensor_tensor(out=QI.t[:, ch, :], in0=w32.t[:, 4, :], in1=w32.t[:, 3, :], op=ALU.mult),
                         reads=[w32.r[4], w32.r[3]], writes=[QI.r[ch]])
                    k.op(k.dve, lambda b3=b3: nc.vector.tensor_tensor(out=w32.t[:, 3, :].rearrange("p (c s) -> p c s", s=64),
                                                                      in0=b3[:, :, 63:64].to_broadcast([128, nch, 64]),
                                                                      in1=w32.t[:, 1, :].rearrange("p (c s) -> p c s", s=64), op=ALU.subtract),
                         reads=[w32.r[2], w32.r[1]], writes=[w32.r[3]])
                    k.op(k.act, lambda: nc.scalar.activation(out=w32.t[:, 3, :], in_=w32.t[:, 3, :], func=AF.Exp), reads=[w32.r[3]], writes=[w32.r[3]])
                    k.op(k.dve, lambda hi=hi: nc.vector.tensor_tensor(out=QX.t[:, hi, :], in0=w32.t[:, 4, :], in1=w32.t[:, 3, :], op=ALU.mult),
                         reads=[w32.r[4], w32.r[3]], writes=[QX.r[hi]])
                    self.copy_op(k.dve, w32.t[:, 3, :], w32.t[:, 1, :], [w32.r[1]], [w32.r[3]])
                k.op(k.act, lambda: nc.scalar.activation(out=w32.t[:, 3, :], in_=w32.t[:, 3, :], func=AF.Exp), reads=[w32.r[3]], writes=[w32.r[3]])
                k.op(k.dve, lambda: nc.vector.tensor_tensor(out=w32.t[:, 3, :], in0=w32.t[:, 0, :], in1=w32.t[:, 3, :], op=ALU.mult),
                     reads=[w32.r[0], w32.r[3]], writes=[w32.r[3]])
                if d == 1:
                    self.copy_op(k.act, KI.t[:, ch, :], w32.t[:, 3, :], [w32.r[3]], [KI.r[ch]])
                for t0 in range(0, ntb, 4):
                    tn = min(4, ntb - t0)
                    b = k.psum()
                    k.group(k.pe, [(lambda q=q: nc.tensor.transpose(b.t[:, q * 128:(q + 1) * 128], w32.t[:, 3, (t0 + q) * 128:(t0 + q + 1) * 128], self.ident.t[:, 0, :])) for q in range(tn)],
                            reads=[w32.r[3], self.ident.r[0]], writes=[b.r])
                    self.copy_op(self.ev(t0 // 4), KS.t[:, ch * ntb + t0:ch * ntb + t0 + tn, :], b.t[:, 0:tn * 128].rearrange("p (q t) -> p q t", t=128), [b.r], [KS.r[ch]])
        k.op(k.dve, lambda: nc.vector.memset(oacc.t[:, :, :], 0.0), writes=oacc.r)
        for (off, _L) in unit["seqs"]:
            bi = off // L
            nblk_ = L // 128
            for hi, hd_ in enumerate(heads):
                for d in range(2):
                    ch = hi * 2 + d
                    if isS:
                        k.dma(k.sp, Sst.t[:, ch, :], self.din["state"][d, hd_], writes=[Sst.r[ch]])
                    else:
                        k.op(k.dve, lambda ch=ch: nc.vector.memset(Sst.t[:, ch, :], 0.0), writes=[Sst.r[ch]])
                    self.copy_op(k.act, Sbf.t[:, ch, :], Sst.t[:, ch, :], [Sst.r[ch]], [Sbf.r[ch]])
            for step in range(nblk_):
                for hi, hd_ in enumerate(heads):
                    for d in range(2):
                        ch = hi * 2 + d
                        blk = step if d == 0 else nblk_ - 1 - step
                        tbg = off // 128 + blk
                        c0 = tbg * 128
                        vcols = slice(hi * 128, (hi + 1) * 128)
                        ba = k.psum()
                        k.op(k.pe, k.mm(ba.t[:, 0:128], KI.t[:, ch, c0:c0 + 128], QI.t[:, ch, c0:c0 + 128], True, True),
                             reads=[KI.r[ch], QI.r[ch]], writes=[ba.r])
                        k.op(k.dve, lambda ba=ba, ch=ch, d=d: nc.vector.tensor_tensor(out=ATs.t[:, ch, :], in0=ba.t[:, 0:128], in1=tri.t[:, d, :], op=ALU.mult),
                             reads=[ba.r, tri.r[d]], writes=[ATs.r[ch]])
                        bo = k.psum()
                        k.op(k.pe, k.mm(bo.t[:, 0:128], v_tm.t[:, tbg, vcols], ATs.t[:, ch, :], True, True),
                             reads=[v_tm.r[tbg], ATs.r[ch]], writes=[bo.r])
                        k.op(k.dve, lambda bo=bo, hi=hi, c0=c0: nc.vector.tensor_tensor(out=oacc.t[:, hi, c0:c0 + 128], in0=bo.t[:, 0:128], in1=oacc.t[:, hi, c0:c0 + 128], op=ALU.add),
                             reads=[bo.r, oacc.r[hi]], writes=[oacc.r[hi]])
                        qx = QI if d == 0 else QX
                        qxi = ch if d == 0 else hi
                        for cc in ((0, 1) if d == 0 else (1, 0)):
                            t0 = c0 + cc * 64
                            gch = t0 // 64
                            bi_ = k.psum()
                            k.op(k.pe, k.mm(bi_.t[:, 0:64], Sbf.t[:, ch, :], qx.t[:, qxi, t0:t0 + 64], True, True),
                                 reads=[Sbf.r[ch], qx.r[qxi]], writes=[bi_.r])
                            k.op(k.dve, lambda bi_=bi_, hi=hi, t0=t0: nc.vector.tensor_tensor(out=oacc.t[:, hi, t0:t0 + 64], in0=bi_.t[:, 0:64], in1=oacc.t[:, hi, t0:t0 + 64], op=ALU.add),
                                 reads=[bi_.r, oacc.r[hi]], writes=[oacc.r[hi]])
                            bs = k.psum()
                            pb = cc * 64
                            k.op(k.pe, k.mm(bs.t[:, 0:128], KS.t[pb:pb + 64, ch * ntb + tbg, :], v_tm.t[pb:pb + 64, tbg, vcols], True, True),
                                 reads=[KS.r[ch], v_tm.r[tbg]], writes=[bs.r])
                            k.op(k.dve, lambda bs=bs, ch=ch, gch=gch: nc.vector.scalar_tensor_tensor(
                                out=Sst.t[:, ch, :], in0=Sst.t[:, ch, :], scalar=dec.t[:, ch, gch:gch + 1], in1=bs.t[:, 0:128], op0=ALU.mult, op1=ALU.add),
                                reads=[Sst.r[ch], dec.r[ch], bs.r], writes=[Sst.r[ch]])
                            self.copy_op(k.act, Sbf.t[:, ch, :], Sst.t[:, ch, :], [Sst.r[ch]], [Sbf.r[ch]])
            if not isS:
                for hi, hd_ in enumerate(heads):
                    for d in range(2):
                        k.dma(k.sp, self.dout["ns"][bi, d, hd_], Sst.t[:, hi * 2 + d, :], reads=[Sst.r[hi * 2 + d]])
        for hi, hd_ in enumerate(heads):
            for tt in range(Tu // 512):
                sl = slice(tt * 512, (tt + 1) * 512)
                self.rstd_gen([oacc.t[:, hi, sl]], [oacc.r[hi]], 512, self.rstd, 128.0)
                k.op(k.act, lambda hi=hi, sl=sl: nc.scalar.activation(out=self.ntmp.t[:, 0, :], in_=pj.t[:, 4 * nh + hi, sl], func=AF.Silu), reads=[pj.r[4 * nh + hi]], writes=[self.ntmp.r[0]])
                k.op(k.dve, lambda hi=hi, sl=sl: nc.vector.scalar_tensor_tensor(
                    out=self.ntmp.t[:, 1, :], in0=oacc.t[:, hi, sl], scalar=gn.t[:, 0, 0:1], in1=self.rstd.t[:, 0, 0:512], op0=ALU.mult, op1=ALU.mult),
                    reads=[oacc.r[hi], gn.r[0], self.rstd.r[0]], writes=[self.ntmp.r[1]])
                k.op(k.dve, lambda hi=hi, sl=sl: nc.vector.tensor_tensor(out=ostage.t[:, hi, sl], in0=self.ntmp.t[:, 0, :], in1=self.ntmp.t[:, 1, :], op=ALU.mult),
                     reads=[self.ntmp.r[0], self.ntmp.r[1]], writes=[ostage.r[hi]])
            k.dma(k.sp, self.mo_ap(unit, hd_, 1), ostage.t[:, hi:hi + 1, :], reads=[ostage.r[hi]], writes=[self.mo_rs[hd_]])
        self.end_phase(es_w)
        self.end_phase(es)

    def odd_mixer(self, l, unit):
        mp = self.cfg.get("mixparts", ("att", "hg", "out"))
        if "att" in mp:
            self.mla_pass(l, unit)
        if "hg" in mp:
            nh = 2 if unit["isS"] else 4
            for hg in range(8 // nh):
                self.hgrn_pass(l, unit, hg, nh)
        if "out" in mp:
            self.out_proj_unit(unit, self.din["od_w_out"][l // 2], self.cos[l][unit["col"]])

    def build(self):
        cfg = self.cfg
        k, nc = self.k, self.nc
        xs_d = self.inp("xs", [LS, D])
        xp_d = self.inp("xp", [NP_ * LP, D])
        self.inp("c2", [2, D])
        self.inp("mod_w", [2, D, 9 * D])
        self.inp("mod_b", [2, 9 * D])
        self.inp("norm_g", [12, D])
        self.inp("ffn_wg", [2, 2, D, DFF])
        self.inp("ffn_wu", [2, 2, D, DFF])
        self.inp("ffn_wd", [2, 2, DFF, D])
        self.inp("ev_w_in", [1, D, 4608])
        self.inp("ev_w_out", [1, D, D])
        self.inp("hy_conv_w", [1, 3, 3072])
        self.inp("hy_conv_b", [1, 3072])
        self.inp("hy_f_w1", [1, 33, 64])
        self.inp("hy_f_b1", [1, 64])
        self.inp("hy_f_w2", [1, 64, 64])
        self.inp("hy_f_b2", [1, 64])
        self.inp("hy_f_w3", [1, 64, 4096])
        self.inp("hy_f_freq", [1, 64])
        self.inp("hy_bias", [1, 2, 1024])
        self.inp("attn_sink", [1, 8])
        self.inp("cache_k", [2, 512, 128])
        self.inp("cache_v", [2, 512, 128])
        self.inp("od_w_in", [1, D, 5952])
        self.inp("od_w_out", [1, D, D])
        self.inp("hg_lb", [2, 2, 1024])
        self.inp("hg_norm", [1, 128])
        self.inp("mla_q_norm", [1, 512])
        self.inp("mla_w_qb", [1, 512, 1536])
        self.inp("mla_kv_norm", [1, 256])
        self.inp("mla_w_kvb", [1, 256, 2048])
        self.inp("cache_ckv", [512, 256])
        self.inp("cache_kr", [512, 64])
        self.inp("state", [2, 8, 128, 128])
        self.inp("ropeO_cos", [128, LS])
        self.inp("ropeO_sin", [128, LS])
        self.inp("permO", [128, 128], BF16)
        self.inp("hg_tri", [2, 128, 128])
        self.inp("ropeE_cos", [128, LS])
        self.inp("ropeE_sin", [128, LS])
        self.inp("permE", [128, 128], BF16)
        for sfx, L_ in (("S", LS), ("P", LP)):
            for nm in ("CF", "SF", "SI"):
                self.inp(nm + sfx, [L_, L_], BF16)
            self.inp("zT" + sfx, [33, L_])
            self.inp("decay" + sfx, [L_, 1024])
            self.inp("wf0" + sfx, [128, 1])
        ys_d = self.outp("ys", [LS, D])
        yp_d = self.outp("yp", [NP_ * LP, D])
        self.outp("nk", [NP_, 2, LP, 128])
        self.outp("nv", [NP_, 2, LP, 128])
        self.outp("nckv", [NP_, LP, 256])
        self.outp("nkr", [NP_, LP, 64])
        self.outp("ns", [NP_, 2, 8, 128, 128])
        self.xres = nc.dram_tensor("xres", [16, 128, 1536], F32).ap()
        self.xres_r = [Res() for _ in range(3)]
        if cfg.get("debug"):
            self.mo_d = self.outp("mo_d", [16, 128, 1536], BF16)
        else:
            self.mo_d = nc.dram_tensor("mo_d", [16, 128, 1536], BF16).ap()
        self.mo_rs = [Res() for _ in range(16)]
        self.h_d = nc.dram_tensor("h_d", [16, 128, 1536], BF16).ap()
        self.h_r = Res()
        self.h_key = None
        self.attn_hook = None
        self.late_mods = []
        self.kab_d = {}
        for sfx, L_ in (("S", LS), ("P", LP)):
            for cb in range(2):
                nt_ = L_ // 128
                self.kab_d[(sfx, cb)] = (nc.dram_tensor(f"ka_{sfx}{cb}", [2 * nt_, 128, 512], BF16).ap(),
                                         nc.dram_tensor(f"kb_{sfx}{cb}", [2 * nt_, 128, 512], BF16).ap(), Res())
        self.bg = []
        self.units = {
            "S": dict(tiles=[0, 1], Tu=LS, t0=0, seqs=[(0, LS)], col=0, isS=True),
            "P": dict(tiles=[2], Tu=NP_ * LP, t0=LS, seqs=[(0, LP), (LP, LP)], col=1, isS=False),
        }

        self.consts()
        self.mods_all()
        self.load_x_all([(xs_d, LS), (xp_d, NP_ * LP)])
        nlayers = cfg.get("nlayers", 2)
        tiles = cfg.get("tiles", (0, 1, 2))
        parts = cfg.get("parts", ("ffn0", "mix", "ffn1"))
        try:
            self.body(cfg, nlayers, tiles, parts)
        except StopBuild:
            pass
        self.store_x_all([(ys_d, LS), (yp_d, NP_ * LP)])
        k.barrier(engines=(k.sp,))
        return nc

    def body(self, cfg, nlayers, tiles, parts):
        for l in cfg.get("layers", range(nlayers)):
            if "ffn0" in parts:
                self.ffn_phase(l, 0, 0, tiles)
            if "mix" in parts:
                for un in cfg.get("units", ("S", "P")):
                    if l % 2 == 0:
                        self.even_mixer(l, self.units[un])
                    else:
                        self.odd_mixer(l, self.units[un])
            if "ffn1" in parts:
                self.ffn_phase(l, 1, 2, tiles)


def _bf(a):
    return np.asarray(a, dtype=np.float32).astype(ml_dtypes.bfloat16)


def _rope_tables(L, rot_dim, nrep):
    half = rot_dim // 2
    inv = (10000.0 ** (-np.arange(0, half, 2, dtype=np.float32) / half)).astype(np.float32)
    nq = half // 2
    t = np.arange(L)
    row = (t // 64).astype(np.float32)
    colp = (t % 64).astype(np.float32)
    cos = np.zeros((rot_dim, L), np.float32)
    sin = np.zeros((rot_dim, L), np.float32)
    perm = np.zeros((rot_dim, rot_dim), np.float32)
    for p_ in range(rot_dim):
        first = p_ < half
        q = (p_ if first else p_ - half)
        i = q % nq
        ang = ((row if first else colp) * inv[i]).astype(np.float32)
        cos[p_] = np.cos(ang)
        sin[p_] = np.sin(ang)
        base = 0 if first else half
        if q < nq:
            perm[base + q + nq, p_] = -1.0
        else:
            perm[base + q - nq, p_] = 1.0
    cos = np.tile(cos, (nrep, 1))
    sin = np.tile(sin, (nrep, 1))
    P = np.zeros((nrep * rot_dim, nrep * rot_dim), np.float32)
    for r in range(nrep):
        P[r * rot_dim:(r + 1) * rot_dim, r * rot_dim:(r + 1) * rot_dim] = perm
    return cos, sin, P


def make_consts():
    cst = {"ident": np.eye(128, dtype=np.float32)}
    q = np.arange(128)[:, None]
    kk = np.arange(128)[None, :]
    NEG = -30000.0
    m = np.zeros((128, 384), np.float32)
    m[:, 0:128] = np.where(kk >= q, 0.0, NEG)
    m[:, 256:384] = np.where(kk <= q, 0.0, NEG)
    cst["maskL"] = m
    sI = np.arange(128)[:, None]
    tI = np.arange(128)[None, :]
    same = (sI // 64) == (tI // 64)
    cst["hg_tri"] = np.stack([(same & (tI >= sI)), (same & (tI <= sI))], 0).astype(np.float32)
    c, s_, P = _rope_tables(LS, 128, 1)
    cst["ropeE_cos"], cst["ropeE_sin"], cst["permE"] = c, s_, _bf(P)
    c, s_, P = _rope_tables(LS, 64, 2)
    cst["ropeO_cos"], cst["ropeO_sin"], cst["permO"] = c, s_, _bf(P)
    for sfx, L in (("S", LS), ("P", LP)):
        t = np.arange(L, dtype=np.float64)
        ft = np.outer(t, t) * (np.pi / L)
        CF = np.cos(ft)
        SF = np.sin(ft)
        SF[:, 0] = (-1.0) ** t
        cst["CF" + sfx] = _bf(CF)
        cst["SF" + sfx] = _bf(SF)
        cst["SI" + sfx] = _bf(SF.T.copy())
        tt = np.linspace(0.0, 1.0, L, dtype=np.float32)[:, None]
        bands = 16
        w = (2.0 * math.pi * np.arange(L, dtype=np.float32)[:, None] / L).astype(np.float32)
        fb = np.linspace(1e-4, bands - 1, bands, dtype=np.float32)[None, :]
        z = np.concatenate([tt, np.cos(fb * w), -np.sin(fb * w)], axis=-1).astype(np.float32)
        cst["zT" + sfx] = np.ascontiguousarray(z.T)
        deltas = np.abs(np.linspace(math.log(1e-2) / 1.5, math.log(1e-2) / 0.3, 1024, dtype=np.float32))
        cst["decay" + sfx] = np.exp(-tt * deltas).astype(np.float32)
        wf = np.full((128, 1), 1.0 / L, np.float32)
        wf[0, 0] = 0.5 / L
        cst["wf0" + sfx] = wf
    return cst


def core_inputs(core, inputs, consts):
    m = dict(consts)
    m["xs"] = np.ascontiguousarray(inputs["x_sample"][core])
    m["xp"] = np.ascontiguousarray(inputs["x_prompt"][2 * core:2 * core + 2].reshape(NP_ * LP, D))
    m["c2"] = np.ascontiguousarray(np.stack([inputs["c"][core], inputs["c_ctx"]], 0))
    m["norm_g"] = np.ascontiguousarray(inputs["norm_g"].reshape(12, D))
    m["cache_k"] = np.ascontiguousarray(inputs["cache_attn_k"][core, 0])
    m["cache_v"] = np.ascontiguousarray(inputs["cache_attn_v"][core, 0])
    m["cache_ckv"] = np.ascontiguousarray(inputs["cache_mla_ckv"][core, 0])
    m["cache_kr"] = np.ascontiguousarray(inputs["cache_mla_krope"][core, 0])
    m["state"] = np.ascontiguousarray(inputs["state_hgrn"][core, 0])
    for n, v in inputs.items():
        if n not in m and n not in ("x_sample", "x_prompt", "c", "c_ctx", "cache_attn_k", "cache_attn_v",
                                    "cache_mla_ckv", "cache_mla_krope", "state_hgrn"):
            m[n] = v
    return m


def run(inputs, cfg, cores=range(NCORES)):
    inputs = {k_: np.asarray(v) for k_, v in inputs.items()}
    b = Builder(cfg)
    nc = b.build()
    consts = make_consts()
    cores = list(cores)
    in_maps = [{n: m[n] for n in b.din} for m in (core_inputs(c, inputs, consts) for c in cores)]
    res = run_bass_kernel_spmd(nc, in_maps, core_ids=list(range(len(cores))))
    return b, res


def kernel(**inputs):
    b, res = run(inputs, {})
    R = res.results
    ys = np.stack([r["ys"] for r in R], 0)
    yp = np.concatenate([r["yp"].reshape(NP_, LP, D) for r in R], 0)
    nk = np.concatenate([r["nk"] for r in R], 0)[:, None]
    nv = np.concatenate([r["nv"] for r in R], 0)[:, None]
    nckv = np.concatenate([r["nckv"] for r in R], 0)[:, None]
    nkr = np.concatenate([r["nkr"] for r in R], 0)[:, None]
    ns = np.concatenate([r["ns"] for r in R], 0)[:, None]
    return (yp.astype(np.float32), ys.astype(np.float32), nk.astype(np.float32), nv.astype(np.float32),
            nckv.astype(np.float32), nkr.astype(np.float32), ns.astype(np.float32))
```
